# Optimizing a Trainium2 kernel written in Bass

```python
import jax, jax.numpy as jnp
from jax import lax
import numpy as np

D_MODEL = 1024
BATCH = 4
SEQ = 4096
DEPTH = 1
DEC_BATCH = 128
DEC_SEQ = 8
PAST_LEN = 8192
PAGE_SIZE = 128

MLA_HEADS = 8
Q_RANK = 384
KV_RANK = 256
NOPE_DIM = 128
ROPE_DIM = 64
V_DIM = 128
ROPE_THETA = 10000.0
Q_BLOCK = 128
RNN_WIDTH = D_MODEL
RNN_BLOCKS = 8
RNN_BLOCK_DIM = RNN_WIDTH // RNN_BLOCKS
CONV_WIDTH = 4
LRU_C = 8.0
MEM_TOKENS = 256
MEM_HEADS = 4
MEM_HEAD_DIM = D_MODEL // MEM_HEADS
D_FF = 2816
EPS = 1e-6
IN_SPLITS = (Q_RANK, KV_RANK, ROPE_DIM, RNN_WIDTH, RNN_WIDTH, D_MODEL, D_MODEL)
IN_WIDTH = Q_RANK + KV_RANK + ROPE_DIM + 2 * RNN_WIDTH + 2 * D_MODEL

kernel_name = 'hybrid_mla_rglru_macaron_decode_step'


def rmsnorm(x, g):
    xf = x.astype(jnp.float32)
    inv = lax.rsqrt(jnp.mean(xf * xf, axis=-1, keepdims=True) + EPS)
    return (xf * inv).astype(x.dtype) * g


def swiglu(x, wg, wu, wd):
    return (jax.nn.silu(x @ wg) * (x @ wu)) @ wd


def half_ffn(x, pre, post, wg, wu, wd):
    return x + 0.5 * rmsnorm(swiglu(rmsnorm(x, pre), wg, wu, wd), post)


def rope_tables(pos):
    inv = ROPE_THETA ** (-jnp.arange(0, ROPE_DIM, 2, dtype=jnp.float32) / ROPE_DIM)
    ang = pos.astype(jnp.float32)[:, None] * inv[None, :]
    return jnp.cos(ang), jnp.sin(ang)


def apply_rope(x, cos, sin):
    x1, x2 = jnp.split(x, 2, axis=-1)
    cos = cos.astype(x.dtype)
    sin = sin.astype(x.dtype)
    return jnp.concatenate([x1 * cos - x2 * sin, x2 * cos + x1 * sin], axis=-1)


def mixer_front(h, pos, w_in, q_norm, kv_norm, w_uq, w_uk):
    z = h @ w_in
    bounds = np.cumsum(IN_SPLITS)[:-1].tolist()
    cq, ckv, kr, rnn_x, rnn_gate, g_mla, g_rnn = jnp.split(z, bounds, axis=-1)
    b, t = h.shape[0], h.shape[1]
    cos, sin = rope_tables(pos)
    q = (rmsnorm(cq, q_norm) @ w_uq).reshape(b, t, MLA_HEADS, NOPE_DIM + ROPE_DIM)
    q_nope, q_rope = q[..., :NOPE_DIM], q[..., NOPE_DIM:]
    q_rope = apply_rope(q_rope, cos[:, None, :], sin[:, None, :])
    q_lat = jnp.einsum('bthn,rhn->bthr', q_nope, w_uk)
    c_kv = rmsnorm(ckv, kv_norm)
    k_rope = apply_rope(kr, cos, sin)
    return q_lat, q_rope, c_kv, k_rope, rnn_x, rnn_gate, g_mla, g_rnn


def mla_prompt_attention(q_lat, q_rope, c_kv, k_rope):
    scale = (NOPE_DIM + ROPE_DIM) ** -0.5
    b, s = q_lat.shape[0], q_lat.shape[1]
    nb = s // Q_BLOCK
    ql = q_lat.reshape(b, nb, Q_BLOCK, MLA_HEADS, KV_RANK).transpose(1, 0, 2, 3, 4)
    qr = q_rope.reshape(b, nb, Q_BLOCK, MLA_HEADS, ROPE_DIM).transpose(1, 0, 2, 3, 4)
    key_pos = jnp.arange(s)

    def block(args):
        qlb, qrb, i = args
        sc = (jnp.einsum('bthr,bsr->bhts', qlb, c_kv)
              + jnp.einsum('bthd,bsd->bhts', qrb, k_rope)).astype(jnp.float32) * scale
        qpos = i * Q_BLOCK + jnp.arange(Q_BLOCK)
        sc = jnp.where(key_pos[None, :] <= qpos[:, None], sc, -jnp.inf)
        p = jax.nn.softmax(sc, axis=-1).astype(c_kv.dtype)
        return jnp.einsum('bhts,bsr->bthr', p, c_kv)

    o = lax.map(block, (ql, qr, jnp.arange(nb)))
    return o.transpose(1, 0, 2, 3, 4).reshape(b, s, MLA_HEADS, KV_RANK)


def mla_sample_attention(q_lat, q_rope, c_new, kr_new, c_past, kr_past):
    scale = (NOPE_DIM + ROPE_DIM) ** -0.5
    t = q_lat.shape[1]
    p_len = c_past.shape[1]
    s_past = (jnp.einsum('bthr,bsr->bhts', q_lat, c_past)
              + jnp.einsum('bthd,bsd->bhts', q_rope, kr_past)).astype(jnp.float32)
    s_new = (jnp.einsum('bthr,bsr->bhts', q_lat, c_new)
             + jnp.einsum('bthd,bsd->bhts', q_rope, kr_new)).astype(jnp.float32)
    causal = jnp.arange(t)[None, :] <= jnp.arange(t)[:, None]
    s_new = jnp.where(causal, s_new, -jnp.inf)
    p = jax.nn.softmax(jnp.concatenate([s_past, s_new], axis=-1) * scale, axis=-1).astype(c_new.dtype)
    return (jnp.einsum('bhts,bsr->bthr', p[..., :p_len], c_past)
            + jnp.einsum('bhts,bsr->bthr', p[..., p_len:], c_new))


def causal_conv(x, buf, conv_w, conv_b):
    t = x.shape[1]
    xp = jnp.concatenate([buf, x], axis=1)
    y = conv_b
    for k in range(CONV_WIDTH):
        y = y + xp[:, k:k + t] * conv_w[k]
    return y, xp[:, -(CONV_WIDTH - 1):]


def block_diag(x, w, b):
    xb = x.reshape(x.shape[0], x.shape[1], RNN_BLOCKS, RNN_BLOCK_DIM)
    return jnp.einsum('btgi,gij->btgj', xb, w).reshape(x.shape) + b


def rg_lru(x, h0, w_rg, b_rg, w_ig, b_ig, lru_lambda):
    r = jax.nn.sigmoid(block_diag(x, w_rg, b_rg).astype(jnp.float32))
    i = jax.nn.sigmoid(block_diag(x, w_ig, b_ig).astype(jnp.float32))
    log_a = -LRU_C * r * jax.nn.softplus(-lru_lambda.astype(jnp.float32))
    a = jnp.exp(log_a)
    u = jnp.sqrt(-jnp.expm1(2.0 * log_a)) * (i * x.astype(jnp.float32))

    def step(h, au):
        a_t, u_t = au
        h = a_t * h + u_t
        return h, h

    h_last, hs = lax.scan(step, h0.astype(jnp.float32), (jnp.swapaxes(a, 0, 1), jnp.swapaxes(u, 0, 1)))
    return jnp.swapaxes(hs, 0, 1).astype(x.dtype), h_last.astype(h0.dtype)


def mixer_back(o_lat, rnn_y, rnn_gate, g_mla, g_rnn, w_uv, w_o_mla, w_o_rnn, w_out):
    v_heads = jnp.einsum('bthr,rhv->bthv', o_lat, w_uv)
    o_mla = v_heads.reshape(v_heads.shape[0], v_heads.shape[1], MLA_HEADS * V_DIM) @ w_o_mla
    o_rnn = (rnn_y * jax.nn.gelu(rnn_gate)) @ w_o_rnn
    return (jax.nn.sigmoid(g_mla) * o_mla + jax.nn.sigmoid(g_rnn) * o_rnn) @ w_out


def mem_kv(mem, mem_norm, w_mem_k, w_mem_v):
    m = rmsnorm(mem, mem_norm)
    b, n = mem.shape[0], mem.shape[1]
    k = (m @ w_mem_k).reshape(b, n, MEM_HEADS, MEM_HEAD_DIM)
    v = (m @ w_mem_v).reshape(b, n, MEM_HEADS, MEM_HEAD_DIM)
    return k, v


def mem_attend(h, k, v, w_mem_q, w_mem_o):
    b, t = h.shape[0], h.shape[1]
    q = (h @ w_mem_q).reshape(b, t, MEM_HEADS, MEM_HEAD_DIM)
    s = jnp.einsum('bthd,bmhd->bhtm', q, k).astype(jnp.float32) * (MEM_HEAD_DIM ** -0.5)
    p = jax.nn.softmax(s, axis=-1).astype(v.dtype)
    o = jnp.einsum('bhtm,bmhd->bthd', p, v).reshape(b, t, D_MODEL)
    return o @ w_mem_o


def run_layer(x_p, x_s, mem_p, c_lat, c_kr, page_table, conv_s, h_s, mem_k_s, mem_v_s,
              norms, w_ffn_gate, w_ffn_up, w_ffn_down, w_in, q_norm, kv_norm, w_uq, w_uk, w_uv,
              w_o_mla, conv_w, conv_b, w_rg, b_rg, w_ig, b_ig, lru_lambda, w_o_rnn, w_out,
              mem_norm, w_mem_q, w_mem_k, w_mem_v, w_mem_o):
    b = x_p.shape[0]
    db, n_pages = page_table.shape
    past = n_pages * PAGE_SIZE
    pos_p = jnp.arange(x_p.shape[1])
    pos_s = past + jnp.arange(x_s.shape[1])

    x_p = half_ffn(x_p, norms[0], norms[1], w_ffn_gate[0], w_ffn_up[0], w_ffn_down[0])
    x_s = half_ffn(x_s, norms[0], norms[1], w_ffn_gate[0], w_ffn_up[0], w_ffn_down[0])

    h = rmsnorm(x_p, norms[2])
    q_lat, q_rope, lat_p, kr_p, rx, rgate, gm, gr = mixer_front(h, pos_p, w_in, q_norm, kv_norm, w_uq, w_uk)
    o_lat = mla_prompt_attention(q_lat, q_rope, lat_p, kr_p)
    xc, conv_p_new = causal_conv(rx, jnp.zeros((b, CONV_WIDTH - 1, RNN_WIDTH), rx.dtype), conv_w, conv_b)
    ry, h_p_new = rg_lru(xc, jnp.zeros((b, RNN_WIDTH), rx.dtype), w_rg, b_rg, w_ig, b_ig, lru_lambda)
    x_p = x_p + rmsnorm(mixer_back(o_lat, ry, rgate, gm, gr, w_uv, w_o_mla, w_o_rnn, w_out), norms[3])

    h = rmsnorm(x_s, norms[2])
    q_lat, q_rope, lat_s, kr_s, rx, rgate, gm, gr = mixer_front(h, pos_s, w_in, q_norm, kv_norm, w_uq, w_uk)
    c_past = c_lat[page_table].reshape(db, past, KV_RANK)
    kr_past = c_kr[page_table].reshape(db, past, ROPE_DIM)
    o_lat = mla_sample_attention(q_lat, q_rope, lat_s, kr_s, c_past, kr_past)
    xc, conv_s_new = causal_conv(rx, conv_s, conv_w, conv_b)
    ry, h_s_new = rg_lru(xc, h_s, w_rg, b_rg, w_ig, b_ig, lru_lambda)
    x_s = x_s + rmsnorm(mixer_back(o_lat, ry, rgate, gm, gr, w_uv, w_o_mla, w_o_rnn, w_out), norms[3])

    mk_p, mv_p = mem_kv(mem_p, mem_norm, w_mem_k, w_mem_v)
    x_p = x_p + rmsnorm(mem_attend(rmsnorm(x_p, norms[4]), mk_p, mv_p, w_mem_q, w_mem_o), norms[5])
    x_s = x_s + rmsnorm(mem_attend(rmsnorm(x_s, norms[4]), mem_k_s, mem_v_s, w_mem_q, w_mem_o), norms[5])

    x_p = half_ffn(x_p, norms[6], norms[7], w_ffn_gate[1], w_ffn_up[1], w_ffn_down[1])
    x_s = half_ffn(x_s, norms[6], norms[7], w_ffn_gate[1], w_ffn_up[1], w_ffn_down[1])
    return (x_p, x_s, lat_p, kr_p, lat_s, kr_s, conv_p_new, conv_s_new, h_p_new, h_s_new, mk_p, mv_p)


def setup_inputs(seed: int = 0) -> dict:
    key = jax.random.key(seed)
    ks = iter(jax.random.split(key, 48))
    f32 = jnp.float32
    L = DEPTH

    def nrm(shape, scale):
        return jax.random.normal(next(ks), shape, f32) * scale

    n_pages = PAST_LEN // PAGE_SIZE
    n_used = DEC_BATCH * n_pages
    n_pool = n_used + n_used // 4
    x_prompt = nrm((BATCH, SEQ, D_MODEL), 1.0)
    x_sample = nrm((DEC_BATCH, DEC_SEQ, D_MODEL), 1.0)
    mem_prompt = nrm((BATCH, MEM_TOKENS, D_MODEL), 1.0)
    cache_mla_latent = nrm((L, n_pool, PAGE_SIZE, KV_RANK), 1.0)
    cache_mla_krope = nrm((L, n_pool, PAGE_SIZE, ROPE_DIM), 1.0)
    page_table = jax.random.permutation(next(ks), n_pool)[:n_used].reshape(DEC_BATCH, n_pages).astype(jnp.int32)
    state_rnn_conv = nrm((L, DEC_BATCH, CONV_WIDTH - 1, RNN_WIDTH), 1.0)
    state_rnn_h = nrm((L, DEC_BATCH, RNN_WIDTH), 0.5)
    cache_mem_k = nrm((L, DEC_BATCH, MEM_TOKENS, MEM_HEADS, MEM_HEAD_DIM), 1.0)
    cache_mem_v = nrm((L, DEC_BATCH, MEM_TOKENS, MEM_HEADS, MEM_HEAD_DIM), 1.0)

    norms = 1.0 + nrm((L, 8, D_MODEL), 0.05)
    w_ffn_gate = nrm((L, 2, D_MODEL, D_FF), D_MODEL ** -0.5)
    w_ffn_up = nrm((L, 2, D_MODEL, D_FF), D_MODEL ** -0.5)
    w_ffn_down = nrm((L, 2, D_FF, D_MODEL), D_FF ** -0.5)
    w_in = nrm((L, D_MODEL, IN_WIDTH), D_MODEL ** -0.5)
    q_norm = 1.0 + nrm((L, Q_RANK), 0.05)
    kv_norm = 1.0 + nrm((L, KV_RANK), 0.05)
    w_uq = nrm((L, Q_RANK, MLA_HEADS * (NOPE_DIM + ROPE_DIM)), Q_RANK ** -0.5)
    w_uk = nrm((L, KV_RANK, MLA_HEADS, NOPE_DIM), KV_RANK ** -0.5)
    w_uv = nrm((L, KV_RANK, MLA_HEADS, V_DIM), KV_RANK ** -0.5)
    w_o_mla = nrm((L, MLA_HEADS * V_DIM, D_MODEL), (MLA_HEADS * V_DIM) ** -0.5)
    conv_w = nrm((L, CONV_WIDTH, RNN_WIDTH), CONV_WIDTH ** -0.5)
    conv_b = nrm((L, RNN_WIDTH), 0.01)
    w_rg = nrm((L, RNN_BLOCKS, RNN_BLOCK_DIM, RNN_BLOCK_DIM), RNN_BLOCK_DIM ** -0.5)
    b_rg = nrm((L, RNN_WIDTH), 0.01)
    w_ig = nrm((L, RNN_BLOCKS, RNN_BLOCK_DIM, RNN_BLOCK_DIM), RNN_BLOCK_DIM ** -0.5)
    b_ig = nrm((L, RNN_WIDTH), 0.01)
    a_c = jax.random.uniform(next(ks), (L, RNN_WIDTH), f32, minval=0.9, maxval=0.999)
    a0 = a_c ** (1.0 / LRU_C)
    lru_lambda = jnp.log(a0) - jnp.log1p(-a0)
    w_o_rnn = nrm((L, RNN_WIDTH, D_MODEL), RNN_WIDTH ** -0.5)
    w_out = nrm((L, D_MODEL, D_MODEL), D_MODEL ** -0.5)
    mem_norm = 1.0 + nrm((L, D_MODEL), 0.05)
    w_mem_q = nrm((L, D_MODEL, D_MODEL), D_MODEL ** -0.5)
    w_mem_k = nrm((L, D_MODEL, D_MODEL), D_MODEL ** -0.5)
    w_mem_v = nrm((L, D_MODEL, D_MODEL), D_MODEL ** -0.5)
    w_mem_o = nrm((L, D_MODEL, D_MODEL), D_MODEL ** -0.5)
    return {'x_prompt': x_prompt, 'x_sample': x_sample, 'mem_prompt': mem_prompt,
            'cache_mla_latent': cache_mla_latent, 'cache_mla_krope': cache_mla_krope, 'page_table': page_table,
            'state_rnn_conv': state_rnn_conv, 'state_rnn_h': state_rnn_h,
            'cache_mem_k': cache_mem_k, 'cache_mem_v': cache_mem_v,
            'norms': norms, 'w_ffn_gate': w_ffn_gate, 'w_ffn_up': w_ffn_up, 'w_ffn_down': w_ffn_down,
            'w_in': w_in, 'q_norm': q_norm, 'kv_norm': kv_norm, 'w_uq': w_uq, 'w_uk': w_uk, 'w_uv': w_uv,
            'w_o_mla': w_o_mla, 'conv_w': conv_w, 'conv_b': conv_b, 'w_rg': w_rg, 'b_rg': b_rg,
            'w_ig': w_ig, 'b_ig': b_ig, 'lru_lambda': lru_lambda, 'w_o_rnn': w_o_rnn, 'w_out': w_out,
            'mem_norm': mem_norm, 'w_mem_q': w_mem_q, 'w_mem_k': w_mem_k, 'w_mem_v': w_mem_v, 'w_mem_o': w_mem_o}


def reference(x_prompt, x_sample, mem_prompt, cache_mla_latent, cache_mla_krope, page_table,
              state_rnn_conv, state_rnn_h, cache_mem_k, cache_mem_v,
              norms, w_ffn_gate, w_ffn_up, w_ffn_down, w_in, q_norm, kv_norm, w_uq, w_uk, w_uv,
              w_o_mla, conv_w, conv_b, w_rg, b_rg, w_ig, b_ig, lru_lambda, w_o_rnn, w_out,
              mem_norm, w_mem_q, w_mem_k, w_mem_v, w_mem_o):
    x_p, x_s = x_prompt, x_sample
    layer_states = []
    for l in range(DEPTH):
        x_p, x_s, *st = run_layer(
            x_p, x_s, mem_prompt, cache_mla_latent[l], cache_mla_krope[l], page_table,
            state_rnn_conv[l], state_rnn_h[l], cache_mem_k[l], cache_mem_v[l],
            norms[l], w_ffn_gate[l], w_ffn_up[l], w_ffn_down[l], w_in[l], q_norm[l], kv_norm[l],
            w_uq[l], w_uk[l], w_uv[l], w_o_mla[l], conv_w[l], conv_b[l], w_rg[l], b_rg[l],
            w_ig[l], b_ig[l], lru_lambda[l], w_o_rnn[l], w_out[l],
            mem_norm[l], w_mem_q[l], w_mem_k[l], w_mem_v[l], w_mem_o[l])
        layer_states.append(st)
    new = [jnp.stack(s, axis=0) for s in zip(*layer_states)]
    return (x_p, x_s, new[0], new[1], new[2], new[3], new[4], new[5], new[6], new[7], new[8], new[9])
```

```python
import numpy as np
from contextlib import ExitStack
import concourse.bass as bass
import concourse.mybir as mybir
from concourse.bass_utils import run_bass_kernel_spmd

F32 = mybir.dt.float32
BF16 = mybir.dt.bfloat16
I32 = mybir.dt.int32
AF = mybir.ActivationFunctionType
ALU = mybir.AluOpType

NCORES = 8
D = 1024
DC = 8
FF = 2816
FC = 22
TT = 256
NBLK = TT // 128
SEQ_HALF = 2048
NT = SEQ_HALF // TT
NSMP = 16
EPS = 1e-6
SCALE = 192.0 ** -0.5
NEG = -30000.0


class Tok:
    __slots__ = ("sem", "val", "eng")

    def __init__(self, sem, val, eng):
        self.sem = sem
        self.val = val
        self.eng = eng


class Buf:
    def __init__(self, ctx, t, name, space):
        self.ctx = ctx
        self.t = t
        self.name = name
        self.space = space
        self.last_w = None
        self.readers = {}
        self.dsem = None
        self.dcount = 0
        self.core = self

    def __getitem__(self, idx):
        return self.t[idx]

    def get_dsem(self):
        if self.dsem is None:
            self.dsem = self.ctx.new_sem("d_" + self.name)
        return self.dsem


class EngState:
    def __init__(self, name, obj, sem):
        self.name = name
        self.obj = obj
        self.sem = sem
        self.count = 0
        self.known = {}
        self.pending = None
        self.n_wait = 0
        self.n_inst = 0


class Ctx:
    def __init__(self, nc, es):
        self.nc = nc
        self.es = es
        self.nsem = 0
        self.engs = {}
        for name, obj in (("pe", nc.tensor), ("act", nc.scalar), ("dve", nc.vector),
                          ("pool", nc.gpsimd), ("sp", nc.sync)):
            self.engs[name] = EngState(name, obj, self.new_sem("e_" + name))
        self.out_toks = []
        self.all_bufs = []

    def view(self, ap, name, core=None):
        b = Buf(self, ap, name, "sb")
        if core is not None:
            b.core = core
        else:
            self.all_bufs.append(b)
        return b

    def new_sem(self, name):
        self.nsem += 1
        return self.es.enter_context(self.nc.semaphore(name))

    def sb(self, name, shape, dtype):
        t = self.es.enter_context(self.nc.sbuf_tensor(name, list(shape), dtype))
        b = Buf(self, t, name, "sb")
        self.all_bufs.append(b)
        return b

    def ps(self, name, shape, dtype=F32):
        t = self.es.enter_context(self.nc.psum_tensor(name, list(shape), dtype))
        return Buf(self, t, name, "ps")

    def _need(self, E, reads, writes, strict=False):
        reads = [b.core for b in reads]
        writes = [b.core for b in writes]
        toks = []
        for b in reads:
            if b.last_w is not None:
                toks.append(b.last_w)
        for b in writes:
            if b.last_w is not None and (strict or b.last_w.eng != E.name):
                toks.append(b.last_w)
            for tk in b.readers.values():
                if strict or tk.eng != E.name:
                    toks.append(tk)
        best = {}
        for tk in toks:
            if tk.eng == "pe" and E.name == "pe":
                continue
            assert tk.val is not None, "dependency on un-signalled PE op"
            k = id(tk.sem)
            if E.known.get(k, 0) >= tk.val:
                continue
            if k not in best or best[k].val < tk.val:
                best[k] = tk
        for k, tk in best.items():
            E.obj.wait_ge(tk.sem, tk.val)
            E.known[k] = tk.val
            E.n_wait += 1

    def _record(self, tok, reads, writes):
        reads = [b.core for b in reads]
        writes = [b.core for b in writes]
        k = id(tok.sem)
        for b in reads:
            b.readers[k] = tok
        for b in writes:
            b.last_w = tok
            b.readers = {}

    def op(self, eng, fn, reads=(), writes=(), inc=True):
        E = self.engs[eng]
        self._need(E, reads, writes)
        inst = fn(E.obj)
        E.n_inst += 1
        if eng == "pe":
            if E.pending is None:
                E.pending = Tok(E.sem, None, "pe")
            tok = E.pending
            if inc:
                E.count += 1
                inst.then_inc(E.sem, 1)
                tok.val = E.count
                E.pending = None
        else:
            E.count += 1
            inst.then_inc(E.sem, 1)
            tok = Tok(E.sem, E.count, eng)
        self._record(tok, reads, writes)
        return inst

    def dma(self, q, parts, reads=(), writes=(), sembuf=None, is_output=False, indirect=False):
        E = self.engs[q]
        self._need(E, reads, writes, strict=True)
        sb = sembuf
        if sb is None:
            cands = [b for b in list(writes) + list(reads) if b.space != "dram"]
            sb = cands[0]
        sb = sb.core
        sem = sb.get_dsem()
        for p in parts:
            if len(p) == 3:
                inst = E.obj.indirect_dma_start(out=p[0], out_offset=None, in_=p[1],
                                                in_offset=bass.IndirectOffsetOnAxis(ap=p[2], axis=0))
            else:
                inst = E.obj.dma_start(out=p[0], in_=p[1])
            sb.dcount += 16
            inst.then_inc(sem, 16)
            E.n_inst += 1
        tok = Tok(sem, sb.dcount, "dma")
        self._record(tok, reads, writes)
        if is_output:
            self.out_toks.append(tok)
        return tok

    def barrier(self):
        dsems = [(b.dsem, b.dcount) for b in self.all_bufs if b.dsem is not None and b.dcount > 0]
        for name, E in self.engs.items():
            for oname, X in self.engs.items():
                if oname != name and X.count > 0 and E.known.get(id(X.sem), 0) < X.count:
                    E.obj.wait_ge(X.sem, X.count)
                    E.known[id(X.sem)] = X.count
            for sem, cnt in dsems:
                if E.known.get(id(sem), 0) < cnt:
                    E.obj.wait_ge(sem, cnt)
                    E.known[id(sem)] = cnt

    def finish(self, eng="sp"):
        E = self.engs[eng]
        best = {}
        for tk in self.out_toks:
            k = id(tk.sem)
            if k not in best or best[k].val < tk.val:
                best[k] = tk
        for k, tk in best.items():
            E.obj.wait_ge(tk.sem, tk.val)
        for name, X in self.engs.items():
            if X.count > 0 and name != eng:
                E.obj.wait_ge(X.sem, X.count)


def build(n_pool):
    nc = bass.Bass("TRN2", target_bir_lowering=False)

    def din(name, shape, dt=F32):
        return nc.dram_tensor(name, list(shape), dt, kind="ExternalInput").ap()

    def dout(name, shape, dt=F32):
        return nc.dram_tensor(name, list(shape), dt, kind="ExternalOutput").ap()

    x_oth = din("x_oth", [SEQ_HALF, D])
    x_own = din("x_own", [SEQ_HALF, D])
    x_smp = din("x_smp", [128, D])
    mem_p = din("mem_p", [256, D])
    pool_lat = din("pool_lat", [n_pool * 128, 256])
    pool_kr = din("pool_kr", [n_pool * 128, 64])
    ptab = din("ptab", [1, NSMP * 64], I32)
    conv_s = din("conv_s", [NSMP * 3, D])
    h_s = din("h_s", [NSMP, D])
    mem_k_s = din("mem_k_s", [NSMP, 256, D])
    mem_v_s = din("mem_v_s", [NSMP, 256, D])
    cs_oth = din("cs_oth", [2, 64, SEQ_HALF])
    cs_own = din("cs_own", [2, 64, SEQ_HALF])
    cs_smp = din("cs_smp", [2, 64, 128])
    kmask = din("kmask", [1, 2 * SEQ_HALF])
    pmask = din("pmask", [1, 8 * TT])
    cmask_d = din("cmask", [128, 1])
    vec_d = din("vec", [19, D])
    w_gate = din("w_gate", [2, D, FF])
    w_up = din("w_up", [2, D, FF])
    w_down = din("w_down", [2, FF, D])
    w_in = din("w_in", [D, 4800])
    w_uq = din("w_uq", [384, 1536])
    w_uk = din("w_uk", [256, 1024])
    w_uv = din("w_uv", [256, 1024])
    w_o_mla = din("w_o_mla", [D, D])
    w_rg = din("w_rg", [8, 128, 128])
    w_ig = din("w_ig", [8, 128, 128])
    w_o_rnn = din("w_o_rnn", [D, D])
    w_out = din("w_out", [D, D])
    w_mem_q = din("w_mem_q", [D, D])
    w_mem_k = din("w_mem_k", [D, D])
    w_mem_v = din("w_mem_v", [D, D])
    w_mem_o = din("w_mem_o", [D, D])

    y_own = dout("y_own", [SEQ_HALF, D])
    y_smp = dout("y_smp", [128, D])
    lat_own = dout("lat_own", [SEQ_HALF, 256])
    kr_own = dout("kr_own", [SEQ_HALF, 64])
    lat_smp = dout("lat_smp", [128, 256])
    kr_smp = dout("kr_smp", [128, 64])
    conv_p = dout("conv_p", [3, D])
    conv_smp = dout("conv_smp", [NSMP * 3, D])
    h_p = dout("h_p", [8, 128])
    h_smp = dout("h_smp", [NSMP, D])
    mk_p = dout("mk_p", [256, D])
    mv_p = dout("mv_p", [256, D])

    es = ExitStack()
    with es:
        c = Ctx(nc, es)
        W = TT + 8

        PS = [c.ps(f"ps{i}", [128, 512], F32) for i in range(8)]
        st = {"ps": 0, "w": 0, "tf": 0, "tb": 0, "pt": 0}

        pinned = set()

        def ps_next(pin=False):
            while True:
                b = PS[st["ps"] % 8]
                st["ps"] += 1
                if id(b) not in pinned:
                    break
            if pin:
                pinned.add(id(b))
            return b

        def unpin(*bs):
            for b in bs:
                pinned.discard(id(b))

        NW = 4
        WS = [c.sb(f"wslot{i}", [128, 4096], BF16) for i in range(NW)]
        WORK_WORDS = 22672
        WORK = es.enter_context(nc.sbuf_tensor("WORK", [128, WORK_WORDS], F32))
        wk = {"o": 0}

        def wcarve(name, shape, dtype, parts=128):
            n = int(np.prod(shape[1:]))
            words = n if dtype == F32 else (n + 1) // 2
            ap = WORK[:, wk["o"]:wk["o"] + words]
            wk["o"] += words
            assert wk["o"] <= WORK_WORDS, (name, wk["o"])
            if dtype != F32:
                ap = ap.bitcast(dtype)[:, 0:n]
            if len(shape) == 3:
                ap = ap.rearrange("p (a b) -> p a b", a=shape[1])
            elif len(shape) == 4:
                ap = ap.rearrange("p (a b c) -> p a b c", a=shape[1], b=shape[2])
            if parts != 128:
                ap = ap[0:parts]
            return c.view(ap, name)
        TF = [wcarve(f"tf{i}", [128, W], F32) for i in range(8)]
        TB = [wcarve(f"tb{i}", [128, W], BF16) for i in range(3)]
        PTs = [c.sb(f"pt{i}", [128, 512], BF16) for i in range(3)]
        RS = [wcarve(f"rs{i}", [128, W], F32) for i in range(3)]

        def tmpf():
            b = TF[st["tf"] % len(TF)]
            st["tf"] += 1
            return b

        def tmpb():
            b = TB[st["tb"] % len(TB)]
            st["tb"] += 1
            return b

        def pt_next():
            b = PTs[st["pt"] % len(PTs)]
            st["pt"] += 1
            return b

        ident_f = c.sb("ident_f", [128, 128], F32)
        ident_b = c.sb("ident_b", [128, 128], BF16)
        ones_b = c.sb("ones_b", [128, 128], BF16)
        maskb = c.sb("maskb", [128, 4, 128], BF16)
        masks = c.sb("masks", [128, NSMP, 8, 8], BF16)
        VT = c.sb("VT", [128, DC, 19], F32)
        DER = c.sb("DER", [128, DC, 4], F32)
        HB = c.sb("HB", [128, DC, 2], F32)
        CARRY = c.sb("CARRY", [128, DC], F32)
        HALO = c.sb("HALO", [128, DC, 3], F32)
        CMASK = c.sb("CMASK", [128, 1], F32)
        ARENA = es.enter_context(nc.sbuf_tensor("ARENA", [128, 20480], BF16))
        ar = {"o": 0}

        def carve(name, shape):
            n = int(np.prod(shape[1:]))
            ap = ARENA[:, ar["o"]:ar["o"] + n]
            ar["o"] += n
            assert ar["o"] <= 20480, (name, ar["o"])
            if len(shape) == 3:
                ap = ap.rearrange("p (a b) -> p a b", a=shape[1])
            elif len(shape) == 4:
                ap = ap.rearrange("p (a b c) -> p a b c", a=shape[1], b=shape[2])
            return c.view(ap, name)
        KT = [carve(f"KT{i}", [128, 3, TT]) for i in range(2 * NT)]
        VV = [carve(f"VV{i}", [128, NBLK, 256]) for i in range(2 * NT)]
        ar["o"] = 0
        PGK = [carve(f"PGK{i}", [128, 8, 256]) for i in range(3)]
        PGR = [carve(f"PGR{i}", [128, 8, 64]) for i in range(3)]
        KTP = [carve(f"KTP{i}", [128, 3, 8, 128]) for i in range(2)]
        MKS = [carve(f"MKS{i}", [128, 2, D]) for i in range(1)] * 2
        MVS = [carve(f"MVS{i}", [128, 2, D]) for i in range(1)] * 2
        MKST = [carve(f"MKST{i}", [128, DC, 256]) for i in range(1)] * 2
        KTS = c.sb("KTS", [128, 3, 128], BF16)
        VVS = c.sb("VVS", [128, 1, 256], BF16)
        MKT = c.sb("MKT", [128, DC, 256], BF16)
        MV = c.sb("MV", [128, 2, D], BF16)
        WUKT = c.sb("WUKT", [128, 8, 256], BF16)
        WUV = c.sb("WUV", [128, 2, 1024], BF16)
        WRG = c.sb("WRG", [128, 8, 128], BF16)
        WIG = c.sb("WIG", [128, 8, 128], BF16)
        XT = wcarve("XT", [128, DC, TT], F32)
        YB = wcarve("YB", [128, DC * (TT + 1)], F32)
        RXB = wcarve("RXB", [128, DC * (TT + 3)], F32)
        OLAT = wcarve("OLAT", [128, 2, 8, TT], BF16)
        H = wcarve("H", [128, DC, TT], BF16)
        off_sq = wk["o"]
        SQ = wcarve("SQ", [128, DC, TT], BF16)
        MIX = wcarve("MIX", [128, DC, TT], BF16)
        AT = wcarve("AT", [128, 24 * TT], BF16)
        BIGT = AT.t
        AT.t = BIGT[:, 0:FC * TT].rearrange("p (f n) -> p f n", f=FC)
        QL = c.view(BIGT[:, 0:16 * TT].rearrange("p (j h n) -> p j h n", j=2, h=8), "QL", core=AT)
        QN = c.view(BIGT[:, 16 * TT:24 * TT].rearrange("p (h n) -> p h n", h=8), "QN", core=AT)
        QRT = c.sb("QRT", [128, 8, TT], BF16)
        QRS = QRT
        YR = wcarve("YR", [128, DC, TT], BF16)
        MIXR = wcarve("MIXR", [128, DC, TT], BF16)
        off_rend = wk["o"]
        RNW = DC * (TT + 1)
        assert off_sq + 3 * RNW <= off_rend, (off_sq, off_rend)
        RA = c.view(WORK[:, off_sq:off_sq + RNW], "RA")
        RM = c.view(WORK[:, off_sq + RNW:off_sq + 2 * RNW], "RM")
        RW = c.view(WORK[:, off_sq + 2 * RNW:off_sq + 3 * RNW], "RW")
        SCR = wcarve("SCR", [128, 4, TT], F32)
        CQ = wcarve("CQ", [128, 3, TT], F32)
        CQN = wcarve("CQN", [128, 3, TT], BF16)
        KRT = wcarve("KRT", [128, TT], F32, parts=64)
        CS = wcarve("CS", [128, 2, TT], F32, parts=64)
        main_words = wk["o"]
        wk["o"] = 0
        TF_ = 512
        XT5 = wcarve("XT5", [128, DC, TF_], F32)
        YB5 = wcarve("YB5", [128, DC * TF_], F32)
        H5 = wcarve("H5", [128, DC, TF_], BF16)
        SQ5 = wcarve("SQ5", [128, DC, TF_], BF16)
        AT5 = wcarve("AT5", [128, FC, TF_], BF16)
        TF5 = [wcarve(f"tf5_{i}", [128, TF_ + 8], F32) for i in range(4)]
        RS5 = [wcarve(f"rs5_{i}", [128, TF_ + 8], F32) for i in range(3)]
        print("WORK words: main", main_words, "ffn", wk["o"])
        OSTG = [c.sb(f"OSTG{i}", [128, D], F32) for i in range(2)]
        RD = c.sb("RD", [128, 8], F32)
        ONF = c.sb("ONF", [128, 256], F32)
        IDXG = c.sb("IDXG", [128, NSMP * 8], I32)
        IOTA = c.sb("IOTA", [128, 1], F32)
        STG = OSTG[1]
        VEC = STG
        H0S = c.sb("H0S", [128, DC, NSMP], F32)
        QM = QN
        OM = MIX

        def ACT(out, in_, func, reads, writes, **kw):
            return c.op("act", lambda e: e.activation(out, in_, func, **kw), reads=reads, writes=writes)

        def DVE(fn, reads, writes):
            return c.op("dve", fn, reads=reads, writes=writes)

        def mm(out_ap, pairs, reads, writes, start=True, stop=True):
            n = len(pairs)
            for i, (l, r) in enumerate(pairs):
                c.op("pe", lambda e, l=l, r=r, i=i: e.matmul(out_ap, l, r, start=(start and i == 0),
                                                              stop=(stop and i == n - 1)),
                     reads=reads, writes=writes, inc=(i == n - 1))

        def transpose(out_ap, in_ap, ident_ap, reads, writes):
            c.op("pe", lambda e: e.transpose(out_ap, in_ap, ident_ap), reads=reads, writes=writes, inc=True)

        class WRef:
            def __init__(self, ap, buf):
                self.ap = ap
                self.buf = buf

            def __getitem__(self, idx):
                return WRef(self.ap[idx], self.buf)

            def rearrange(self, pat, **kw):
                return WRef(self.ap.rearrange(pat, **kw), self.buf)

        def prep(name, src_ap, shape, split):
            t = nc.dram_tensor("wb_" + name, list(shape), BF16, kind="Internal")
            b = Buf(c, t, "wb_" + name, "dscr")
            c.all_bufs.append(b)
            dst = t.ap()
            lead = len(shape) - 2
            srcs = [src_ap[i] for i in range(shape[0])] if lead else [src_ap]
            dsts = [dst[i] for i in range(shape[0])] if lead else [dst]
            parts = []
            for sa, da in zip(srcs, dsts):
                if split > 1:
                    sa = sa.rearrange("k (a b) -> (k a) b", a=split)
                    da = da.rearrange("k (a b) -> (k a) b", a=split)
                parts.append((da, sa))
            c.dma("pool", parts, writes=[b], sembuf=b)
            return WRef(dst, b)

        def wload(parts):
            slot = WS[st["w"] % NW]
            st["w"] += 1
            c.dma("sp", [(fn(slot), src.ap) for fn, src in parts], reads=[src.buf for fn, src in parts], writes=[slot],
                  sembuf=slot)
            return slot

        def wview(slot, k, n):
            return slot[:, 0:k * n].rearrange("p (k n) -> p k n", k=k)

        def wsrc(w_ap, r0, r1, c0, c1):
            return w_ap[r0:r1, c0:c1].rearrange("(k p) n -> p k n", p=128)

        w_gate_f, w_up_f, w_down_f = w_gate, w_up, w_down
        w_gate = prep("gate", w_gate_f, [2, D, FF], 2)
        w_up = prep("up", w_up_f, [2, D, FF], 2)
        w_down = prep("down", w_down_f, [2, FF, D], 1)
        w_in = prep("in", w_in, [D, 4800], 3)
        w_uq = prep("uq", w_uq, [384, 1536], 1)
        w_o_rnn = prep("o_rnn", w_o_rnn, [D, D], 1)
        w_o_mla = prep("o_mla", w_o_mla, [D, D], 1)
        w_out = prep("out", w_out, [D, D], 1)
        w_mem_q = prep("mem_q", w_mem_q, [D, D], 1)
        w_mem_k = prep("mem_k", w_mem_k, [D, D], 1)
        w_mem_v = prep("mem_v", w_mem_v, [D, D], 1)
        w_mem_o = prep("mem_o", w_mem_o, [D, D], 1)

        c.op("pool", lambda e: e.memset(ident_f[:], 0.0), writes=[ident_f])
        c.op("pool", lambda e: e.affine_select(ident_f[:], ident_f[:], [[-1, 128]], ALU.not_equal, 1.0,
                                               base=0, channel_multiplier=1), reads=[ident_f], writes=[ident_f])
        DVE(lambda e: e.tensor_copy(ident_b[:], ident_f[:]), [ident_f], [ident_b])
        c.op("pool", lambda e: e.memset(ones_b[:], 1.0), writes=[ones_b])
        c.op("pool", lambda e: e.memset(maskb[:], 0.0), writes=[maskb])
        c.op("pool", lambda e: e.affine_select(maskb[:], maskb[:], [[0, 4], [1, 128]], ALU.is_ge, NEG,
                                               base=0, channel_multiplier=-1), reads=[maskb], writes=[maskb])
        c.op("pool", lambda e: e.memset(masks[:], 0.0), writes=[masks])
        c.op("pool", lambda e: e.affine_select(masks[:], masks[:], [[8, NSMP], [0, 8], [1, 8]], ALU.is_ge, NEG,
                                               base=0, channel_multiplier=-1), reads=[masks], writes=[masks])
        c.op("pool", lambda e: e.affine_select(masks[:], masks[:], [[-8, NSMP], [0, 8], [0, 8]], ALU.is_ge, NEG,
                                               base=0, channel_multiplier=1), reads=[masks], writes=[masks])
        c.op("pool", lambda e: e.memset(HALO[:], 0.0), writes=[HALO])
        c.op("pool", lambda e: e.memset(CARRY[:], 0.0), writes=[CARRY])
        c.dma("pool", [(CMASK[:], cmask_d)], writes=[CMASK])
        c.dma("pool", [(VEC[0:19, :], vec_d)], writes=[VEC])
        pv = ps_next()
        for ch in range(DC):
            transpose(pv[:, ch * 19:(ch + 1) * 19], VEC[0:19, ch * 128:(ch + 1) * 128], ident_f[0:19, 0:19],
                      [VEC, ident_f], [pv])
        DVE(lambda e: e.tensor_copy(VT[:], pv[:, 0:DC * 19].rearrange("p (c r) -> p c r", c=DC)), [pv], [VT])
        def vrow(r, ch):
            return VT[:, ch, r:r + 1]
        t0 = tmpf()
        ACT(t0[:, 0:DC], VT[:, :, 16], AF.Exp, [VT], [t0], scale=-1.0)
        ACT(t0[:, 0:DC], t0[:, 0:DC], AF.Ln, [t0], [t0], bias=1.0)
        DVE(lambda e: e.tensor_scalar(DER[:, :, 0], t0[:, 0:DC], -8.0, None, op0=ALU.mult), [t0], [DER])
        DVE(lambda e: e.tensor_scalar(DER[:, :, 1], VT[:, :, 1], 0.5, None, op0=ALU.mult), [VT], [DER])
        DVE(lambda e: e.tensor_scalar(DER[:, :, 2], VT[:, :, 7], 0.5, None, op0=ALU.mult), [VT], [DER])
        DVE(lambda e: e.tensor_scalar(DER[:, :, 3], t0[:, 0:DC], -4.0, None, op0=ALU.mult), [t0], [DER])
        DVE(lambda e: e.tensor_scalar(HB[:, :, 0], VT[:, :, 14], 0.5, None, op0=ALU.mult), [VT], [HB])
        DVE(lambda e: e.tensor_scalar(HB[:, :, 1], VT[:, :, 15], 0.5, None, op0=ALU.mult), [VT], [HB])

        c.dma("pool", [(WUV[:], w_uv.rearrange("(j p) n -> p j n", p=128))], writes=[WUV])
        c.dma("pool", [(WRG[:], w_rg.rearrange("g i j -> i g j"))], writes=[WRG])
        c.dma("pool", [(WIG[:], w_ig.rearrange("g i j -> i g j"))], writes=[WIG])
        c.dma("pool", [(YB[:, 0:2048].rearrange("p (j n) -> p j n", j=2), w_uk.rearrange("(j p) n -> p j n", p=128))],
              writes=[YB])
        for hh in range(8):
            pw = ps_next()
            for j in range(2):
                transpose(pw[:, j * 128:(j + 1) * 128], YB[:, j * 1024 + hh * 128: j * 1024 + (hh + 1) * 128],
                          ident_f[:], [YB, ident_f], [pw])
            DVE(lambda e, pw=pw, hh=hh: e.tensor_copy(WUKT[:, hh, :], pw[:, 0:256]), [pw], [WUKT])
        for i in range(2 * NT):
            c.op("pool", lambda e, i=i: e.memset(KT[i][64:128, 2, :], 0.0), writes=[KT[i]])
            c.dma("pool", [(KT[i][64:65, 2, :], kmask[:, i * TT:(i + 1) * TT])], writes=[KT[i]])
        c.op("pool", lambda e: e.memset(QRT[64:128, :, :], 0.0), writes=[QRT])
        c.dma("pool", [(QRT[64:65, :, :].rearrange("p h n -> p (h n)"), pmask)], writes=[QRT])

        def rms_stats(xap, nch, N, Dn, xbuf):
            for ch in range(nch):
                ACT(SQ[:, ch, 0:N], xap(ch), AF.Square, [xbuf], [SQ])
            ps = ps_next()
            mm(ps[:, 0:N], [(ones_b[:], SQ[:, ch, 0:N]) for ch in range(nch)], [ones_b, SQ], [ps])
            t = RS[2]
            ACT(t[:, 0:N], ps[:, 0:N], AF.Sqrt, [ps], [t], bias=EPS, scale=1.0 / Dn)
            r = RS[st.setdefault("rs", 0) % 2]
            st["rs"] += 1
            DVE(lambda e: e.reciprocal(r[:, 0:N], t[:, 0:N]), [t], [r])
            return r

        def rmsnorm_to(xap, xbuf, nch, N, Dn, gain, outap, outbuf):
            r = rms_stats(xap, nch, N, Dn, xbuf)
            for ch in range(nch):
                DVE(lambda e, ch=ch: e.scalar_tensor_tensor(outap(ch), xap(ch), gain(ch), r[:, 0:N],
                                                            op0=ALU.mult, op1=ALU.mult), [xbuf, r, VT, DER], [outbuf])

        def Yv(N):
            return YB[:, 0:DC * N].rearrange("p (c n) -> p c n", c=DC)

        def residual_add(N, gain):
            Y = Yv(N)
            r = rms_stats(lambda ch: Y[:, ch, :], DC, N, D, YB)
            for ch in range(DC):
                t = tmpf()
                DVE(lambda e, ch=ch, t=t: e.scalar_tensor_tensor(t[:, 0:N], Y[:, ch, :], gain(ch), r[:, 0:N],
                                                                  op0=ALU.mult, op1=ALU.mult), [YB, r, VT, DER], [t])
                DVE(lambda e, ch=ch, t=t: e.tensor_tensor(XT[:, ch, 0:N], XT[:, ch, 0:N], t[:, 0:N], op=ALU.add),
                    [XT, t], [XT])

        def proj(w_ap, col0, ncols, N, consumer, src=None, kch=DC, chunk=128):
            src = H if src is None else src
            done = 0
            while done < ncols:
                gw = min(512, ncols - done)
                slot = wload([(lambda s, gw=gw: wview(s, kch, gw), wsrc(w_ap, 0, kch * 128, col0 + done, col0 + done + gw))])
                wv = wview(slot, kch, gw)
                for cc in range(0, gw, chunk):
                    m = min(chunk, gw - cc)
                    ps = ps_next()
                    mm(ps[0:m, 0:N], [(wv[:, k, cc:cc + m], src[:, k, 0:N]) for k in range(kch)], [slot, src], [ps])
                    consumer((done + cc) // chunk, ps, m)
                done += gw

        def ffn(l, N, pre_row, post_gain):
            rmsnorm_to(lambda ch: XT[:, ch, 0:N], XT, DC, N, D, lambda ch: vrow(pre_row, ch),
                       lambda ch: H[:, ch, 0:N], H)
            for g in range(11):
                sl = wload([(lambda s_: wview(s_, DC, 512)[:, :, 0:256], wsrc(w_gate[l], 0, D, g * 256, (g + 1) * 256)),
                            (lambda s_: wview(s_, DC, 512)[:, :, 256:512], wsrc(w_up[l], 0, D, g * 256, (g + 1) * 256))])
                wv = wview(sl, DC, 512)
                for f in range(2):
                    pg = ps_next()
                    mm(pg[:, 0:N], [(wv[:, k, f * 128:(f + 1) * 128], H[:, k, 0:N]) for k in range(DC)], [sl, H], [pg])
                    pu = ps_next()
                    mm(pu[:, 0:N], [(wv[:, k, 256 + f * 128:256 + (f + 1) * 128], H[:, k, 0:N]) for k in range(DC)], [sl, H], [pu])
                    t = tmpf()
                    ACT(t[:, 0:N], pg[:, 0:N], AF.Silu, [pg], [t])
                    DVE(lambda e, t=t, pu=pu, g=g, f=f: e.tensor_tensor(AT[:, g * 2 + f, 0:N], t[:, 0:N], pu[:, 0:N],
                                                                        op=ALU.mult), [t, pu], [AT])
            Y = Yv(N)
            for dh in range(2):
                banks = [ps_next(pin=True) for _ in range(4)]
                for rg in range(3):
                    nfr = 8 if rg < 2 else 6
                    wd = wload([(lambda s, nfr=nfr: wview(s, nfr, 512),
                                 w_down[l][rg * 1024: rg * 1024 + nfr * 128, dh * 512:(dh + 1) * 512].rearrange(
                                     "(f p) n -> p f n", p=128))])
                    wdv = wview(wd, nfr, 512)
                    for dd in range(4):
                        mm(banks[dd][:, 0:N], [(wdv[:, f, dd * 128:(dd + 1) * 128], AT[:, rg * 8 + f, 0:N])
                                               for f in range(nfr)], [wd, AT], [banks[dd]], start=(rg == 0), stop=(rg == 2))
                for dd in range(4):
                    ACT(Y[:, dh * 4 + dd, :], banks[dd][:, 0:N], AF.Copy, [banks[dd]], [YB])
                unpin(*banks)
            residual_add(N, post_gain)

        def load_xT(x_ap, row0, N):
            nb = N // 128
            xin = YB[:, 0:nb * D].rearrange("p (b f) -> p b f", b=nb)
            c.dma("pool", [(xin, x_ap[row0:row0 + N, :].rearrange("(b p) f -> p b f", p=128))], writes=[YB])
            for ch in range(DC):
                ps = ps_next()
                for b in range(nb):
                    transpose(ps[:, b * 128:(b + 1) * 128], xin[:, b, ch * 128:(ch + 1) * 128], ident_f[:],
                              [YB, ident_f], [ps])
                ACT(XT[:, ch, 0:N], ps[:, 0:N], AF.Copy, [ps], [XT])

        def store_rows(out_ap, row0, N, src_ap_fn, srcbuf, nfeat):
            nch = (nfeat + 127) // 128
            for b in range(N // 128):
                og = OSTG[st.setdefault("os", 0) % 2]
                st["os"] += 1
                for c0 in range(0, nch, 4):
                    ps = ps_next()
                    ncc = min(4, nch - c0)
                    for ch in range(c0, c0 + ncc):
                        m = min(128, nfeat - ch * 128)
                        transpose(ps[:, (ch - c0) * 128:(ch - c0) * 128 + m], src_ap_fn(ch)[0:m, b * 128:(b + 1) * 128],
                                  ident_f[0:m, 0:m], [srcbuf, ident_f], [ps])
                    wid = min(512, nfeat - c0 * 128)
                    ACT(og[:, c0 * 128:c0 * 128 + wid], ps[:, 0:wid], AF.Copy, [ps], [og])
                c.dma("act", [(out_ap[row0 + b * 128: row0 + (b + 1) * 128, :], og[:, 0:nfeat])], reads=[og], is_output=True)
                yield b, og

        def rope(psA, psB, N, out_ap, outbuf, m=64):
            t1 = tmpf()
            t2 = tmpf()
            DVE(lambda e: e.tensor_tensor(t1[0:m, 0:N], psA[0:m, 0:N], CS[:, 0, 0:N], op=ALU.mult), [psA, CS], [t1])
            DVE(lambda e: e.tensor_tensor(t2[0:m, 0:N], psB[0:m, 0:N], CS[:, 1, 0:N], op=ALU.mult), [psB, CS], [t2])
            DVE(lambda e: e.tensor_tensor(out_ap, t1[0:m, 0:N], t2[0:m, 0:N], op=ALU.add), [t1, t2], [outbuf])

        def rnn_tile(N, B, L, h0_fn, smp):
            RX = RXB[:, 0:DC * B * (L + 3)].rearrange("p (c b l) -> p c b l", c=DC, b=B)
            HS = YB[:, 0:DC * B * (L + 1)].rearrange("p (c b l) -> p c b l", c=DC, b=B)
            W1 = L + 1 if smp else L
            off = 1 if smp else 0
            n1 = B * W1
            A4 = RA[:, 0:DC * n1].rearrange("p (c b l) -> p c b l", c=DC, b=B)
            M4 = RM[:, 0:DC * n1].rearrange("p (c b l) -> p c b l", c=DC, b=B)
            W4 = RW[:, 0:DC * n1].rearrange("p (c b l) -> p c b l", c=DC, b=B)
            alias = [SQ, MIX, AT, YR, MIXR]
            DVE(lambda e: e.memset(RA[:, 0:1], 0.0), [], [RA, RM, RW] + alias)
            for ch in range(DC):
                xc = tmpf()
                xcv = xc[:, 0:N].rearrange("p (b l) -> p b l", b=B)
                DVE(lambda e: e.tensor_scalar(xcv, RX[:, ch, :, 0:L], vrow(9, ch), vrow(13, ch), op0=ALU.mult, op1=ALU.add),
                    [RXB, VT], [xc])
                for k in range(1, 4):
                    DVE(lambda e, k=k: e.scalar_tensor_tensor(xcv, RX[:, ch, :, k:k + L], vrow(9 + k, ch), xcv,
                                                              op0=ALU.mult, op1=ALU.add), [RXB, VT, xc], [xc])
                xb = tmpb()
                DVE(lambda e: e.tensor_copy(xb[:, 0:N], xc[:, 0:N]), [xc], [xb])
                pr = ps_next()
                mm(pr[:, 0:N], [(WRG[:, ch, :], xb[:, 0:N])], [WRG, xb], [pr])
                pi = ps_next()
                mm(pi[:, 0:N], [(WIG[:, ch, :], xb[:, 0:N])], [WIG, xb], [pi])
                rt = tmpf()
                it = tmpf()
                ACT(rt[:, 0:N], pr[:, 0:N], AF.Tanh, [pr, HB], [rt], bias=HB[:, ch, 0:1], scale=0.5)
                ACT(it[:, 0:N], pi[:, 0:N], AF.Tanh, [pi, HB], [it], bias=HB[:, ch, 1:2], scale=0.5)
                rtv = rt[:, 0:N].rearrange("p (b l) -> p b l", b=B)
                itv = it[:, 0:N].rearrange("p (b l) -> p b l", b=B)
                ACT(A4[:, ch, :, off:off + L], rtv, AF.Exp, [rt, DER], [RA], scale=DER[:, ch, 3:4], bias=DER[:, ch, 3:4])
                ACT(M4[:, ch, :, off:off + L], A4[:, ch, :, off:off + L], AF.Square, [RA], [RM])
                DVE(lambda e: e.scalar_tensor_tensor(W4[:, ch, :, off:off + L], itv, 1.0, xcv, op0=ALU.add, op1=ALU.mult),
                    [it, xc], [RW])
                if smp:
                    DVE(lambda e: e.memset(A4[:, ch, :, 0:1], 0.0), [], [RA])
                    DVE(lambda e: e.memset(M4[:, ch, :, 0:1], -3.0), [], [RM])
                    DVE(lambda e: e.tensor_copy(W4[:, ch, :, 0], h0_fn(ch)), [H0S], [RW])
            ACT(RM[:, 0:DC * n1], RM[:, 0:DC * n1], AF.Sqrt, [RM], [RM], bias=0.25, scale=-0.25)
            DVE(lambda e: e.tensor_tensor(RW[:, 0:DC * n1], RW[:, 0:DC * n1], RM[:, 0:DC * n1], op=ALU.mult), [RW, RM], [RW])
            for ch in range(DC):
                if smp:
                    DVE(lambda e: e.tensor_tensor_scan(HS[:, ch].rearrange("p b l -> p (b l)"), A4[:, ch].rearrange("p b l -> p (b l)"),
                                                       W4[:, ch].rearrange("p b l -> p (b l)"), 0.0, op0=ALU.mult, op1=ALU.add),
                        [RA, RW], [YB])
                else:
                    DVE(lambda e: e.tensor_tensor_scan(HS[:, ch, 0, 1:L + 1], A4[:, ch, 0, :], W4[:, ch, 0, :], CARRY[:, ch:ch + 1],
                                                       op0=ALU.mult, op1=ALU.add), [RA, RW, CARRY], [YB])
            DVE(lambda e: e.memset(RA[:, 0:1], 0.0), [], [RA, RM, RW] + alias)
            return RX, HS

        def kv_path(N, kt_buf, vv_buf, lat_out, kr_out, row0):
            slot = wload([(lambda s: wview(s, DC, 384)[:, :, 0:320], wsrc(w_in, 0, D, 384, 704)),
                          (lambda s: wview(s, DC, 384)[:, :, 320:352], wsrc(w_in, 0, D, 672, 704)),
                          (lambda s: wview(s, DC, 384)[:, :, 352:384], wsrc(w_in, 0, D, 640, 672))])
            wv = wview(slot, DC, 384)
            for j in range(2):
                ps = ps_next()
                mm(ps[:, 0:N], [(wv[:, k, j * 128:(j + 1) * 128], H[:, k, 0:N]) for k in range(DC)], [slot, H], [ps])
                ACT(SCR[:, j, 0:N], ps[:, 0:N], AF.Copy, [ps], [SCR])
            pA = ps_next()
            mm(pA[0:64, 0:N], [(wv[:, k, 256:320], H[:, k, 0:N]) for k in range(DC)], [slot, H], [pA])
            pB = ps_next()
            mm(pB[0:64, 0:N], [(wv[:, k, 320:384], H[:, k, 0:N]) for k in range(DC)], [slot, H], [pB])
            rope(pA, pB, N, KRT[:, 0:N], KRT)
            DVE(lambda e: e.tensor_copy(kt_buf[0:64, 2, 0:N], KRT[:, 0:N]), [KRT], [kt_buf])
            rmsnorm_to(lambda ch: SCR[:, ch, 0:N], SCR, 2, N, 256.0, lambda ch: vrow(18, ch),
                       lambda ch: SCR[:, 2 + ch, 0:N], SCR)
            for j in range(2):
                DVE(lambda e, j=j: e.tensor_copy(kt_buf[:, j, 0:N], SCR[:, 2 + j, 0:N]), [SCR], [kt_buf])
            for b, og in store_rows(lat_out, row0, N, lambda ch: SCR[:, 2 + ch, 0:N], SCR, 256) if lat_out is not None else \
                    transposed_blocks(N, lambda ch: SCR[:, 2 + ch, 0:N], SCR, 256):
                DVE(lambda e, b=b, og=og: e.tensor_copy(vv_buf[:, b, :], og[:, 0:256]), [og], [vv_buf])
            if kr_out is not None:
                for _ in store_rows(kr_out, row0, N, lambda ch: KRT[:, 0:N], KRT, 64):
                    pass

        def transposed_blocks(N, src_ap_fn, srcbuf, nfeat):
            nch = (nfeat + 127) // 128
            for b in range(N // 128):
                og = OSTG[st.setdefault("os", 0) % 2]
                st["os"] += 1
                ps = ps_next()
                for ch in range(nch):
                    transpose(ps[:, ch * 128:(ch + 1) * 128], src_ap_fn(ch)[:, b * 128:(b + 1) * 128], ident_f[:],
                              [srcbuf, ident_f], [ps])
                ACT(og[:, 0:nfeat], ps[:, 0:nfeat], AF.Copy, [ps], [og])
                yield b, og

        def q_path(N, qr_buf):
            def cons(ci, ps, m):
                ACT(CQ[:, ci, 0:N], ps[:, 0:N], AF.Copy, [ps], [CQ])
            proj(w_in, 0, 384, N, cons)
            rmsnorm_to(lambda ch: CQ[:, ch, 0:N], CQ, 3, N, 384.0, lambda ch: vrow(17, ch),
                       lambda ch: CQN[:, ch, 0:N], CQN)
            uq4 = w_uq.rearrange("(k p) (h f) -> p k h f", p=128, f=192)
            s2parts = []
            for k in range(3):
                s2parts.append((lambda s, k=k: wview(s, 3, 512).rearrange("p k (h f) -> p k h f", f=64)[:, k, :, 0:32], uq4[:, k, :, 160:192]))
                s2parts.append((lambda s, k=k: wview(s, 3, 512).rearrange("p k (h f) -> p k h f", f=64)[:, k, :, 32:64], uq4[:, k, :, 128:160]))
            s2 = wload(s2parts)
            uqs = wview(s2, 3, 512)
            s1h = [wload([(lambda s: wview(s, 3, 768), w_uq[:, hf * 768:(hf + 1) * 768].rearrange("(k p) n -> p k n", p=128))])
                   for hf in range(2)]
            for hh in range(8):
                s1 = s1h[hh // 4]
                uq = wview(s1, 3, 768)
                hl = hh % 4
                ps = ps_next()
                mm(ps[:, 0:N], [(uq[:, k, hl * 192: hl * 192 + 128], CQN[:, k, 0:N]) for k in range(3)], [s1, CQN], [ps])
                ACT(QN[:, hh, 0:N], ps[:, 0:N], AF.Copy, [ps], [QN])
                pA = ps_next()
                mm(pA[0:64, 0:N], [(uq[:, k, hl * 192 + 128: hl * 192 + 192], CQN[:, k, 0:N]) for k in range(3)], [s1, CQN], [pA])
                pB = ps_next()
                mm(pB[0:64, 0:N], [(uqs[:, k, hh * 64:(hh + 1) * 64], CQN[:, k, 0:N]) for k in range(3)], [s2, CQN], [pB])
                rope(pA, pB, N, qr_buf[0:64, hh, 0:N], qr_buf)
                for j in range(2):
                    ps2 = ps_next()
                    mm(ps2[:, 0:N], [(WUKT[:, hh, j * 128:(j + 1) * 128], QN[:, hh, 0:N])], [WUKT, QN], [ps2])
                    ACT(QL[:, j, hh, 0:N], ps2[:, 0:N], AF.Copy, [ps2], [QL])

        def finish_heads(accs, den, n_rows, dst_fn, src_view=lambda ap: ap):
            DVE(lambda e: e.reciprocal(RD[0:n_rows, 0:len(accs)], den[0:n_rows, 0:len(accs)]), [den], [RD])
            for hh, (abuf, aap) in enumerate(accs):
                ACT(ONF[0:n_rows, :], aap, AF.Copy, [abuf, RD], [ONF], scale=RD[0:n_rows, hh:hh + 1])
                for j in range(2):
                    ps = ps_next()
                    transpose(ps[:, 0:n_rows], ONF[0:n_rows, j * 128:(j + 1) * 128], ident_f[0:n_rows, 0:n_rows],
                              [ONF, ident_f], [ps])
                    ACT(dst_fn(j, hh), src_view(ps[:, 0:n_rows]), AF.Copy, [ps], [OLAT])

        def prompt_attention(ot):
            for qb in range(NBLK):
                keyblocks = [(kt, kb) for kt in range(NT) for kb in range(NBLK)]
                keyblocks += [(NT + kt, kb) for kt in range(ot) for kb in range(NBLK)]
                keyblocks += [(NT + ot, kb) for kb in range(qb + 1)]
                for g in range(2):
                    acc = [ps_next(pin=True), ps_next(pin=True)]
                    den = ps_next(pin=True)
                    nk = len(keyblocks)

                    def stage_s(ki):
                        kt, kb = keyblocks[ki]
                        diag = (kt == NT + ot and kb == qb)
                        S = ps_next()
                        pairs = [(KT[kt][:, j, kb * 128:(kb + 1) * 128], QL[:, j, 4 * g:4 * g + 4, qb * 128:(qb + 1) * 128])
                                 for j in range(2)]
                        pairs.append((KT[kt][:, 2, kb * 128:(kb + 1) * 128], QRT[:, 4 * g:4 * g + 4, qb * 128:(qb + 1) * 128]))
                        rd = [KT[kt], QL, QRT]
                        if diag:
                            pairs.append((ident_b[:], maskb[:]))
                            rd += [ident_b, maskb]
                        mm(S[:].rearrange("p (h q) -> p h q", h=4), pairs, rd, [S])
                        P = pt_next()
                        ACT(P[:], S[:], AF.Exp, [S], [P], scale=SCALE)
                        return P

                    def stage_pv(ki, P):
                        kt, kb = keyblocks[ki]
                        for hh in range(4):
                            a = acc[hh // 2]
                            mm(a[:, (hh % 2) * 256:(hh % 2 + 1) * 256], [(P[:, hh * 128:(hh + 1) * 128], VV[kt][:, kb, :])],
                               [P, VV[kt]], [a], start=(ki == 0), stop=(ki == nk - 1))
                            mm(den[:, hh:hh + 1], [(P[:, hh * 128:(hh + 1) * 128], ones_b[:, 0:1])], [P, ones_b], [den],
                               start=(ki == 0), stop=(ki == nk - 1))
                    prev = stage_s(0)
                    for ki in range(1, nk):
                        cur = stage_s(ki)
                        stage_pv(ki - 1, prev)
                        prev = cur
                    stage_pv(nk - 1, prev)
                    finish_heads([(acc[hh // 2], acc[hh // 2][:, (hh % 2) * 256:(hh % 2 + 1) * 256]) for hh in range(4)],
                                 den, 128, lambda j, hh, g=g, qb=qb: OLAT[:, j, 4 * g + hh, qb * 128:(qb + 1) * 128])
                    unpin(acc[0], acc[1], den)

        def sample_attention():
            c.barrier()
            GP = 8
            NG = 64 // GP
            lat_g = pool_lat.rearrange("(g r) f -> g (r f)", r=8)
            kr_g = pool_kr.rearrange("(g r) f -> g (r f)", r=8)
            groups = [(b, gi) for b in range(NSMP) for gi in range(NG)]

            def stage_a(n):
                b, gi = groups[n]
                pk = PGK[n % 3]
                pr = PGR[n % 3]
                ktp = KTP[n % 2]
                col = b * NG + gi
                c.dma("pool", [(pk[:].rearrange("p r f -> p (r f)"), lat_g, IDXG[:, col:col + 1])], reads=[IDXG], writes=[pk])
                c.dma("pool", [(pr[:].rearrange("p r f -> p (r f)"), kr_g, IDXG[:, col:col + 1])], reads=[IDXG], writes=[pr])
                for p in range(GP):
                    tp = ps_next()
                    tpb = tp[:].bitcast(BF16)
                    for j in range(2):
                        transpose(tpb[:, j * 128:(j + 1) * 128], pk[:, p, j * 128:(j + 1) * 128], ident_b[:],
                                  [pk, ident_b], [tp])
                    transpose(tpb[0:64, 256:384], pr[:, p, :], ident_b[:], [pr, ident_b], [tp])
                    ACT(ktp[:, 0:2, p, :], tpb[:, 0:256].rearrange("p (j k) -> p j k", j=2), AF.Copy, [tp], [ktp])
                    DVE(lambda e, tpb=tpb, p=p, ktp=ktp: e.tensor_copy(ktp[0:64, 2, p, :], tpb[0:64, 256:384]), [tp], [ktp])

            def stage_b(n):
                b, gi = groups[n]
                ktp = KTP[n % 2]
                S = ps_next()
                for p in range(GP):
                    pairs = [(ktp[:, j, p, :], QL[:, j, :, b * 8:(b + 1) * 8]) for j in range(2)]
                    pairs.append((ktp[0:64, 2, p, :], QRS[0:64, :, b * 8:(b + 1) * 8]))
                    mm(S[:, p * 64:(p + 1) * 64].rearrange("p (h t) -> p h t", h=8), pairs, [ktp, QL, QRS], [S])
                P = pt_next()
                ACT(P[:, 0:GP * 64], S[:, 0:GP * 64], AF.Exp, [S], [P], scale=SCALE)
                return P

            def pv(acc, den, P, vlist, first, last):
                for p, (vb, vap) in enumerate(vlist):
                    mm(acc[0:64, 0:256], [(P[:, p * 64:(p + 1) * 64], vap)], [P, vb], [acc], start=(first and p == 0), stop=last)
                    mm(den[0:64, 0:1], [(P[:, p * 64:(p + 1) * 64], ones_b[:, 0:1])], [P, ones_b], [den],
                       start=(first and p == 0), stop=last)

            ng = len(groups)
            stage_a(0)
            acc = den = None
            for n in range(ng):
                b, gi = groups[n]
                if gi == 0:
                    acc = ps_next(pin=True)
                    den = ps_next(pin=True)
                P = stage_b(n)
                if n + 1 < ng:
                    stage_a(n + 1)
                pk = PGK[n % 3]
                pv(acc, den, P, [(pk, pk[:, p, :]) for p in range(GP)], gi == 0, False)
                if gi == NG - 1:
                    S = ps_next()
                    pairs = [(KTS[:, j, :], QL[:, j, :, b * 8:(b + 1) * 8]) for j in range(2)]
                    pairs.append((KTS[0:64, 2, :], QRS[0:64, :, b * 8:(b + 1) * 8]))
                    pairs.append((ident_b[:], masks[:, b, :, :]))
                    mm(S[:, 0:64].rearrange("p (h t) -> p h t", h=8), pairs, [KTS, QL, QRS, ident_b, masks], [S])
                    P2 = pt_next()
                    ACT(P2[:, 0:64], S[:, 0:64], AF.Exp, [S], [P2], scale=SCALE)
                    pv(acc, den, P2, [(VVS, VVS[:, 0, :])], False, True)
                    finish_heads([(acc, acc[0:64, 0:256])], den, 64,
                                 lambda j, hh, b=b: OLAT[:, j, :, b * 8:(b + 1) * 8],
                                 src_view=lambda ap: ap.rearrange("p (h t) -> p h t", h=8))
                    unpin(acc, den)

        def mem_attention_prompt(N):
            for hh in range(4):
                Ps = []
                for mb in range(2):
                    S = ps_next()
                    mm(S[:, 0:N], [(MKT[:, 2 * hh + j, mb * 128:(mb + 1) * 128], QM[:, 2 * hh + j, 0:N]) for j in range(2)],
                       [MKT, QM], [S])
                    P = pt_next()
                    ACT(P[:, 0:N], S[:, 0:N], AF.Exp, [S], [P], scale=1.0 / 16.0)
                    Ps.append(P)
                dn = ps_next()
                mm(dn[:, 0:N], [(ones_b[:], Ps[mb][:, 0:N]) for mb in range(2)], [ones_b] + Ps, [dn])
                rdn = tmpf()
                DVE(lambda e: e.reciprocal(rdn[:, 0:N], dn[:, 0:N]), [dn], [rdn])
                for j in range(2):
                    po = ps_next()
                    mm(po[:, 0:N], [(MV[:, mb, (2 * hh + j) * 128:(2 * hh + j + 1) * 128], Ps[mb][:, 0:N]) for mb in range(2)],
                       [MV] + Ps, [po])
                    DVE(lambda e, po=po, j=j: e.tensor_tensor(OM[:, 2 * hh + j, 0:N], po[:, 0:N], rdn[:, 0:N], op=ALU.mult),
                        [po, rdn], [OM])

        def mem_attention_sample():
            for b in range(NSMP):
                mk = MKS[b % 2]
                mv = MVS[b % 2]
                mkt = MKST[b % 2]
                c.dma("pool", [(mk[:], mem_k_s[b].rearrange("(m p) f -> p m f", p=128))], writes=[mk])
                c.dma("pool", [(mv[:], mem_v_s[b].rearrange("(m p) f -> p m f", p=128))], writes=[mv])
                for ch in range(DC):
                    tp = ps_next()
                    tpb = tp[:].bitcast(BF16)
                    for mb in range(2):
                        transpose(tpb[:, mb * 128:(mb + 1) * 128], mk[:, mb, ch * 128:(ch + 1) * 128], ident_b[:],
                                  [mk, ident_b], [tp])
                    ACT(mkt[:, ch, :], tpb[:, 0:256], AF.Copy, [tp], [mkt])
                S = ps_next()
                for hh in range(4):
                    for mb in range(2):
                        mm(S[:, mb * 32 + hh * 8: mb * 32 + (hh + 1) * 8],
                           [(mkt[:, 2 * hh + j, mb * 128:(mb + 1) * 128], QM[:, 2 * hh + j, b * 8:(b + 1) * 8]) for j in range(2)],
                           [mkt, QM], [S])
                Pb = pt_next()
                ACT(Pb[:, 0:64], S[:, 0:64], AF.Exp, [S], [Pb], scale=1.0 / 16.0)
                dn = ps_next()
                mm(dn[:, 0:32], [(ones_b[:], Pb[:, mb * 32:(mb + 1) * 32]) for mb in range(2)], [ones_b, Pb], [dn])
                rdn = tmpf()
                DVE(lambda e, dn=dn, rdn=rdn: e.reciprocal(rdn[:, 0:32], dn[:, 0:32]), [dn], [rdn])
                po = ps_next()
                for ch in range(DC):
                    hh = ch // 2
                    mm(po[:, ch * 8:(ch + 1) * 8],
                       [(mv[:, mb, ch * 128:(ch + 1) * 128], Pb[:, mb * 32 + hh * 8: mb * 32 + (hh + 1) * 8]) for mb in range(2)],
                       [mv, Pb], [po])
                for ch in range(DC):
                    hh = ch // 2
                    DVE(lambda e, po=po, rdn=rdn, hh=hh, ch=ch, b=b: e.tensor_tensor(
                        OM[:, ch, b * 8:(b + 1) * 8], po[:, ch * 8:(ch + 1) * 8], rdn[:, hh * 8:(hh + 1) * 8], op=ALU.mult),
                        [po, rdn], [OM])

        def process_tile(kind, ti):
            own = kind == "own"
            smp = kind == "smp"
            N = 128 if smp else TT
            B, L = (NSMP, 8) if smp else (1, TT)
            xsrc = {"oth": x_oth, "own": x_own, "smp": x_smp}[kind]
            cssrc = {"oth": cs_oth, "own": cs_own, "smp": cs_smp}[kind]
            row0 = 0 if smp else ti * TT
            tok0 = {"oth": 0, "own": SEQ_HALF, "smp": 2 * SEQ_HALF}[kind] + row0
            c.dma("pool", [(XT[:, :, 0:N], X1.ap[:, :, tok0:tok0 + N].rearrange("c p n -> p c n"))], reads=[X1.buf], writes=[XT])
            c.dma("pool", [(CS[:, :, 0:N], cssrc[:, :, row0:row0 + N].rearrange("a r n -> r a n"))], writes=[CS])
            rmsnorm_to(lambda ch: XT[:, ch, 0:N], XT, DC, N, D, lambda ch: vrow(2, ch), lambda ch: H[:, ch, 0:N], H)
            if smp:
                kv_path(N, KTS, VVS, lat_smp, kr_smp, 0)
            elif own:
                kv_path(N, KT[NT + ti], VV[NT + ti], lat_own, kr_own, row0)
            else:
                kv_path(N, KT[ti], VV[ti], None, None, 0)
            RX = RXB[:, 0:DC * B * (L + 3)].rearrange("p (c b l) -> p c b l", c=DC, b=B)
            if smp:
                c.dma("pool", [(STG[0:48, :], conv_s)], writes=[STG])
                for ch in range(DC):
                    ps = ps_next()
                    transpose(ps[:, 0:48], STG[0:48, ch * 128:(ch + 1) * 128], ident_f[0:48, 0:48], [STG, ident_f], [ps])
                    ACT(RX[:, ch, :, 0:3], ps[:, 0:48].rearrange("p (b k) -> p b k", b=NSMP), AF.Copy, [ps], [RXB])
                c.dma("pool", [(STG[0:16, :], h_s)], writes=[STG])
                for ch in range(DC):
                    ps = ps_next()
                    transpose(ps[:, 0:16], STG[0:16, ch * 128:(ch + 1) * 128], ident_f[0:16, 0:16], [STG, ident_f], [ps])
                    ACT(H0S[:, ch, :], ps[:, 0:16], AF.Copy, [ps], [H0S])
            else:
                if own and ti == 0:
                    DVE(lambda e: e.tensor_scalar(HALO[:], HALO[:], CMASK[:, 0:1], None, op0=ALU.mult), [HALO, CMASK], [HALO])
                    DVE(lambda e: e.tensor_scalar(CARRY[:], CARRY[:], CMASK[:, 0:1], None, op0=ALU.mult), [CARRY, CMASK], [CARRY])
                DVE(lambda e: e.tensor_copy(RX[:, :, 0, 0:3], HALO[:]), [HALO], [RXB])

            def cons_rx(ci, ps, m):
                ACT(RX[:, ci, :, 3:3 + L], ps[:, 0:N].rearrange("p (b l) -> p b l", b=B), AF.Copy, [ps], [RXB])
            proj(w_in, 704, 1024, N, cons_rx)
            if not smp:
                DVE(lambda e: e.tensor_copy(HALO[:], RX[:, :, 0, L:L + 3]), [RXB], [HALO])
            if smp:
                for ch in range(DC):
                    ps = ps_next()
                    t = tmpf()
                    DVE(lambda e, t=t, ch=ch: e.tensor_copy(t[:, 0:48].rearrange("p (b k) -> p b k", b=NSMP), RX[:, ch, :, 8:11]),
                        [RXB], [t])
                    transpose(ps[0:48, 0:128], t[:, 0:48], ident_f[:], [t, ident_f], [ps])
                    ACT(STG[0:48, ch * 128:(ch + 1) * 128], ps[0:48, 0:128], AF.Copy, [ps], [STG])
                c.dma("act", [(conv_smp, STG[0:48, :])], reads=[STG], is_output=True)
            elif own and ti == NT - 1:
                for ch in range(DC):
                    ps = ps_next()
                    t = tmpf()
                    DVE(lambda e, t=t, ch=ch: e.tensor_copy(t[:, 0:3], RX[:, ch, 0, L:L + 3]), [RXB], [t])
                    transpose(ps[0:3, 0:128], t[:, 0:3], ident_f[:], [t, ident_f], [ps])
                    ACT(STG[0:3, ch * 128:(ch + 1) * 128], ps[0:3, 0:128], AF.Copy, [ps], [STG])
                c.dma("act", [(conv_p, STG[0:3, :])], reads=[STG], is_output=True)
            h0_fn = (lambda ch: H0S[:, ch, :]) if smp else (lambda ch: CARRY[:, ch:ch + 1])
            RX, HS = rnn_tile(N, B, L, h0_fn, smp)
            if smp:
                for ch in range(DC):
                    ps = ps_next()
                    t = tmpf()
                    DVE(lambda e, t=t, ch=ch: e.tensor_copy(t[:, 0:NSMP], HS[:, ch, :, L]), [YB], [t])
                    transpose(ps[0:16, 0:128], t[:, 0:16], ident_f[:], [t, ident_f], [ps])
                    ACT(STG[0:16, ch * 128:(ch + 1) * 128], ps[0:16, 0:128], AF.Copy, [ps], [STG])
                c.dma("act", [(h_smp, STG[0:16, :])], reads=[STG], is_output=True)
            else:
                DVE(lambda e: e.tensor_copy(CARRY[:], HS[:, :, 0, L]), [YB], [CARRY])
                if own and ti == NT - 1:
                    ps = ps_next()
                    transpose(ps[0:8, 0:128], CARRY[:], ident_f[:], [CARRY, ident_f], [ps])
                    ACT(STG[0:8, 0:128], ps[0:8, 0:128], AF.Copy, [ps], [STG])
                    c.dma("act", [(h_p, STG[0:8, 0:128])], reads=[STG], is_output=True)
            if not (own or smp):
                return

            def cons_rg(ci, ps, m):
                t = tmpf()
                ACT(t[:, 0:N], ps[:, 0:N], AF.Gelu, [ps], [t])
                DVE(lambda e, t=t, ci=ci: e.tensor_tensor(YR[:, ci, 0:N].rearrange("p (b l) -> p b l", b=B), HS[:, ci, :, 1:L + 1],
                                                          t[:, 0:N].rearrange("p (b l) -> p b l", b=B), op=ALU.mult), [YB, t], [YR])
            proj(w_in, 1728, 1024, N, cons_rg)
            sg = {}

            def cons_sg(ci, ps, m):
                t = tmpf()
                ACT(t[:, 0:N], ps[:, 0:N], AF.Sigmoid, [ps], [t])
                sg[ci] = t
            for half in range(2):
                sg.clear()
                proj(w_in, 3776 + half * 512, 512, N, cons_sg)

                def cons_orn(ci, ps, m, half=half):
                    DVE(lambda e, ps=ps, ci=ci: e.tensor_tensor(MIXR[:, half * 4 + ci, 0:N], sg[ci][:, 0:N], ps[:, 0:N], op=ALU.mult),
                        [sg[ci], ps], [MIXR])
                proj(w_o_rnn, half * 512, 512, N, cons_orn, src=YR)
            q_path(N, QRS if smp else QRT)
            if smp:
                sample_attention()
            else:
                prompt_attention(ti)
            for hh in range(8):
                ps = ps_next()
                mm(ps[:, 0:N], [(WUV[:, j, hh * 128:(hh + 1) * 128], OLAT[:, j, hh, 0:N]) for j in range(2)], [WUV, OLAT], [ps])
                ACT(YR[:, hh, 0:N], ps[:, 0:N], AF.Copy, [ps], [YR])
            for half in range(2):
                sg.clear()
                proj(w_in, 2752 + half * 512, 512, N, cons_sg)

                def cons_om(ci, ps, m, half=half):
                    t = tmpf()
                    DVE(lambda e, ps=ps, ci=ci, t=t: e.tensor_tensor(t[:, 0:N], sg[ci][:, 0:N], ps[:, 0:N], op=ALU.mult), [sg[ci], ps], [t])
                    DVE(lambda e, ci=ci, t=t: e.tensor_tensor(MIX[:, half * 4 + ci, 0:N], t[:, 0:N], MIXR[:, half * 4 + ci, 0:N], op=ALU.add),
                        [t, MIXR], [MIX])
                proj(w_o_mla, half * 512, 512, N, cons_om, src=YR)
            Y = Yv(N)

            def cons_y(ci, ps, m):
                ACT(Y[:, ci, :], ps[:, 0:N], AF.Copy, [ps], [YB])
            proj(w_out, 0, 1024, N, cons_y, src=MIX)
            residual_add(N, lambda ch: vrow(3, ch))
            rmsnorm_to(lambda ch: XT[:, ch, 0:N], XT, DC, N, D, lambda ch: vrow(4, ch), lambda ch: H[:, ch, 0:N], H)

            def cons_qm(ci, ps, m):
                ACT(QM[:, ci, 0:N], ps[:, 0:N], AF.Copy, [ps], [QM])
            proj(w_mem_q, 0, 1024, N, cons_qm)
            if smp:
                mem_attention_sample()
            else:
                mem_attention_prompt(N)
            proj(w_mem_o, 0, 1024, N, cons_y, src=OM)
            residual_add(N, lambda ch: vrow(5, ch))
            tok3 = (SEQ_HALF if smp else 0) + row0
            c.dma("pool", [(X3.ap[:, :, tok3:tok3 + N].rearrange("c p n -> p c n"), XT[:, :, 0:N])], reads=[XT], writes=[X3.buf],
                  sembuf=X3.buf)

        def prompt_mem_kv():
            N = 256
            load_xT(mem_p, 0, N)
            rmsnorm_to(lambda ch: XT[:, ch, 0:N], XT, DC, N, D, lambda ch: vrow(8, ch), lambda ch: H[:, ch, 0:N], H)

            def cons_k(ci, ps, m):
                ACT(MKT[:, ci, :], ps[:, 0:N], AF.Copy, [ps], [MKT])
            proj(w_mem_k, 0, 1024, N, cons_k)
            for wi, (w_ap, o_ap) in enumerate(((w_mem_k, mk_p), (w_mem_v, mv_p))):
                for half in range(2):
                    slot = wload([(lambda s: wview(s, DC, 512), wsrc(w_ap, 0, D, half * 512, (half + 1) * 512))])
                    wv = wview(slot, DC, 512)
                    for mb in range(2):
                        ps = ps_next()
                        mm(ps[:, 0:512], [(H[:, k, mb * 128:(mb + 1) * 128], wv[:, k, :]) for k in range(DC)], [slot, H], [ps])
                        og = OSTG[st.setdefault("os", 0) % 2]
                        st["os"] += 1
                        ACT(og[:, 0:512], ps[:, 0:512], AF.Copy, [ps], [og])
                        c.dma("act", [(o_ap[mb * 128:(mb + 1) * 128, half * 512:(half + 1) * 512], og[:, 0:512])], reads=[og], is_output=True)
                        if wi == 1:
                            DVE(lambda e, og=og, mb=mb, half=half: e.tensor_copy(MV[:, mb, half * 512:(half + 1) * 512], og[:, 0:512]),
                                [og], [MV])

        PI = OSTG[1]
        PF = OSTG[0]
        c.dma("pool", [(PI[:].bitcast(I32), ptab.to_broadcast([128, NSMP * 64]))], writes=[PI])
        c.op("pool", lambda e: e.iota(IOTA[:], [[0, 1]], base=0, channel_multiplier=1, allow_small_or_imprecise_dtypes=True),
             writes=[IOTA])
        DVE(lambda e: e.tensor_copy(PF[:], PI[:].bitcast(I32)), [PI], [PF])
        OH = tmpf()
        c.op("pool", lambda e: e.memset(OH[:, 0:8], 1.0), writes=[OH])
        c.op("pool", lambda e: e.affine_select(OH[:, 0:8], OH[:, 0:8], [[-16, 8]], ALU.is_ge, 0.0, base=0, channel_multiplier=1),
             reads=[OH], writes=[OH])
        c.op("pool", lambda e: e.affine_select(OH[:, 0:8], OH[:, 0:8], [[16, 8]], ALU.is_ge, 0.0, base=15, channel_multiplier=-1),
             reads=[OH], writes=[OH])
        ACCI = tmpf()
        PF3 = PF[:].rearrange("p (c k) -> p c k", k=8)
        DVE(lambda e: e.tensor_scalar(ACCI[:, 0:128], PF3[:, :, 0], OH[:, 0:1], None, op0=ALU.mult), [PF, OH], [ACCI])
        for k in range(1, 8):
            DVE(lambda e, k=k: e.scalar_tensor_tensor(ACCI[:, 0:128], PF3[:, :, k], OH[:, k:k + 1], ACCI[:, 0:128],
                                                      op0=ALU.mult, op1=ALU.add), [PF, OH, ACCI], [ACCI])
        PM = tmpf()
        DVE(lambda e: e.tensor_scalar(PM[:, 1:2], OH[:, 1:2], 1.0, None, op0=ALU.mult), [OH], [PM])
        for k in range(2, 8):
            DVE(lambda e, k=k: e.scalar_tensor_tensor(PM[:, 1:2], OH[:, k:k + 1], float(k), PM[:, 1:2], op0=ALU.mult, op1=ALU.add),
                [OH, PM], [PM])
        DVE(lambda e: e.scalar_tensor_tensor(PM[:, 0:1], PM[:, 1:2], -16.0, IOTA[:, 0:1], op0=ALU.mult, op1=ALU.add), [PM, IOTA], [PM])
        DVE(lambda e: e.tensor_scalar(ACCI[:, 0:128], ACCI[:, 0:128], 16.0, PM[:, 0:1], op0=ALU.mult, op1=ALU.add), [ACCI, PM], [ACCI])
        DVE(lambda e: e.tensor_copy(IDXG[:], ACCI[:, 0:128]), [ACCI], [IDXG])

        def scratch(name, shape):
            t = nc.dram_tensor(name, list(shape), F32, kind="Internal")
            b = Buf(c, t, name, "dscr")
            c.all_bufs.append(b)
            return WRef(t.ap(), b)
        X1 = scratch("X1", [DC, 128, 2 * SEQ_HALF + 128])
        X3 = scratch("X3", [DC, 128, SEQ_HALF + 128])
        main_set = (XT, YB, H, SQ, AT, TF, RS)
        ffn_set = (XT5, YB5, H5, SQ5, AT5, TF5, RS5)

        c.barrier()
        XT, YB, H, SQ, AT, TF, RS = ffn_set
        jobs = [(x_oth, t * TF_, TF_, t * TF_) for t in range(SEQ_HALF // TF_)]
        jobs += [(x_own, t * TF_, TF_, SEQ_HALF + t * TF_) for t in range(SEQ_HALF // TF_)]
        jobs += [(x_smp, 0, 128, 2 * SEQ_HALF)]
        for (xs, r0, n, tk) in jobs:
            load_xT(xs, r0, n)
            ffn(0, n, 0, lambda ch: DER[:, ch, 1:2])
            c.dma("pool", [(X1.ap[:, :, tk:tk + n].rearrange("c p n -> p c n"), XT[:, :, 0:n])], reads=[XT], writes=[X1.buf],
                  sembuf=X1.buf)
        c.barrier()
        XT, YB, H, SQ, AT, TF, RS = main_set

        prompt_mem_kv()
        for ti in range(NT):
            process_tile("oth", ti)
        for ti in range(NT):
            process_tile("own", ti)
        process_tile("smp", 0)

        c.barrier()
        XT, YB, H, SQ, AT, TF, RS = ffn_set
        jobs = [(y_own, t * TF_, TF_, t * TF_) for t in range(SEQ_HALF // TF_)] + [(y_smp, 0, 128, SEQ_HALF)]
        for (yo, r0, n, tk) in jobs:
            c.dma("pool", [(XT[:, :, 0:n], X3.ap[:, :, tk:tk + n].rearrange("c p n -> p c n"))], reads=[X3.buf], writes=[XT])
            ffn(1, n, 6, lambda ch: DER[:, ch, 2:3])
            for _ in store_rows(yo, r0, n, lambda ch: XT[:, ch, 0:n], XT, D):
                pass
        c.finish("sp")
        print("build: sems", c.nsem, {k: (v.n_inst, v.n_wait) for k, v in c.engs.items()})
    return nc


_CACHE = {}


def _rope_tables(pos):
    inv = (np.float32(10000.0) ** (-np.arange(0, 64, 2, dtype=np.float32) / np.float32(64))).astype(np.float32)
    ang = pos.astype(np.float32)[:, None] * inv[None, :]
    cos = np.cos(ang).astype(np.float32).T
    sin = np.sin(ang).astype(np.float32).T
    return np.ascontiguousarray(np.stack([np.concatenate([cos, cos], 0), np.concatenate([-sin, sin], 0)], 0))


def kernel(x_prompt, x_sample, mem_prompt, cache_mla_latent, cache_mla_krope, page_table,
           state_rnn_conv, state_rnn_h, cache_mem_k, cache_mem_v,
           norms, w_ffn_gate, w_ffn_up, w_ffn_down, w_in, q_norm, kv_norm, w_uq, w_uk, w_uv,
           w_o_mla, conv_w, conv_b, w_rg, b_rg, w_ig, b_ig, lru_lambda, w_o_rnn, w_out,
           mem_norm, w_mem_q, w_mem_k, w_mem_v, w_mem_o):
    f32 = np.float32
    A = lambda a: np.ascontiguousarray(np.asarray(a))
    x_prompt, x_sample, mem_prompt = A(x_prompt), A(x_sample), A(mem_prompt)
    n_pool = cache_mla_latent.shape[1]
    pool_lat = A(cache_mla_latent).reshape(n_pool * 128, 256)
    pool_kr = A(cache_mla_krope).reshape(n_pool * 128, 64)
    page_table = A(page_table).astype(np.int32)
    vec = np.zeros((19, D), f32)
    vec[0:8] = A(norms)[0]
    vec[8] = A(mem_norm)[0]
    vec[9:13] = A(conv_w)[0]
    vec[13] = A(conv_b)[0]
    vec[14] = A(b_rg)[0]
    vec[15] = A(b_ig)[0]
    vec[16] = A(lru_lambda)[0]
    vec[17, :384] = A(q_norm)[0]
    vec[18, :256] = A(kv_norm)[0]
    shared = {
        "pool_lat": pool_lat, "pool_kr": pool_kr, "vec": vec,
        "w_gate": A(w_ffn_gate)[0], "w_up": A(w_ffn_up)[0], "w_down": A(w_ffn_down)[0], "w_in": A(w_in)[0],
        "w_uq": A(w_uq)[0], "w_uk": A(w_uk)[0].reshape(256, 1024), "w_uv": A(w_uv)[0].reshape(256, 1024),
        "w_o_mla": A(w_o_mla)[0], "w_rg": A(w_rg)[0], "w_ig": A(w_ig)[0], "w_o_rnn": A(w_o_rnn)[0], "w_out": A(w_out)[0],
        "w_mem_q": A(w_mem_q)[0], "w_mem_k": A(w_mem_k)[0], "w_mem_v": A(w_mem_v)[0], "w_mem_o": A(w_mem_o)[0],
    }
    past = page_table.shape[1] * 128
    cs_first = _rope_tables(np.arange(0, SEQ_HALF))
    cs_second = _rope_tables(np.arange(SEQ_HALF, 2 * SEQ_HALF))
    cs_smp = _rope_tables(np.tile(past + np.arange(8), NSMP))
    kmask = np.concatenate([np.ones(SEQ_HALF, f32), np.zeros(SEQ_HALF, f32)])[None, :]
    in_maps = []
    for core in range(NCORES):
        s, half = core // 2, core % 2
        bs = slice(core * NSMP, (core + 1) * NSMP)
        m = dict(shared)
        m["x_oth"] = x_prompt[s, 0:SEQ_HALF]
        m["x_own"] = x_prompt[s, half * SEQ_HALF:(half + 1) * SEQ_HALF]
        m["x_smp"] = x_sample[bs].reshape(128, D)
        m["mem_p"] = mem_prompt[s]
        m["ptab"] = page_table[bs].reshape(1, NSMP * 64)
        m["conv_s"] = A(state_rnn_conv)[0, bs].reshape(NSMP * 3, D)
        m["h_s"] = A(state_rnn_h)[0, bs]
        m["mem_k_s"] = A(cache_mem_k)[0, bs].reshape(NSMP, 256, D)
        m["mem_v_s"] = A(cache_mem_v)[0, bs].reshape(NSMP, 256, D)
        m["cs_oth"] = cs_first
        m["cs_own"] = cs_second if half else cs_first
        m["cs_smp"] = cs_smp
        m["kmask"] = kmask
        m["pmask"] = np.full((1, 8 * TT), 0.0 if half else NEG, f32)
        m["cmask"] = np.full((128, 1), 1.0 if half else 0.0, f32)
        in_maps.append(m)
    key = n_pool
    if key not in _CACHE:
        _CACHE[key] = build(n_pool)
    nc = _CACHE[key]
    res = run_bass_kernel_spmd(nc, in_maps, core_ids=list(range(NCORES))).results
    B, S = x_prompt.shape[0], x_prompt.shape[1]
    y_p = np.zeros((B, S, D), f32)
    lat_p = np.zeros((1, B, S, 256), f32)
    kr_p = np.zeros((1, B, S, 64), f32)
    conv_p = np.zeros((1, B, 3, D), f32)
    h_p = np.zeros((1, B, D), f32)
    mk_p = np.zeros((1, B, 256, 4, 256), f32)
    mv_p = np.zeros((1, B, 256, 4, 256), f32)
    y_s = np.zeros((128, 8, D), f32)
    lat_s = np.zeros((1, 128, 8, 256), f32)
    kr_s = np.zeros((1, 128, 8, 64), f32)
    conv_sn = np.zeros((1, 128, 3, D), f32)
    h_sn = np.zeros((1, 128, D), f32)
    for core in range(NCORES):
        r = res[core]
        s, half = core // 2, core % 2
        sl = slice(half * SEQ_HALF, (half + 1) * SEQ_HALF)
        bs = slice(core * NSMP, (core + 1) * NSMP)
        y_p[s, sl] = r["y_own"]
        lat_p[0, s, sl] = r["lat_own"]
        kr_p[0, s, sl] = r["kr_own"]
        if half == 1:
            conv_p[0, s] = r["conv_p"]
            h_p[0, s] = r["h_p"].reshape(D)
        else:
            mk_p[0, s] = r["mk_p"].reshape(256, 4, 256)
            mv_p[0, s] = r["mv_p"].reshape(256, 4, 256)
        y_s[bs] = r["y_smp"].reshape(NSMP, 8, D)
        lat_s[0, bs] = r["lat_smp"].reshape(NSMP, 8, 256)
        kr_s[0, bs] = r["kr_smp"].reshape(NSMP, 8, 64)
        conv_sn[0, bs] = r["conv_smp"].reshape(NSMP, 3, D)
        h_sn[0, bs] = r["h_smp"]
    return (y_p, y_s, lat_p, kr_p, lat_s, kr_s, conv_p, conv_sn, h_p, h_sn, mk_p, mv_p)
```

```python
import numpy as np
from contextlib import ExitStack
import concourse.bass as bass
import concourse.mybir as mybir
from concourse.bass_utils import run_bass_kernel_spmd

F32 = mybir.dt.float32
BF16 = mybir.dt.bfloat16
I32 = mybir.dt.int32
AF = mybir.ActivationFunctionType
ALU = mybir.AluOpType

NCORES = 8
D = 1024
DC = 8
FF = 2816
FC = 22
TT = 256
NBLK = TT // 128
SEQ_HALF = 2048
NT = SEQ_HALF // TT
NSMP = 16
EPS = 1e-6
SCALE = 192.0 ** -0.5
NEG = -30000.0


class Tok:
    __slots__ = ("sem", "val", "eng")

    def __init__(self, sem, val, eng):
        self.sem = sem
        self.val = val
        self.eng = eng


class Buf:
    def __init__(self, ctx, t, name, space):
        self.ctx = ctx
        self.t = t
        self.name = name
        self.space = space
        self.last_w = None
        self.readers = {}
        self.dsem = None
        self.dcount = 0
        self.core = self

    def __getitem__(self, idx):
        return self.t[idx]

    def get_dsem(self):
        if self.dsem is None:
            self.dsem = self.ctx.new_sem("d_" + self.name)
        return self.dsem


class EngState:
    def __init__(self, name, obj, sem):
        self.name = name
        self.obj = obj
        self.sem = sem
        self.count = 0
        self.known = {}
        self.pending = None
        self.n_wait = 0
        self.n_inst = 0


class Ctx:
    def __init__(self, nc, es):
        self.nc = nc
        self.es = es
        self.nsem = 0
        self.engs = {}
        for name, obj in (("pe", nc.tensor), ("act", nc.scalar), ("dve", nc.vector),
                          ("pool", nc.gpsimd), ("sp", nc.sync)):
            self.engs[name] = EngState(name, obj, self.new_sem("e_" + name))
        self.out_toks = []
        self.all_bufs = []

    def view(self, ap, name, core=None):
        b = Buf(self, ap, name, "sb")
        if core is not None:
            b.core = core
        else:
            self.all_bufs.append(b)
        return b

    def new_sem(self, name):
        self.nsem += 1
        return self.es.enter_context(self.nc.semaphore(name))

    def sb(self, name, shape, dtype):
        t = self.es.enter_context(self.nc.sbuf_tensor(name, list(shape), dtype))
        b = Buf(self, t, name, "sb")
        self.all_bufs.append(b)
        return b

    def ps(self, name, shape, dtype=F32):
        t = self.es.enter_context(self.nc.psum_tensor(name, list(shape), dtype))
        return Buf(self, t, name, "ps")

    def _need(self, E, reads, writes, strict=False):
        reads = [b.core for b in reads]
        writes = [b.core for b in writes]
        toks = []
        for b in reads:
            if b.last_w is not None:
                toks.append(b.last_w)
        for b in writes:
            if b.last_w is not None and (strict or b.last_w.eng != E.name):
                toks.append(b.last_w)
            for tk in b.readers.values():
                if strict or tk.eng != E.name:
                    toks.append(tk)
        best = {}
        for tk in toks:
            if tk.eng == "pe" and E.name == "pe":
                continue
            assert tk.val is not None, "dependency on un-signalled PE op"
            k = id(tk.sem)
            if E.known.get(k, 0) >= tk.val:
                continue
            if k not in best or best[k].val < tk.val:
                best[k] = tk
        for k, tk in best.items():
            E.obj.wait_ge(tk.sem, tk.val)
            E.known[k] = tk.val
            E.n_wait += 1

    def _record(self, tok, reads, writes):
        reads = [b.core for b in reads]
        writes = [b.core for b in writes]
        k = id(tok.sem)
        for b in reads:
            b.readers[k] = tok
        for b in writes:
            b.last_w = tok
            b.readers = {}

    def op(self, eng, fn, reads=(), writes=(), inc=True):
        E = self.engs[eng]
        self._need(E, reads, writes)
        inst = fn(E.obj)
        E.n_inst += 1
        if eng == "pe":
            if E.pending is None:
                E.pending = Tok(E.sem, None, "pe")
            tok = E.pending
            if inc:
                E.count += 1
                inst.then_inc(E.sem, 1)
                tok.val = E.count
                E.pending = None
        else:
            E.count += 1
            inst.then_inc(E.sem, 1)
            tok = Tok(E.sem, E.count, eng)
        self._record(tok, reads, writes)
        return inst

    def dma(self, q, parts, reads=(), writes=(), sembuf=None, is_output=False, indirect=False):
        E = self.engs[q]
        self._need(E, reads, writes, strict=True)
        sb = sembuf
        if sb is None:
            cands = [b for b in list(writes) + list(reads) if b.space != "dram"]
            sb = cands[0]
        sb = sb.core
        sem = sb.get_dsem()
        for p in parts:
            if len(p) == 3:
                inst = E.obj.indirect_dma_start(out=p[0], out_offset=None, in_=p[1],
                                                in_offset=bass.IndirectOffsetOnAxis(ap=p[2], axis=0))
            else:
                inst = E.obj.dma_start(out=p[0], in_=p[1])
            sb.dcount += 16
            inst.then_inc(sem, 16)
            E.n_inst += 1
        tok = Tok(sem, sb.dcount, "dma")
        self._record(tok, reads, writes)
        if is_output:
            self.out_toks.append(tok)
        return tok

    def barrier(self):
        dsems = [(b.dsem, b.dcount) for b in self.all_bufs if b.dsem is not None and b.dcount > 0]
        for name, E in self.engs.items():
            for oname, X in self.engs.items():
                if oname != name and X.count > 0 and E.known.get(id(X.sem), 0) < X.count:
                    E.obj.wait_ge(X.sem, X.count)
                    E.known[id(X.sem)] = X.count
            for sem, cnt in dsems:
                if E.known.get(id(sem), 0) < cnt:
                    E.obj.wait_ge(sem, cnt)
                    E.known[id(sem)] = cnt

    def finish(self, eng="sp"):
        E = self.engs[eng]
        best = {}
        for tk in self.out_toks:
            k = id(tk.sem)
            if k not in best or best[k].val < tk.val:
                best[k] = tk
        for k, tk in best.items():
            E.obj.wait_ge(tk.sem, tk.val)
        for name, X in self.engs.items():
            if X.count > 0 and name != eng:
                E.obj.wait_ge(X.sem, X.count)


def build(n_pool):
    nc = bass.Bass("TRN2", target_bir_lowering=False)

    def din(name, shape, dt=F32):
        return nc.dram_tensor(name, list(shape), dt, kind="ExternalInput").ap()

    def dout(name, shape, dt=F32):
        return nc.dram_tensor(name, list(shape), dt, kind="ExternalOutput").ap()

    x_oth = din("x_oth", [SEQ_HALF, D])
    x_own = din("x_own", [SEQ_HALF, D])
    x_smp = din("x_smp", [128, D])
    mem_p = din("mem_p", [256, D])
    pool_lat = din("pool_lat", [n_pool * 128, 256])
    pool_kr = din("pool_kr", [n_pool * 128, 64])
    ptab = din("ptab", [1, NSMP * 64], I32)
    conv_s = din("conv_s", [NSMP * 3, D])
    h_s = din("h_s", [NSMP, D])
    mem_k_s = din("mem_k_s", [NSMP, 256, D])
    mem_v_s = din("mem_v_s", [NSMP, 256, D])
    cs_oth = din("cs_oth", [2, 64, SEQ_HALF])
    cs_own = din("cs_own", [2, 64, SEQ_HALF])
    cs_smp = din("cs_smp", [2, 64, 128])
    kmask = din("kmask", [1, 2 * SEQ_HALF])
    pmask = din("pmask", [1, 8 * TT])
    cmask_d = din("cmask", [128, 1])
    vec_d = din("vec", [19, D])
    w_gate = din("w_gate", [2, D, FF])
    w_up = din("w_up", [2, D, FF])
    w_down = din("w_down", [2, FF, D])
    w_in = din("w_in", [D, 4800])
    w_uq = din("w_uq", [384, 1536])
    w_uk = din("w_uk", [256, 1024])
    w_uv = din("w_uv", [256, 1024])
    w_o_mla = din("w_o_mla", [D, D])
    w_rg = din("w_rg", [8, 128, 128])
    w_ig = din("w_ig", [8, 128, 128])
    w_o_rnn = din("w_o_rnn", [D, D])
    w_out = din("w_out", [D, D])
    w_mem_q = din("w_mem_q", [D, D])
    w_mem_k = din("w_mem_k", [D, D])
    w_mem_v = din("w_mem_v", [D, D])
    w_mem_o = din("w_mem_o", [D, D])

    y_own = dout("y_own", [SEQ_HALF, D])
    y_smp = dout("y_smp", [128, D])
    lat_own = dout("lat_own", [SEQ_HALF, 256])
    kr_own = dout("kr_own", [SEQ_HALF, 64])
    lat_smp = dout("lat_smp", [128, 256])
    kr_smp = dout("kr_smp", [128, 64])
    conv_p = dout("conv_p", [3, D])
    conv_smp = dout("conv_smp", [NSMP * 3, D])
    h_p = dout("h_p", [8, 128])
    h_smp = dout("h_smp", [NSMP, D])
    mk_p = dout("mk_p", [256, D])
    mv_p = dout("mv_p", [256, D])

    es = ExitStack()
    with es:
        c = Ctx(nc, es)
        W = TT + 8

        PS = [c.ps(f"ps{i}", [128, 512], F32) for i in range(8)]
        st = {"ps": 0, "w": 0, "tf": 0, "tb": 0, "pt": 0}

        pinned = set()

        def ps_next(pin=False):
            while True:
                b = PS[st["ps"] % 8]
                st["ps"] += 1
                if id(b) not in pinned:
                    break
            if pin:
                pinned.add(id(b))
            return b

        def unpin(*bs):
            for b in bs:
                pinned.discard(id(b))

        NW = 4
        WS = [c.sb(f"wslot{i}", [128, 4096], BF16) for i in range(NW)]
        WORK_WORDS = 22672
        WORK = es.enter_context(nc.sbuf_tensor("WORK", [128, WORK_WORDS], F32))
        wk = {"o": 0}

        def wcarve(name, shape, dtype, parts=128):
            n = int(np.prod(shape[1:]))
            words = n if dtype == F32 else (n + 1) // 2
            ap = WORK[:, wk["o"]:wk["o"] + words]
            wk["o"] += words
            assert wk["o"] <= WORK_WORDS, (name, wk["o"])
            if dtype != F32:
                ap = ap.bitcast(dtype)[:, 0:n]
            if len(shape) == 3:
                ap = ap.rearrange("p (a b) -> p a b", a=shape[1])
            elif len(shape) == 4:
                ap = ap.rearrange("p (a b c) -> p a b c", a=shape[1], b=shape[2])
            if parts != 128:
                ap = ap[0:parts]
            return c.view(ap, name)
        TF = [wcarve(f"tf{i}", [128, W], F32) for i in range(8)]
        TB = [wcarve(f"tb{i}", [128, W], BF16) for i in range(3)]
        PTs = [c.sb(f"pt{i}", [128, 512], BF16) for i in range(3)]
        RS = [wcarve(f"rs{i}", [128, W], F32) for i in range(3)]

        def tmpf():
            b = TF[st["tf"] % len(TF)]
            st["tf"] += 1
            return b

        def tmpb():
            b = TB[st["tb"] % len(TB)]
            st["tb"] += 1
            return b

        def pt_next():
            b = PTs[st["pt"] % len(PTs)]
            st["pt"] += 1
            return b

        ident_f = c.sb("ident_f", [128, 128], F32)
        ident_b = c.sb("ident_b", [128, 128], BF16)
        ones_b = c.sb("ones_b", [128, 128], BF16)
        maskb = c.sb("maskb", [128, 4, 128], BF16)
        masks = c.sb("masks", [128, NSMP, 8, 8], BF16)
        VT = c.sb("VT", [128, DC, 19], F32)
        DER = c.sb("DER", [128, DC, 4], F32)
        HB = c.sb("HB", [128, DC, 2], F32)
        CARRY = c.sb("CARRY", [128, DC], F32)
        HALO = c.sb("HALO", [128, DC, 3], F32)
        CMASK = c.sb("CMASK", [128, 1], F32)
        ARENA = es.enter_context(nc.sbuf_tensor("ARENA", [128, 20480], BF16))
        ar = {"o": 0}

        def carve(name, shape):
            n = int(np.prod(shape[1:]))
            ap = ARENA[:, ar["o"]:ar["o"] + n]
            ar["o"] += n
            assert ar["o"] <= 20480, (name, ar["o"])
            if len(shape) == 3:
                ap = ap.rearrange("p (a b) -> p a b", a=shape[1])
            elif len(shape) == 4:
                ap = ap.rearrange("p (a b c) -> p a b c", a=shape[1], b=shape[2])
            return c.view(ap, name)
        KT = [carve(f"KT{i}", [128, 3, TT]) for i in range(2 * NT)]
        VV = [carve(f"VV{i}", [128, NBLK, 256]) for i in range(2 * NT)]
        ar["o"] = 0
        PGK = [carve(f"PGK{i}", [128, 8, 256]) for i in range(3)]
        PGR = [carve(f"PGR{i}", [128, 8, 64]) for i in range(3)]
        KTP = [carve(f"KTP{i}", [128, 3, 8, 128]) for i in range(2)]
        MKS = [carve(f"MKS{i}", [128, 2, D]) for i in range(1)] * 2
        MVS = [carve(f"MVS{i}", [128, 2, D]) for i in range(1)] * 2
        MKST = [carve(f"MKST{i}", [128, DC, 256]) for i in range(1)] * 2
        KTS = c.sb("KTS", [128, 3, 128], BF16)
        VVS = c.sb("VVS", [128, 1, 256], BF16)
        MKT = c.sb("MKT", [128, DC, 256], BF16)
        MV = c.sb("MV", [128, 2, D], BF16)
        WUKT = c.sb("WUKT", [128, 8, 256], BF16)
        WUV = c.sb("WUV", [128, 2, 1024], BF16)
        WRG = c.sb("WRG", [128, 8, 128], BF16)
        WIG = c.sb("WIG", [128, 8, 128], BF16)
        XT = wcarve("XT", [128, DC, TT], F32)
        YB = wcarve("YB", [128, DC * (TT + 1)], F32)
        RXB = wcarve("RXB", [128, DC * (TT + 3)], F32)
        OLAT = wcarve("OLAT", [128, 2, 8, TT], BF16)
        H = wcarve("H", [128, DC, TT], BF16)
        off_sq = wk["o"]
        SQ = wcarve("SQ", [128, DC, TT], BF16)
        MIX = wcarve("MIX", [128, DC, TT], BF16)
        AT = wcarve("AT", [128, 24 * TT], BF16)
        BIGT = AT.t
        AT.t = BIGT[:, 0:FC * TT].rearrange("p (f n) -> p f n", f=FC)
        QL = c.view(BIGT[:, 0:16 * TT].rearrange("p (j h n) -> p j h n", j=2, h=8), "QL", core=AT)
        QN = c.view(BIGT[:, 16 * TT:24 * TT].rearrange("p (h n) -> p h n", h=8), "QN", core=AT)
        QRT = c.sb("QRT", [128, 8, TT], BF16)
        QRS = QRT
        YR = wcarve("YR", [128, DC, TT], BF16)
        MIXR = wcarve("MIXR", [128, DC, TT], BF16)
        off_rend = wk["o"]
        RNW = DC * (TT + 1)
        assert off_sq + 3 * RNW <= off_rend, (off_sq, off_rend)
        RA = c.view(WORK[:, off_sq:off_sq + RNW], "RA")
        RM = c.view(WORK[:, off_sq + RNW:off_sq + 2 * RNW], "RM")
        RW = c.view(WORK[:, off_sq + 2 * RNW:off_sq + 3 * RNW], "RW")
        SCR = wcarve("SCR", [128, 4, TT], F32)
        CQ = wcarve("CQ", [128, 3, TT], F32)
        CQN = wcarve("CQN", [128, 3, TT], BF16)
        KRT = wcarve("KRT", [128, TT], F32, parts=64)
        CS = wcarve("CS", [128, 2, TT], F32, parts=64)
        main_words = wk["o"]
        wk["o"] = 0
        TF_ = 512
        XT5 = wcarve("XT5", [128, DC, TF_], F32)
        YB5 = wcarve("YB5", [128, DC * TF_], F32)
        H5 = wcarve("H5", [128, DC, TF_], BF16)
        SQ5 = wcarve("SQ5", [128, DC, TF_], BF16)
        AT5 = wcarve("AT5", [128, FC, TF_], BF16)
        TF5 = [wcarve(f"tf5_{i}", [128, TF_ + 8], F32) for i in range(4)]
        RS5 = [wcarve(f"rs5_{i}", [128, TF_ + 8], F32) for i in range(3)]
        print("WORK words: main", main_words, "ffn", wk["o"])
        XT5b = c.view(ARENA[:, 0:8192].bitcast(F32).rearrange("p (c n) -> p c n", c=DC), "XT5b")
        H5b = c.view(ARENA[:, 8192:12288].rearrange("p (c n) -> p c n", c=DC), "H5b")
        STG5 = c.view(ARENA[:, 12288:20480].bitcast(F32), "STG5")
        OSTG = [c.sb(f"OSTG{i}", [128, D], F32) for i in range(2)]
        RD = c.sb("RD", [128, 8], F32)
        ONF = c.sb("ONF", [128, 256], F32)
        IDXG = c.sb("IDXG", [128, NSMP * 8], I32)
        IOTA = c.sb("IOTA", [128, 1], F32)
        STG = OSTG[1]
        VEC = STG
        H0S = c.sb("H0S", [128, DC, NSMP], F32)
        QM = QN
        OM = MIX

        def ACT(out, in_, func, reads, writes, **kw):
            return c.op("act", lambda e: e.activation(out, in_, func, **kw), reads=reads, writes=writes)

        def DVE(fn, reads, writes):
            return c.op("dve", fn, reads=reads, writes=writes)

        def mm(out_ap, pairs, reads, writes, start=True, stop=True):
            n = len(pairs)
            for i, (l, r) in enumerate(pairs):
                c.op("pe", lambda e, l=l, r=r, i=i: e.matmul(out_ap, l, r, start=(start and i == 0),
                                                              stop=(stop and i == n - 1)),
                     reads=reads, writes=writes, inc=(i == n - 1))

        def transpose(out_ap, in_ap, ident_ap, reads, writes):
            c.op("pe", lambda e: e.transpose(out_ap, in_ap, ident_ap), reads=reads, writes=writes, inc=True)

        class WRef:
            def __init__(self, ap, buf):
                self.ap = ap
                self.buf = buf
                self.colbufs = None

            def __getitem__(self, idx):
                buf = self.buf
                if self.colbufs and isinstance(idx, tuple) and len(idx) == 2 and isinstance(idx[1], slice):
                    c0 = idx[1].start or 0
                    for lo, hi, b in self.colbufs:
                        if lo <= c0 < hi:
                            buf = b
                return WRef(self.ap[idx], buf)

            def rearrange(self, pat, **kw):
                return WRef(self.ap.rearrange(pat, **kw), self.buf)

        def prep(name, src_ap, shape, split):
            t = nc.dram_tensor("wb_" + name, list(shape), BF16, kind="Internal")
            b = Buf(c, t, "wb_" + name, "dscr")
            c.all_bufs.append(b)
            dst = t.ap()
            lead = len(shape) - 2
            srcs = [src_ap[i] for i in range(shape[0])] if lead else [src_ap]
            dsts = [dst[i] for i in range(shape[0])] if lead else [dst]
            parts = []
            for sa, da in zip(srcs, dsts):
                if split > 1:
                    sa = sa.rearrange("k (a b) -> (k a) b", a=split)
                    da = da.rearrange("k (a b) -> (k a) b", a=split)
                parts.append((da, sa))
            c.dma("pool", parts, writes=[b], sembuf=b)
            return WRef(dst, b)

        def wload(parts):
            slot = WS[st["w"] % NW]
            st["w"] += 1
            c.dma("sp", [(fn(slot), src.ap) for fn, src in parts], reads=[src.buf for fn, src in parts], writes=[slot],
                  sembuf=slot)
            return slot

        def wview(slot, k, n):
            return slot[:, 0:k * n].rearrange("p (k n) -> p k n", k=k)

        def wsrc(w_ap, r0, r1, c0, c1):
            return w_ap[r0:r1, c0:c1].rearrange("(k p) n -> p k n", p=128)

        w_gate_f, w_up_f, w_down_f = w_gate, w_up, w_down
        def mk(name, rows, cols):
            t = nc.dram_tensor("wb_" + name, [rows, cols], BF16, kind="Internal")
            r = WRef(t.ap(), None)
            r.colbufs = []
            r.t = t
            return r

        def cast_cols(r, src2d, lo, hi):
            b = Buf(c, r.t, f"wbc_{id(r) % 100000}_{lo}", "dscr")
            c.all_bufs.append(b)
            c.dma("pool", [(r.ap[:, lo:hi], src2d[:, lo:hi])], writes=[b], sembuf=b)
            r.colbufs.append((lo, hi, b))
            r.buf = b
        g0 = mk("gate0", D, FF)
        u0 = mk("up0", D, FF)
        cast_cols(g0, w_gate_f[0], 0, 1280)
        cast_cols(u0, w_up_f[0], 0, 1280)
        cast_cols(g0, w_gate_f[0], 1280, FF)
        cast_cols(u0, w_up_f[0], 1280, FF)
        d0 = prep("down0", w_down_f[0], [FF, D], 1)
        w_in = prep("in", w_in, [D, 4800], 3)
        w_uq = prep("uq", w_uq, [384, 1536], 1)
        w_o_rnn = prep("o_rnn", w_o_rnn, [D, D], 1)
        w_o_mla = prep("o_mla", w_o_mla, [D, D], 1)
        w_out = prep("out", w_out, [D, D], 1)
        w_mem_q = prep("mem_q", w_mem_q, [D, D], 1)
        w_mem_k = prep("mem_k", w_mem_k, [D, D], 1)
        w_mem_v = prep("mem_v", w_mem_v, [D, D], 1)
        w_mem_o = prep("mem_o", w_mem_o, [D, D], 1)
        w_gate = [g0, prep("gate1", w_gate_f[1], [D, FF], 2)]
        w_up = [u0, prep("up1", w_up_f[1], [D, FF], 2)]
        w_down = [d0, prep("down1", w_down_f[1], [FF, D], 1)]

        c.op("pool", lambda e: e.memset(ident_f[:], 0.0), writes=[ident_f])
        c.op("pool", lambda e: e.affine_select(ident_f[:], ident_f[:], [[-1, 128]], ALU.not_equal, 1.0,
                                               base=0, channel_multiplier=1), reads=[ident_f], writes=[ident_f])
        DVE(lambda e: e.tensor_copy(ident_b[:], ident_f[:]), [ident_f], [ident_b])
        c.op("pool", lambda e: e.memset(ones_b[:], 1.0), writes=[ones_b])
        c.op("pool", lambda e: e.memset(maskb[:], 0.0), writes=[maskb])
        c.op("pool", lambda e: e.affine_select(maskb[:], maskb[:], [[0, 4], [1, 128]], ALU.is_ge, NEG,
                                               base=0, channel_multiplier=-1), reads=[maskb], writes=[maskb])
        c.op("pool", lambda e: e.memset(masks[:], 0.0), writes=[masks])
        c.op("pool", lambda e: e.affine_select(masks[:], masks[:], [[8, NSMP], [0, 8], [1, 8]], ALU.is_ge, NEG,
                                               base=0, channel_multiplier=-1), reads=[masks], writes=[masks])
        c.op("pool", lambda e: e.affine_select(masks[:], masks[:], [[-8, NSMP], [0, 8], [0, 8]], ALU.is_ge, NEG,
                                               base=0, channel_multiplier=1), reads=[masks], writes=[masks])
        c.op("pool", lambda e: e.memset(HALO[:], 0.0), writes=[HALO])
        c.op("pool", lambda e: e.memset(CARRY[:], 0.0), writes=[CARRY])
        c.dma("pool", [(CMASK[:], cmask_d)], writes=[CMASK])
        c.dma("pool", [(VEC[0:19, :], vec_d)], writes=[VEC])
        pv = ps_next()
        for ch in range(DC):
            transpose(pv[:, ch * 19:(ch + 1) * 19], VEC[0:19, ch * 128:(ch + 1) * 128], ident_f[0:19, 0:19],
                      [VEC, ident_f], [pv])
        DVE(lambda e: e.tensor_copy(VT[:], pv[:, 0:DC * 19].rearrange("p (c r) -> p c r", c=DC)), [pv], [VT])
        def vrow(r, ch):
            return VT[:, ch, r:r + 1]
        t0 = tmpf()
        ACT(t0[:, 0:DC], VT[:, :, 16], AF.Exp, [VT], [t0], scale=-1.0)
        ACT(t0[:, 0:DC], t0[:, 0:DC], AF.Ln, [t0], [t0], bias=1.0)
        DVE(lambda e: e.tensor_scalar(DER[:, :, 0], t0[:, 0:DC], -8.0, None, op0=ALU.mult), [t0], [DER])
        DVE(lambda e: e.tensor_scalar(DER[:, :, 1], VT[:, :, 1], 0.5, None, op0=ALU.mult), [VT], [DER])
        DVE(lambda e: e.tensor_scalar(DER[:, :, 2], VT[:, :, 7], 0.5, None, op0=ALU.mult), [VT], [DER])
        DVE(lambda e: e.tensor_scalar(DER[:, :, 3], t0[:, 0:DC], -4.0, None, op0=ALU.mult), [t0], [DER])
        DVE(lambda e: e.tensor_scalar(HB[:, :, 0], VT[:, :, 14], 0.5, None, op0=ALU.mult), [VT], [HB])
        DVE(lambda e: e.tensor_scalar(HB[:, :, 1], VT[:, :, 15], 0.5, None, op0=ALU.mult), [VT], [HB])

        c.dma("pool", [(WUV[:], w_uv.rearrange("(j p) n -> p j n", p=128))], writes=[WUV])
        c.dma("pool", [(WRG[:], w_rg.rearrange("g i j -> i g j"))], writes=[WRG])
        c.dma("pool", [(WIG[:], w_ig.rearrange("g i j -> i g j"))], writes=[WIG])
        c.dma("pool", [(YB[:, 0:2048].rearrange("p (j n) -> p j n", j=2), w_uk.rearrange("(j p) n -> p j n", p=128))],
              writes=[YB])
        for hh in range(8):
            pw = ps_next()
            for j in range(2):
                transpose(pw[:, j * 128:(j + 1) * 128], YB[:, j * 1024 + hh * 128: j * 1024 + (hh + 1) * 128],
                          ident_f[:], [YB, ident_f], [pw])
            DVE(lambda e, pw=pw, hh=hh: e.tensor_copy(WUKT[:, hh, :], pw[:, 0:256]), [pw], [WUKT])
        c.op("pool", lambda e: e.memset(QRT[64:128, :, :], 0.0), writes=[QRT])
        c.dma("pool", [(QRT[64:65, :, :].rearrange("p h n -> p (h n)"), pmask)], writes=[QRT])

        def rms_stats(xap, nch, N, Dn, xbuf):
            for ch in range(nch):
                ACT(SQ[:, ch, 0:N], xap(ch), AF.Square, [xbuf], [SQ])
            ps = ps_next()
            mm(ps[:, 0:N], [(ones_b[:], SQ[:, ch, 0:N]) for ch in range(nch)], [ones_b, SQ], [ps])
            t = RS[2]
            ACT(t[:, 0:N], ps[:, 0:N], AF.Sqrt, [ps], [t], bias=EPS, scale=1.0 / Dn)
            r = RS[st.setdefault("rs", 0) % 2]
            st["rs"] += 1
            DVE(lambda e: e.reciprocal(r[:, 0:N], t[:, 0:N]), [t], [r])
            return r

        def rmsnorm_to(xap, xbuf, nch, N, Dn, gain, outap, outbuf):
            r = rms_stats(xap, nch, N, Dn, xbuf)
            for ch in range(nch):
                DVE(lambda e, ch=ch: e.scalar_tensor_tensor(outap(ch), xap(ch), gain(ch), r[:, 0:N],
                                                            op0=ALU.mult, op1=ALU.mult), [xbuf, r, VT, DER], [outbuf])

        def Yv(N):
            return YB[:, 0:DC * N].rearrange("p (c n) -> p c n", c=DC)

        def residual_add(N, gain, xt=None):
            XT_ = XT if xt is None else xt
            Y = Yv(N)
            r = rms_stats(lambda ch: Y[:, ch, :], DC, N, D, YB)
            for ch in range(DC):
                t = tmpf()
                DVE(lambda e, ch=ch, t=t: e.scalar_tensor_tensor(t[:, 0:N], Y[:, ch, :], gain(ch), r[:, 0:N],
                                                                  op0=ALU.mult, op1=ALU.mult), [YB, r, VT, DER], [t])
                DVE(lambda e, ch=ch, t=t: e.tensor_tensor(XT_[:, ch, 0:N], XT_[:, ch, 0:N], t[:, 0:N], op=ALU.add),
                    [XT_, t], [XT_])

        def proj(w_ap, col0, ncols, N, consumer, src=None, kch=DC, chunk=128):
            src = H if src is None else src
            done = 0
            while done < ncols:
                gw = min(512, ncols - done)
                slot = wload([(lambda s, gw=gw: wview(s, kch, gw), wsrc(w_ap, 0, kch * 128, col0 + done, col0 + done + gw))])
                wv = wview(slot, kch, gw)
                for cc in range(0, gw, chunk):
                    m = min(chunk, gw - cc)
                    ps = ps_next()
                    mm(ps[0:m, 0:N], [(wv[:, k, cc:cc + m], src[:, k, 0:N]) for k in range(kch)], [slot, src], [ps])
                    consumer((done + cc) // chunk, ps, m)
                done += gw

        def ffn(l, N, pre_row, post_gain):
            rmsnorm_to(lambda ch: XT[:, ch, 0:N], XT, DC, N, D, lambda ch: vrow(pre_row, ch),
                       lambda ch: H[:, ch, 0:N], H)
            for g in range(11):
                sl = wload([(lambda s_: wview(s_, DC, 512)[:, :, 0:256], wsrc(w_gate[l], 0, D, g * 256, (g + 1) * 256)),
                            (lambda s_: wview(s_, DC, 512)[:, :, 256:512], wsrc(w_up[l], 0, D, g * 256, (g + 1) * 256))])
                wv = wview(sl, DC, 512)
                for f in range(2):
                    pg = ps_next()
                    mm(pg[:, 0:N], [(wv[:, k, f * 128:(f + 1) * 128], H[:, k, 0:N]) for k in range(DC)], [sl, H], [pg])
                    pu = ps_next()
                    mm(pu[:, 0:N], [(wv[:, k, 256 + f * 128:256 + (f + 1) * 128], H[:, k, 0:N]) for k in range(DC)], [sl, H], [pu])
                    t = tmpf()
                    ACT(t[:, 0:N], pg[:, 0:N], AF.Silu, [pg], [t])
                    DVE(lambda e, t=t, pu=pu, g=g, f=f: e.tensor_tensor(AT[:, g * 2 + f, 0:N], t[:, 0:N], pu[:, 0:N],
                                                                        op=ALU.mult), [t, pu], [AT])
            Y = Yv(N)
            for dh in range(2):
                banks = [ps_next(pin=True) for _ in range(4)]
                for rg in range(3):
                    nfr = 8 if rg < 2 else 6
                    wd = wload([(lambda s, nfr=nfr: wview(s, nfr, 512),
                                 w_down[l][rg * 1024: rg * 1024 + nfr * 128, dh * 512:(dh + 1) * 512].rearrange(
                                     "(f p) n -> p f n", p=128))])
                    wdv = wview(wd, nfr, 512)
                    for dd in range(4):
                        mm(banks[dd][:, 0:N], [(wdv[:, f, dd * 128:(dd + 1) * 128], AT[:, rg * 8 + f, 0:N])
                                               for f in range(nfr)], [wd, AT], [banks[dd]], start=(rg == 0), stop=(rg == 2))
                for dd in range(4):
                    ACT(Y[:, dh * 4 + dd, :], banks[dd][:, 0:N], AF.Copy, [banks[dd]], [YB])
                unpin(*banks)
            residual_add(N, post_gain)

        def ffn_pipeline(l, pre_row, post_gain, tiles, loader, finisher, defer=True):
            XTs = [XT5, XT5b]
            Hs = [H5, H5b]
            nt = len(tiles)

            def L2(t):
                n = tiles[t]
                xt = XTs[t % 2]
                h = Hs[t % 2]
                rmsnorm_to(lambda ch: xt[:, ch, 0:n], xt, DC, n, D, lambda ch: vrow(pre_row, ch),
                           lambda ch: h[:, ch, 0:n], h)
            loader(0, XTs[0])
            L2(0)
            for t in range(nt):
                N = tiles[t]
                xt = XTs[t % 2]
                h = Hs[t % 2]
                for g in range(11):
                    sl = wload([(lambda s_: wview(s_, DC, 512)[:, :, 0:256], wsrc(w_gate[l], 0, D, g * 256, (g + 1) * 256)),
                                (lambda s_: wview(s_, DC, 512)[:, :, 256:512], wsrc(w_up[l], 0, D, g * 256, (g + 1) * 256))])
                    wv = wview(sl, DC, 512)
                    for f in range(2):
                        pg = ps_next()
                        mm(pg[:, 0:N], [(wv[:, k, f * 128:(f + 1) * 128], h[:, k, 0:N]) for k in range(DC)], [sl, h], [pg])
                        pu = ps_next()
                        mm(pu[:, 0:N], [(wv[:, k, 256 + f * 128:256 + (f + 1) * 128], h[:, k, 0:N]) for k in range(DC)], [sl, h], [pu])
                        tt_ = tmpf()
                        ACT(tt_[:, 0:N], pg[:, 0:N], AF.Silu, [pg], [tt_])
                        DVE(lambda e, tt_=tt_, pu=pu, g=g, f=f: e.tensor_tensor(AT[:, g * 2 + f, 0:N], tt_[:, 0:N], pu[:, 0:N],
                                                                                op=ALU.mult), [tt_, pu], [AT])
                    if defer and g == 2 and t > 0:
                        finisher(t - 1, XTs[(t - 1) % 2])
                    if g == 5 and t + 1 < nt:
                        loader(t + 1, XTs[(t + 1) % 2])
                if t + 1 < nt:
                    L2(t + 1)
                Y = Yv(N)
                for dh in range(2):
                    banks = [ps_next(pin=True) for _ in range(4)]
                    for rg in range(3):
                        nfr = 8 if rg < 2 else 6
                        wd = wload([(lambda s_, nfr=nfr: wview(s_, nfr, 512),
                                     w_down[l][rg * 1024: rg * 1024 + nfr * 128, dh * 512:(dh + 1) * 512].rearrange(
                                         "(f p) n -> p f n", p=128))])
                        wdv = wview(wd, nfr, 512)
                        for dd in range(4):
                            mm(banks[dd][:, 0:N], [(wdv[:, f, dd * 128:(dd + 1) * 128], AT[:, rg * 8 + f, 0:N])
                                                   for f in range(nfr)], [wd, AT], [banks[dd]], start=(rg == 0), stop=(rg == 2))
                    for dd in range(4):
                        ACT(Y[:, dh * 4 + dd, :], banks[dd][:, 0:N], AF.Copy, [banks[dd]], [YB])
                    unpin(*banks)
                residual_add(N, post_gain, xt=xt)
                if not defer:
                    finisher(t, xt)
            if defer:
                finisher(nt - 1, XTs[(nt - 1) % 2])

        def load_xT(x_ap, row0, N):
            nb = N // 128
            xin = YB[:, 0:nb * D].rearrange("p (b f) -> p b f", b=nb)
            c.dma("pool", [(xin, x_ap[row0:row0 + N, :].rearrange("(b p) f -> p b f", p=128))], writes=[YB])
            for ch in range(DC):
                ps = ps_next()
                for b in range(nb):
                    transpose(ps[:, b * 128:(b + 1) * 128], xin[:, b, ch * 128:(ch + 1) * 128], ident_f[:],
                              [YB, ident_f], [ps])
                ACT(XT[:, ch, 0:N], ps[:, 0:N], AF.Copy, [ps], [XT])

        def store_rows(out_ap, row0, N, src_ap_fn, srcbuf, nfeat):
            nch = (nfeat + 127) // 128
            for b in range(N // 128):
                og = OSTG[st.setdefault("os", 0) % 2]
                st["os"] += 1
                for c0 in range(0, nch, 4):
                    ps = ps_next()
                    ncc = min(4, nch - c0)
                    for ch in range(c0, c0 + ncc):
                        m = min(128, nfeat - ch * 128)
                        transpose(ps[:, (ch - c0) * 128:(ch - c0) * 128 + m], src_ap_fn(ch)[0:m, b * 128:(b + 1) * 128],
                                  ident_f[0:m, 0:m], [srcbuf, ident_f], [ps])
                    wid = min(512, nfeat - c0 * 128)
                    ACT(og[:, c0 * 128:c0 * 128 + wid], ps[:, 0:wid], AF.Copy, [ps], [og])
                c.dma("act", [(out_ap[row0 + b * 128: row0 + (b + 1) * 128, :], og[:, 0:nfeat])], reads=[og], is_output=True)
                yield b, og

        def rope(psA, psB, N, out_ap, outbuf, m=64):
            t1 = tmpf()
            t2 = tmpf()
            DVE(lambda e: e.tensor_tensor(t1[0:m, 0:N], psA[0:m, 0:N], CS[:, 0, 0:N], op=ALU.mult), [psA, CS], [t1])
            DVE(lambda e: e.tensor_tensor(t2[0:m, 0:N], psB[0:m, 0:N], CS[:, 1, 0:N], op=ALU.mult), [psB, CS], [t2])
            DVE(lambda e: e.tensor_tensor(out_ap, t1[0:m, 0:N], t2[0:m, 0:N], op=ALU.add), [t1, t2], [outbuf])

        def rnn_tile(N, B, L, h0_fn, smp):
            RX = RXB[:, 0:DC * B * (L + 3)].rearrange("p (c b l) -> p c b l", c=DC, b=B)
            HS = YB[:, 0:DC * B * (L + 1)].rearrange("p (c b l) -> p c b l", c=DC, b=B)
            W1 = L + 1 if smp else L
            off = 1 if smp else 0
            n1 = B * W1
            A4 = RA[:, 0:DC * n1].rearrange("p (c b l) -> p c b l", c=DC, b=B)
            M4 = RM[:, 0:DC * n1].rearrange("p (c b l) -> p c b l", c=DC, b=B)
            W4 = RW[:, 0:DC * n1].rearrange("p (c b l) -> p c b l", c=DC, b=B)
            alias = [SQ, MIX, AT, YR, MIXR]
            DVE(lambda e: e.memset(RA[:, 0:1], 0.0), [], [RA, RM, RW] + alias)
            for ch in range(DC):
                xc = tmpf()
                xcv = xc[:, 0:N].rearrange("p (b l) -> p b l", b=B)
                DVE(lambda e: e.tensor_scalar(xcv, RX[:, ch, :, 0:L], vrow(9, ch), vrow(13, ch), op0=ALU.mult, op1=ALU.add),
                    [RXB, VT], [xc])
                for k in range(1, 4):
                    DVE(lambda e, k=k: e.scalar_tensor_tensor(xcv, RX[:, ch, :, k:k + L], vrow(9 + k, ch), xcv,
                                                              op0=ALU.mult, op1=ALU.add), [RXB, VT, xc], [xc])
                xb = tmpb()
                DVE(lambda e: e.tensor_copy(xb[:, 0:N], xc[:, 0:N]), [xc], [xb])
                pr = ps_next()
                mm(pr[:, 0:N], [(WRG[:, ch, :], xb[:, 0:N])], [WRG, xb], [pr])
                pi = ps_next()
                mm(pi[:, 0:N], [(WIG[:, ch, :], xb[:, 0:N])], [WIG, xb], [pi])
                rt = tmpf()
                it = tmpf()
                ACT(rt[:, 0:N], pr[:, 0:N], AF.Tanh, [pr, HB], [rt], bias=HB[:, ch, 0:1], scale=0.5)
                ACT(it[:, 0:N], pi[:, 0:N], AF.Tanh, [pi, HB], [it], bias=HB[:, ch, 1:2], scale=0.5)
                rtv = rt[:, 0:N].rearrange("p (b l) -> p b l", b=B)
                itv = it[:, 0:N].rearrange("p (b l) -> p b l", b=B)
                ACT(A4[:, ch, :, off:off + L], rtv, AF.Exp, [rt, DER], [RA], scale=DER[:, ch, 3:4], bias=DER[:, ch, 3:4])
                ACT(M4[:, ch, :, off:off + L], A4[:, ch, :, off:off + L], AF.Square, [RA], [RM])
                DVE(lambda e: e.scalar_tensor_tensor(W4[:, ch, :, off:off + L], itv, 1.0, xcv, op0=ALU.add, op1=ALU.mult),
                    [it, xc], [RW])
                if smp:
                    DVE(lambda e: e.memset(A4[:, ch, :, 0:1], 0.0), [], [RA])
                    DVE(lambda e: e.memset(M4[:, ch, :, 0:1], -3.0), [], [RM])
                    DVE(lambda e: e.tensor_copy(W4[:, ch, :, 0], h0_fn(ch)), [H0S], [RW])
            ACT(RM[:, 0:DC * n1], RM[:, 0:DC * n1], AF.Sqrt, [RM], [RM], bias=0.25, scale=-0.25)
            DVE(lambda e: e.tensor_tensor(RW[:, 0:DC * n1], RW[:, 0:DC * n1], RM[:, 0:DC * n1], op=ALU.mult), [RW, RM], [RW])
            for ch in range(DC):
                if smp:
                    DVE(lambda e: e.tensor_tensor_scan(HS[:, ch].rearrange("p b l -> p (b l)"), A4[:, ch].rearrange("p b l -> p (b l)"),
                                                       W4[:, ch].rearrange("p b l -> p (b l)"), 0.0, op0=ALU.mult, op1=ALU.add),
                        [RA, RW], [YB])
                else:
                    DVE(lambda e: e.tensor_tensor_scan(HS[:, ch, 0, 1:L + 1], A4[:, ch, 0, :], W4[:, ch, 0, :], CARRY[:, ch:ch + 1],
                                                       op0=ALU.mult, op1=ALU.add), [RA, RW, CARRY], [YB])
            DVE(lambda e: e.memset(RA[:, 0:1], 0.0), [], [RA, RM, RW] + alias)
            return RX, HS

        def kv_path(N, kt_buf, vv_buf, lat_out, kr_out, row0):
            slot = wload([(lambda s: wview(s, DC, 384)[:, :, 0:320], wsrc(w_in, 0, D, 384, 704)),
                          (lambda s: wview(s, DC, 384)[:, :, 320:352], wsrc(w_in, 0, D, 672, 704)),
                          (lambda s: wview(s, DC, 384)[:, :, 352:384], wsrc(w_in, 0, D, 640, 672))])
            wv = wview(slot, DC, 384)
            for j in range(2):
                ps = ps_next()
                mm(ps[:, 0:N], [(wv[:, k, j * 128:(j + 1) * 128], H[:, k, 0:N]) for k in range(DC)], [slot, H], [ps])
                ACT(SCR[:, j, 0:N], ps[:, 0:N], AF.Copy, [ps], [SCR])
            pA = ps_next()
            mm(pA[0:64, 0:N], [(wv[:, k, 256:320], H[:, k, 0:N]) for k in range(DC)], [slot, H], [pA])
            pB = ps_next()
            mm(pB[0:64, 0:N], [(wv[:, k, 320:384], H[:, k, 0:N]) for k in range(DC)], [slot, H], [pB])
            rope(pA, pB, N, KRT[:, 0:N], KRT)
            DVE(lambda e: e.tensor_copy(kt_buf[0:64, 2, 0:N], KRT[:, 0:N]), [KRT], [kt_buf])
            rmsnorm_to(lambda ch: SCR[:, ch, 0:N], SCR, 2, N, 256.0, lambda ch: vrow(18, ch),
                       lambda ch: SCR[:, 2 + ch, 0:N], SCR)
            for j in range(2):
                DVE(lambda e, j=j: e.tensor_copy(kt_buf[:, j, 0:N], SCR[:, 2 + j, 0:N]), [SCR], [kt_buf])
            for b, og in store_rows(lat_out, row0, N, lambda ch: SCR[:, 2 + ch, 0:N], SCR, 256) if lat_out is not None else \
                    transposed_blocks(N, lambda ch: SCR[:, 2 + ch, 0:N], SCR, 256):
                DVE(lambda e, b=b, og=og: e.tensor_copy(vv_buf[:, b, :], og[:, 0:256]), [og], [vv_buf])
            if kr_out is not None:
                for _ in store_rows(kr_out, row0, N, lambda ch: KRT[:, 0:N], KRT, 64):
                    pass

        def transposed_blocks(N, src_ap_fn, srcbuf, nfeat):
            nch = (nfeat + 127) // 128
            for b in range(N // 128):
                og = OSTG[st.setdefault("os", 0) % 2]
                st["os"] += 1
                ps = ps_next()
                for ch in range(nch):
                    transpose(ps[:, ch * 128:(ch + 1) * 128], src_ap_fn(ch)[:, b * 128:(b + 1) * 128], ident_f[:],
                              [srcbuf, ident_f], [ps])
                ACT(og[:, 0:nfeat], ps[:, 0:nfeat], AF.Copy, [ps], [og])
                yield b, og

        def q_path(N, qr_buf):
            def cons(ci, ps, m):
                ACT(CQ[:, ci, 0:N], ps[:, 0:N], AF.Copy, [ps], [CQ])
            proj(w_in, 0, 384, N, cons)
            rmsnorm_to(lambda ch: CQ[:, ch, 0:N], CQ, 3, N, 384.0, lambda ch: vrow(17, ch),
                       lambda ch: CQN[:, ch, 0:N], CQN)
            uq4 = w_uq.rearrange("(k p) (h f) -> p k h f", p=128, f=192)
            s2parts = []
            for k in range(3):
                s2parts.append((lambda s, k=k: wview(s, 3, 512).rearrange("p k (h f) -> p k h f", f=64)[:, k, :, 0:32], uq4[:, k, :, 160:192]))
                s2parts.append((lambda s, k=k: wview(s, 3, 512).rearrange("p k (h f) -> p k h f", f=64)[:, k, :, 32:64], uq4[:, k, :, 128:160]))
            s2 = wload(s2parts)
            uqs = wview(s2, 3, 512)
            s1h = [wload([(lambda s: wview(s, 3, 768), w_uq[:, hf * 768:(hf + 1) * 768].rearrange("(k p) n -> p k n", p=128))])
                   for hf in range(2)]
            for hh in range(8):
                s1 = s1h[hh // 4]
                uq = wview(s1, 3, 768)
                hl = hh % 4
                ps = ps_next()
                mm(ps[:, 0:N], [(uq[:, k, hl * 192: hl * 192 + 128], CQN[:, k, 0:N]) for k in range(3)], [s1, CQN], [ps])
                ACT(QN[:, hh, 0:N], ps[:, 0:N], AF.Copy, [ps], [QN])
                pA = ps_next()
                mm(pA[0:64, 0:N], [(uq[:, k, hl * 192 + 128: hl * 192 + 192], CQN[:, k, 0:N]) for k in range(3)], [s1, CQN], [pA])
                pB = ps_next()
                mm(pB[0:64, 0:N], [(uqs[:, k, hh * 64:(hh + 1) * 64], CQN[:, k, 0:N]) for k in range(3)], [s2, CQN], [pB])
                rope(pA, pB, N, qr_buf[0:64, hh, 0:N], qr_buf)
                for j in range(2):
                    ps2 = ps_next()
                    mm(ps2[:, 0:N], [(WUKT[:, hh, j * 128:(j + 1) * 128], QN[:, hh, 0:N])], [WUKT, QN], [ps2])
                    ACT(QL[:, j, hh, 0:N], ps2[:, 0:N], AF.Copy, [ps2], [QL])

        def finish_heads(accs, den, n_rows, dst_fn, src_view=lambda ap: ap):
            DVE(lambda e: e.reciprocal(RD[0:n_rows, 0:len(accs)], den[0:n_rows, 0:len(accs)]), [den], [RD])
            for hh, (abuf, aap) in enumerate(accs):
                ACT(ONF[0:n_rows, :], aap, AF.Copy, [abuf, RD], [ONF], scale=RD[0:n_rows, hh:hh + 1])
                for j in range(2):
                    ps = ps_next()
                    transpose(ps[:, 0:n_rows], ONF[0:n_rows, j * 128:(j + 1) * 128], ident_f[0:n_rows, 0:n_rows],
                              [ONF, ident_f], [ps])
                    ACT(dst_fn(j, hh), src_view(ps[:, 0:n_rows]), AF.Copy, [ps], [OLAT])

        def prompt_attention(ot):
            for qb in range(NBLK):
                keyblocks = [(kt, kb) for kt in range(NT) for kb in range(NBLK)]
                keyblocks += [(NT + kt, kb) for kt in range(ot) for kb in range(NBLK)]
                keyblocks += [(NT + ot, kb) for kb in range(qb + 1)]
                for g in range(2):
                    acc = [ps_next(pin=True), ps_next(pin=True)]
                    den = ps_next(pin=True)
                    nk = len(keyblocks)

                    def stage_s(ki):
                        kt, kb = keyblocks[ki]
                        diag = (kt == NT + ot and kb == qb)
                        S = ps_next()
                        pairs = [(KT[kt][:, j, kb * 128:(kb + 1) * 128], QL[:, j, 4 * g:4 * g + 4, qb * 128:(qb + 1) * 128])
                                 for j in range(2)]
                        pairs.append((KT[kt][:, 2, kb * 128:(kb + 1) * 128], QRT[:, 4 * g:4 * g + 4, qb * 128:(qb + 1) * 128]))
                        rd = [KT[kt], QL, QRT]
                        if diag:
                            pairs.append((ident_b[:], maskb[:]))
                            rd += [ident_b, maskb]
                        mm(S[:].rearrange("p (h q) -> p h q", h=4), pairs, rd, [S])
                        P = pt_next()
                        ACT(P[:], S[:], AF.Exp, [S], [P], scale=SCALE)
                        return P

                    def stage_pv(ki, P):
                        kt, kb = keyblocks[ki]
                        for hh in range(4):
                            a = acc[hh // 2]
                            mm(a[:, (hh % 2) * 256:(hh % 2 + 1) * 256], [(P[:, hh * 128:(hh + 1) * 128], VV[kt][:, kb, :])],
                               [P, VV[kt]], [a], start=(ki == 0), stop=(ki == nk - 1))
                            mm(den[:, hh:hh + 1], [(P[:, hh * 128:(hh + 1) * 128], ones_b[:, 0:1])], [P, ones_b], [den],
                               start=(ki == 0), stop=(ki == nk - 1))
                    prev = stage_s(0)
                    for ki in range(1, nk):
                        cur = stage_s(ki)
                        stage_pv(ki - 1, prev)
                        prev = cur
                    stage_pv(nk - 1, prev)
                    finish_heads([(acc[hh // 2], acc[hh // 2][:, (hh % 2) * 256:(hh % 2 + 1) * 256]) for hh in range(4)],
                                 den, 128, lambda j, hh, g=g, qb=qb: OLAT[:, j, 4 * g + hh, qb * 128:(qb + 1) * 128])
                    unpin(acc[0], acc[1], den)

        def sample_attention():
            c.barrier()
            GP = 8
            NG = 64 // GP
            lat_g = pool_lat.rearrange("(g r) f -> g (r f)", r=8)
            kr_g = pool_kr.rearrange("(g r) f -> g (r f)", r=8)
            groups = [(b, gi) for b in range(NSMP) for gi in range(NG)]

            def stage_a(n):
                b, gi = groups[n]
                pk = PGK[n % 3]
                pr = PGR[n % 3]
                ktp = KTP[n % 2]
                col = b * NG + gi
                c.dma("pool", [(pk[:].rearrange("p r f -> p (r f)"), lat_g, IDXG[:, col:col + 1])], reads=[IDXG], writes=[pk])
                c.dma("pool", [(pr[:].rearrange("p r f -> p (r f)"), kr_g, IDXG[:, col:col + 1])], reads=[IDXG], writes=[pr])
                for p in range(GP):
                    tp = ps_next()
                    tpb = tp[:].bitcast(BF16)
                    for j in range(2):
                        transpose(tpb[:, j * 128:(j + 1) * 128], pk[:, p, j * 128:(j + 1) * 128], ident_b[:],
                                  [pk, ident_b], [tp])
                    transpose(tpb[0:64, 256:384], pr[:, p, :], ident_b[:], [pr, ident_b], [tp])
                    ACT(ktp[:, 0:2, p, :], tpb[:, 0:256].rearrange("p (j k) -> p j k", j=2), AF.Copy, [tp], [ktp])
                    DVE(lambda e, tpb=tpb, p=p, ktp=ktp: e.tensor_copy(ktp[0:64, 2, p, :], tpb[0:64, 256:384]), [tp], [ktp])

            def stage_b(n):
                b, gi = groups[n]
                ktp = KTP[n % 2]
                S = ps_next()
                for p in range(GP):
                    pairs = [(ktp[:, j, p, :], QL[:, j, :, b * 8:(b + 1) * 8]) for j in range(2)]
                    pairs.append((ktp[0:64, 2, p, :], QRS[0:64, :, b * 8:(b + 1) * 8]))
                    mm(S[:, p * 64:(p + 1) * 64].rearrange("p (h t) -> p h t", h=8), pairs, [ktp, QL, QRS], [S])
                P = pt_next()
                ACT(P[:, 0:GP * 64], S[:, 0:GP * 64], AF.Exp, [S], [P], scale=SCALE)
                return P

            def pv(acc, den, P, vlist, first, last):
                for p, (vb, vap) in enumerate(vlist):
                    mm(acc[0:64, 0:256], [(P[:, p * 64:(p + 1) * 64], vap)], [P, vb], [acc], start=(first and p == 0), stop=last)
                    mm(den[0:64, 0:1], [(P[:, p * 64:(p + 1) * 64], ones_b[:, 0:1])], [P, ones_b], [den],
                       start=(first and p == 0), stop=last)

            ng = len(groups)
            stage_a(0)
            acc = den = None
            for n in range(ng):
                b, gi = groups[n]
                if gi == 0:
                    acc = ps_next(pin=True)
                    den = ps_next(pin=True)
                P = stage_b(n)
                if n + 1 < ng:
                    stage_a(n + 1)
                pk = PGK[n % 3]
                pv(acc, den, P, [(pk, pk[:, p, :]) for p in range(GP)], gi == 0, False)
                if gi == NG - 1:
                    S = ps_next()
                    pairs = [(KTS[:, j, :], QL[:, j, :, b * 8:(b + 1) * 8]) for j in range(2)]
                    pairs.append((KTS[0:64, 2, :], QRS[0:64, :, b * 8:(b + 1) * 8]))
                    pairs.append((ident_b[:], masks[:, b, :, :]))
                    mm(S[:, 0:64].rearrange("p (h t) -> p h t", h=8), pairs, [KTS, QL, QRS, ident_b, masks], [S])
                    P2 = pt_next()
                    ACT(P2[:, 0:64], S[:, 0:64], AF.Exp, [S], [P2], scale=SCALE)
                    pv(acc, den, P2, [(VVS, VVS[:, 0, :])], False, True)
                    finish_heads([(acc, acc[0:64, 0:256])], den, 64,
                                 lambda j, hh, b=b: OLAT[:, j, :, b * 8:(b + 1) * 8],
                                 src_view=lambda ap: ap.rearrange("p (h t) -> p h t", h=8))
                    unpin(acc, den)

        def mem_attention_prompt(N):
            for hh in range(4):
                Ps = []
                for mb in range(2):
                    S = ps_next()
                    mm(S[:, 0:N], [(MKT[:, 2 * hh + j, mb * 128:(mb + 1) * 128], QM[:, 2 * hh + j, 0:N]) for j in range(2)],
                       [MKT, QM], [S])
                    P = pt_next()
                    ACT(P[:, 0:N], S[:, 0:N], AF.Exp, [S], [P], scale=1.0 / 16.0)
                    Ps.append(P)
                dn = ps_next()
                mm(dn[:, 0:N], [(ones_b[:], Ps[mb][:, 0:N]) for mb in range(2)], [ones_b] + Ps, [dn])
                rdn = tmpf()
                DVE(lambda e: e.reciprocal(rdn[:, 0:N], dn[:, 0:N]), [dn], [rdn])
                for j in range(2):
                    po = ps_next()
                    mm(po[:, 0:N], [(MV[:, mb, (2 * hh + j) * 128:(2 * hh + j + 1) * 128], Ps[mb][:, 0:N]) for mb in range(2)],
                       [MV] + Ps, [po])
                    DVE(lambda e, po=po, j=j: e.tensor_tensor(OM[:, 2 * hh + j, 0:N], po[:, 0:N], rdn[:, 0:N], op=ALU.mult),
                        [po, rdn], [OM])

        def mem_attention_sample():
            for b in range(NSMP):
                mk = MKS[b % 2]
                mv = MVS[b % 2]
                mkt = MKST[b % 2]
                c.dma("pool", [(mk[:], mem_k_s[b].rearrange("(m p) f -> p m f", p=128))], writes=[mk])
                c.dma("pool", [(mv[:], mem_v_s[b].rearrange("(m p) f -> p m f", p=128))], writes=[mv])
                for ch in range(DC):
                    tp = ps_next()
                    tpb = tp[:].bitcast(BF16)
                    for mb in range(2):
                        transpose(tpb[:, mb * 128:(mb + 1) * 128], mk[:, mb, ch * 128:(ch + 1) * 128], ident_b[:],
                                  [mk, ident_b], [tp])
                    ACT(mkt[:, ch, :], tpb[:, 0:256], AF.Copy, [tp], [mkt])
                S = ps_next()
                for hh in range(4):
                    for mb in range(2):
                        mm(S[:, mb * 32 + hh * 8: mb * 32 + (hh + 1) * 8],
                           [(mkt[:, 2 * hh + j, mb * 128:(mb + 1) * 128], QM[:, 2 * hh + j, b * 8:(b + 1) * 8]) for j in range(2)],
                           [mkt, QM], [S])
                Pb = pt_next()
                ACT(Pb[:, 0:64], S[:, 0:64], AF.Exp, [S], [Pb], scale=1.0 / 16.0)
                dn = ps_next()
                mm(dn[:, 0:32], [(ones_b[:], Pb[:, mb * 32:(mb + 1) * 32]) for mb in range(2)], [ones_b, Pb], [dn])
                rdn = tmpf()
                DVE(lambda e, dn=dn, rdn=rdn: e.reciprocal(rdn[:, 0:32], dn[:, 0:32]), [dn], [rdn])
                po = ps_next()
                for ch in range(DC):
                    hh = ch // 2
                    mm(po[:, ch * 8:(ch + 1) * 8],
                       [(mv[:, mb, ch * 128:(ch + 1) * 128], Pb[:, mb * 32 + hh * 8: mb * 32 + (hh + 1) * 8]) for mb in range(2)],
                       [mv, Pb], [po])
                for ch in range(DC):
                    hh = ch // 2
                    DVE(lambda e, po=po, rdn=rdn, hh=hh, ch=ch, b=b: e.tensor_tensor(
                        OM[:, ch, b * 8:(b + 1) * 8], po[:, ch * 8:(ch + 1) * 8], rdn[:, hh * 8:(hh + 1) * 8], op=ALU.mult),
                        [po, rdn], [OM])

        def process_tile(kind, ti):
            own = kind == "own"
            smp = kind == "smp"
            N = 128 if smp else TT
            B, L = (NSMP, 8) if smp else (1, TT)
            xsrc = {"oth": x_oth, "own": x_own, "smp": x_smp}[kind]
            cssrc = {"oth": cs_oth, "own": cs_own, "smp": cs_smp}[kind]
            row0 = 0 if smp else ti * TT
            tok0 = {"oth": 0, "own": SEQ_HALF, "smp": 2 * SEQ_HALF}[kind] + row0
            c.dma("pool", [(XT[:, :, 0:N], X1.ap[:, :, tok0:tok0 + N].rearrange("c p n -> p c n"))], reads=[X1.buf], writes=[XT])
            c.dma("pool", [(CS[:, :, 0:N], cssrc[:, :, row0:row0 + N].rearrange("a r n -> r a n"))], writes=[CS])
            rmsnorm_to(lambda ch: XT[:, ch, 0:N], XT, DC, N, D, lambda ch: vrow(2, ch), lambda ch: H[:, ch, 0:N], H)
            if smp:
                kv_path(N, KTS, VVS, lat_smp, kr_smp, 0)
            elif own:
                kv_path(N, KT[NT + ti], VV[NT + ti], lat_own, kr_own, row0)
            else:
                kv_path(N, KT[ti], VV[ti], None, None, 0)
            RX = RXB[:, 0:DC * B * (L + 3)].rearrange("p (c b l) -> p c b l", c=DC, b=B)
            if smp:
                c.dma("pool", [(STG[0:48, :], conv_s)], writes=[STG])
                for ch in range(DC):
                    ps = ps_next()
                    transpose(ps[:, 0:48], STG[0:48, ch * 128:(ch + 1) * 128], ident_f[0:48, 0:48], [STG, ident_f], [ps])
                    ACT(RX[:, ch, :, 0:3], ps[:, 0:48].rearrange("p (b k) -> p b k", b=NSMP), AF.Copy, [ps], [RXB])
                c.dma("pool", [(STG[0:16, :], h_s)], writes=[STG])
                for ch in range(DC):
                    ps = ps_next()
                    transpose(ps[:, 0:16], STG[0:16, ch * 128:(ch + 1) * 128], ident_f[0:16, 0:16], [STG, ident_f], [ps])
                    ACT(H0S[:, ch, :], ps[:, 0:16], AF.Copy, [ps], [H0S])
            else:
                if own and ti == 0:
                    DVE(lambda e: e.tensor_scalar(HALO[:], HALO[:], CMASK[:, 0:1], None, op0=ALU.mult), [HALO, CMASK], [HALO])
                    DVE(lambda e: e.tensor_scalar(CARRY[:], CARRY[:], CMASK[:, 0:1], None, op0=ALU.mult), [CARRY, CMASK], [CARRY])
                DVE(lambda e: e.tensor_copy(RX[:, :, 0, 0:3], HALO[:]), [HALO], [RXB])

            def cons_rx(ci, ps, m):
                ACT(RX[:, ci, :, 3:3 + L], ps[:, 0:N].rearrange("p (b l) -> p b l", b=B), AF.Copy, [ps], [RXB])
            proj(w_in, 704, 1024, N, cons_rx)
            if not smp:
                DVE(lambda e: e.tensor_copy(HALO[:], RX[:, :, 0, L:L + 3]), [RXB], [HALO])
            if smp:
                for ch in range(DC):
                    ps = ps_next()
                    t = tmpf()
                    DVE(lambda e, t=t, ch=ch: e.tensor_copy(t[:, 0:48].rearrange("p (b k) -> p b k", b=NSMP), RX[:, ch, :, 8:11]),
                        [RXB], [t])
                    transpose(ps[0:48, 0:128], t[:, 0:48], ident_f[:], [t, ident_f], [ps])
                    ACT(STG[0:48, ch * 128:(ch + 1) * 128], ps[0:48, 0:128], AF.Copy, [ps], [STG])
                c.dma("act", [(conv_smp, STG[0:48, :])], reads=[STG], is_output=True)
            elif own and ti == NT - 1:
                for ch in range(DC):
                    ps = ps_next()
                    t = tmpf()
                    DVE(lambda e, t=t, ch=ch: e.tensor_copy(t[:, 0:3], RX[:, ch, 0, L:L + 3]), [RXB], [t])
                    transpose(ps[0:3, 0:128], t[:, 0:3], ident_f[:], [t, ident_f], [ps])
                    ACT(STG[0:3, ch * 128:(ch + 1) * 128], ps[0:3, 0:128], AF.Copy, [ps], [STG])
                c.dma("act", [(conv_p, STG[0:3, :])], reads=[STG], is_output=True)
            h0_fn = (lambda ch: H0S[:, ch, :]) if smp else (lambda ch: CARRY[:, ch:ch + 1])
            RX, HS = rnn_tile(N, B, L, h0_fn, smp)
            if smp:
                for ch in range(DC):
                    ps = ps_next()
                    t = tmpf()
                    DVE(lambda e, t=t, ch=ch: e.tensor_copy(t[:, 0:NSMP], HS[:, ch, :, L]), [YB], [t])
                    transpose(ps[0:16, 0:128], t[:, 0:16], ident_f[:], [t, ident_f], [ps])
                    ACT(STG[0:16, ch * 128:(ch + 1) * 128], ps[0:16, 0:128], AF.Copy, [ps], [STG])
                c.dma("act", [(h_smp, STG[0:16, :])], reads=[STG], is_output=True)
            else:
                DVE(lambda e: e.tensor_copy(CARRY[:], HS[:, :, 0, L]), [YB], [CARRY])
                if own and ti == NT - 1:
                    ps = ps_next()
                    transpose(ps[0:8, 0:128], CARRY[:], ident_f[:], [CARRY, ident_f], [ps])
                    ACT(STG[0:8, 0:128], ps[0:8, 0:128], AF.Copy, [ps], [STG])
                    c.dma("act", [(h_p, STG[0:8, 0:128])], reads=[STG], is_output=True)
            if not (own or smp):
                return

            def cons_rg(ci, ps, m):
                t = tmpf()
                ACT(t[:, 0:N], ps[:, 0:N], AF.Gelu, [ps], [t])
                DVE(lambda e, t=t, ci=ci: e.tensor_tensor(YR[:, ci, 0:N].rearrange("p (b l) -> p b l", b=B), HS[:, ci, :, 1:L + 1],
                                                          t[:, 0:N].rearrange("p (b l) -> p b l", b=B), op=ALU.mult), [YB, t], [YR])
            proj(w_in, 1728, 1024, N, cons_rg)
            sg = {}

            def cons_sg(ci, ps, m):
                t = tmpf()
                ACT(t[:, 0:N], ps[:, 0:N], AF.Sigmoid, [ps], [t])
                sg[ci] = t
            for half in range(2):
                sg.clear()
                proj(w_in, 3776 + half * 512, 512, N, cons_sg)

                def cons_orn(ci, ps, m, half=half):
                    DVE(lambda e, ps=ps, ci=ci: e.tensor_tensor(MIXR[:, half * 4 + ci, 0:N], sg[ci][:, 0:N], ps[:, 0:N], op=ALU.mult),
                        [sg[ci], ps], [MIXR])
                proj(w_o_rnn, half * 512, 512, N, cons_orn, src=YR)
            q_path(N, QRS if smp else QRT)
            if smp:
                sample_attention()
            else:
                prompt_attention(ti)
            for hh in range(8):
                ps = ps_next()
                mm(ps[:, 0:N], [(WUV[:, j, hh * 128:(hh + 1) * 128], OLAT[:, j, hh, 0:N]) for j in range(2)], [WUV, OLAT], [ps])
                ACT(YR[:, hh, 0:N], ps[:, 0:N], AF.Copy, [ps], [YR])
            for half in range(2):
                sg.clear()
                proj(w_in, 2752 + half * 512, 512, N, cons_sg)

                def cons_om(ci, ps, m, half=half):
                    t = tmpf()
                    DVE(lambda e, ps=ps, ci=ci, t=t: e.tensor_tensor(t[:, 0:N], sg[ci][:, 0:N], ps[:, 0:N], op=ALU.mult), [sg[ci], ps], [t])
                    DVE(lambda e, ci=ci, t=t: e.tensor_tensor(MIX[:, half * 4 + ci, 0:N], t[:, 0:N], MIXR[:, half * 4 + ci, 0:N], op=ALU.add),
                        [t, MIXR], [MIX])
                proj(w_o_mla, half * 512, 512, N, cons_om, src=YR)
            Y = Yv(N)

            def cons_y(ci, ps, m):
                ACT(Y[:, ci, :], ps[:, 0:N], AF.Copy, [ps], [YB])
            proj(w_out, 0, 1024, N, cons_y, src=MIX)
            residual_add(N, lambda ch: vrow(3, ch))
            rmsnorm_to(lambda ch: XT[:, ch, 0:N], XT, DC, N, D, lambda ch: vrow(4, ch), lambda ch: H[:, ch, 0:N], H)

            def cons_qm(ci, ps, m):
                ACT(QM[:, ci, 0:N], ps[:, 0:N], AF.Copy, [ps], [QM])
            proj(w_mem_q, 0, 1024, N, cons_qm)
            if smp:
                mem_attention_sample()
            else:
                mem_attention_prompt(N)
            proj(w_mem_o, 0, 1024, N, cons_y, src=OM)
            residual_add(N, lambda ch: vrow(5, ch))
            tok3 = (SEQ_HALF if smp else 0) + row0
            c.dma("pool", [(X3.ap[:, :, tok3:tok3 + N].rearrange("c p n -> p c n"), XT[:, :, 0:N])], reads=[XT], writes=[X3.buf],
                  sembuf=X3.buf)

        def prompt_mem_kv():
            N = 256
            load_xT(mem_p, 0, N)
            rmsnorm_to(lambda ch: XT[:, ch, 0:N], XT, DC, N, D, lambda ch: vrow(8, ch), lambda ch: H[:, ch, 0:N], H)

            def cons_k(ci, ps, m):
                ACT(MKT[:, ci, :], ps[:, 0:N], AF.Copy, [ps], [MKT])
            proj(w_mem_k, 0, 1024, N, cons_k)
            for wi, (w_ap, o_ap) in enumerate(((w_mem_k, mk_p), (w_mem_v, mv_p))):
                for half in range(2):
                    slot = wload([(lambda s: wview(s, DC, 512), wsrc(w_ap, 0, D, half * 512, (half + 1) * 512))])
                    wv = wview(slot, DC, 512)
                    for mb in range(2):
                        ps = ps_next()
                        mm(ps[:, 0:512], [(H[:, k, mb * 128:(mb + 1) * 128], wv[:, k, :]) for k in range(DC)], [slot, H], [ps])
                        og = OSTG[st.setdefault("os", 0) % 2]
                        st["os"] += 1
                        ACT(og[:, 0:512], ps[:, 0:512], AF.Copy, [ps], [og])
                        c.dma("act", [(o_ap[mb * 128:(mb + 1) * 128, half * 512:(half + 1) * 512], og[:, 0:512])], reads=[og], is_output=True)
                        if wi == 1:
                            DVE(lambda e, og=og, mb=mb, half=half: e.tensor_copy(MV[:, mb, half * 512:(half + 1) * 512], og[:, 0:512]),
                                [og], [MV])

        PI = OSTG[1]
        PF = OSTG[0]
        c.dma("pool", [(PI[:].bitcast(I32), ptab.to_broadcast([128, NSMP * 64]))], writes=[PI])
        c.op("pool", lambda e: e.iota(IOTA[:], [[0, 1]], base=0, channel_multiplier=1, allow_small_or_imprecise_dtypes=True),
             writes=[IOTA])
        DVE(lambda e: e.tensor_copy(PF[:], PI[:].bitcast(I32)), [PI], [PF])
        OH = tmpf()
        c.op("pool", lambda e: e.memset(OH[:, 0:8], 1.0), writes=[OH])
        c.op("pool", lambda e: e.affine_select(OH[:, 0:8], OH[:, 0:8], [[-16, 8]], ALU.is_ge, 0.0, base=0, channel_multiplier=1),
             reads=[OH], writes=[OH])
        c.op("pool", lambda e: e.affine_select(OH[:, 0:8], OH[:, 0:8], [[16, 8]], ALU.is_ge, 0.0, base=15, channel_multiplier=-1),
             reads=[OH], writes=[OH])
        ACCI = tmpf()
        PF3 = PF[:].rearrange("p (c k) -> p c k", k=8)
        DVE(lambda e: e.tensor_scalar(ACCI[:, 0:128], PF3[:, :, 0], OH[:, 0:1], None, op0=ALU.mult), [PF, OH], [ACCI])
        for k in range(1, 8):
            DVE(lambda e, k=k: e.scalar_tensor_tensor(ACCI[:, 0:128], PF3[:, :, k], OH[:, k:k + 1], ACCI[:, 0:128],
                                                      op0=ALU.mult, op1=ALU.add), [PF, OH, ACCI], [ACCI])
        PM = tmpf()
        DVE(lambda e: e.tensor_scalar(PM[:, 1:2], OH[:, 1:2], 1.0, None, op0=ALU.mult), [OH], [PM])
        for k in range(2, 8):
            DVE(lambda e, k=k: e.scalar_tensor_tensor(PM[:, 1:2], OH[:, k:k + 1], float(k), PM[:, 1:2], op0=ALU.mult, op1=ALU.add),
                [OH, PM], [PM])
        DVE(lambda e: e.scalar_tensor_tensor(PM[:, 0:1], PM[:, 1:2], -16.0, IOTA[:, 0:1], op0=ALU.mult, op1=ALU.add), [PM, IOTA], [PM])
        DVE(lambda e: e.tensor_scalar(ACCI[:, 0:128], ACCI[:, 0:128], 16.0, PM[:, 0:1], op0=ALU.mult, op1=ALU.add), [ACCI, PM], [ACCI])
        DVE(lambda e: e.tensor_copy(IDXG[:], ACCI[:, 0:128]), [ACCI], [IDXG])

        def scratch(name, shape):
            t = nc.dram_tensor(name, list(shape), F32, kind="Internal")
            b = Buf(c, t, name, "dscr")
            c.all_bufs.append(b)
            return WRef(t.ap(), b)
        X1 = scratch("X1", [DC, 128, 2 * SEQ_HALF + 128])
        X3 = scratch("X3", [DC, 128, SEQ_HALF + 128])
        main_set = (XT, YB, H, SQ, AT, TF, RS)
        ffn_set = (XT5, YB5, H5, SQ5, AT5, TF5, RS5)

        c.barrier()
        XT, YB, H, SQ, AT, TF, RS = ffn_set
        jobs = [(x_oth, t * TF_, TF_, t * TF_) for t in range(SEQ_HALF // TF_)]
        jobs += [(x_own, t * TF_, TF_, SEQ_HALF + t * TF_) for t in range(SEQ_HALF // TF_)]
        jobs += [(x_smp, 0, 128, 2 * SEQ_HALF)]

        def pre_loader(t, xt):
            xs, r0, n, tk = jobs[t]
            nb = n // 128
            xin = STG5[:, 0:nb * D].rearrange("p (b f) -> p b f", b=nb)
            c.dma("pool", [(xin, xs[r0:r0 + n, :].rearrange("(b p) f -> p b f", p=128))], writes=[STG5])
            for ch in range(DC):
                ps = ps_next()
                for b in range(nb):
                    transpose(ps[:, b * 128:(b + 1) * 128], xin[:, b, ch * 128:(ch + 1) * 128], ident_f[:], [STG5, ident_f], [ps])
                ACT(xt[:, ch, 0:n], ps[:, 0:n], AF.Copy, [ps], [xt])

        def pre_finisher(t, xt):
            xs, r0, n, tk = jobs[t]
            c.dma("pool", [(X1.ap[:, :, tk:tk + n].rearrange("c p n -> p c n"), xt[:, :, 0:n])], reads=[xt], writes=[X1.buf],
                  sembuf=X1.buf)
        ffn_pipeline(0, 0, lambda ch: DER[:, ch, 1:2], [j[2] for j in jobs], pre_loader, pre_finisher)
        c.barrier()
        XT, YB, H, SQ, AT, TF, RS = main_set
        for i in range(2 * NT):
            c.op("pool", lambda e, i=i: e.memset(KT[i][64:128, 2, :], 0.0), writes=[KT[i]])
            c.dma("pool", [(KT[i][64:65, 2, :], kmask[:, i * TT:(i + 1) * TT])], writes=[KT[i]])

        prompt_mem_kv()
        for ti in range(NT):
            process_tile("oth", ti)
        for ti in range(NT):
            process_tile("own", ti)
        process_tile("smp", 0)

        c.barrier()
        XT, YB, H, SQ, AT, TF, RS = ffn_set
        jobs2 = [(y_own, t * TF_, TF_, t * TF_) for t in range(SEQ_HALF // TF_)] + [(y_smp, 0, 128, SEQ_HALF)]

        def post_loader(t, xt):
            yo, r0, n, tk = jobs2[t]
            c.dma("pool", [(xt[:, :, 0:n], X3.ap[:, :, tk:tk + n].rearrange("c p n -> p c n"))], reads=[X3.buf], writes=[xt])

        def post_finisher(t, xt):
            yo, r0, n, tk = jobs2[t]
            for _ in store_rows(yo, r0, n, lambda ch: xt[:, ch, 0:n], xt, D):
                pass
        ffn_pipeline(1, 6, lambda ch: DER[:, ch, 2:3], [j[2] for j in jobs2], post_loader, post_finisher)
        c.finish("sp")
        print("build: sems", c.nsem, {k: (v.n_inst, v.n_wait) for k, v in c.engs.items()})
    return nc


_CACHE = {}


def _rope_tables(pos):
    inv = (np.float32(10000.0) ** (-np.arange(0, 64, 2, dtype=np.float32) / np.float32(64))).astype(np.float32)
    ang = pos.astype(np.float32)[:, None] * inv[None, :]
    cos = np.cos(ang).astype(np.float32).T
    sin = np.sin(ang).astype(np.float32).T
    return np.ascontiguousarray(np.stack([np.concatenate([cos, cos], 0), np.concatenate([-sin, sin], 0)], 0))


def kernel(x_prompt, x_sample, mem_prompt, cache_mla_latent, cache_mla_krope, page_table,
           state_rnn_conv, state_rnn_h, cache_mem_k, cache_mem_v,
           norms, w_ffn_gate, w_ffn_up, w_ffn_down, w_in, q_norm, kv_norm, w_uq, w_uk, w_uv,
           w_o_mla, conv_w, conv_b, w_rg, b_rg, w_ig, b_ig, lru_lambda, w_o_rnn, w_out,
           mem_norm, w_mem_q, w_mem_k, w_mem_v, w_mem_o):
    f32 = np.float32
    A = lambda a: np.ascontiguousarray(np.asarray(a))
    x_prompt, x_sample, mem_prompt = A(x_prompt), A(x_sample), A(mem_prompt)
    n_pool = cache_mla_latent.shape[1]
    pool_lat = A(cache_mla_latent).reshape(n_pool * 128, 256)
    pool_kr = A(cache_mla_krope).reshape(n_pool * 128, 64)
    page_table = A(page_table).astype(np.int32)
    vec = np.zeros((19, D), f32)
    vec[0:8] = A(norms)[0]
    vec[8] = A(mem_norm)[0]
    vec[9:13] = A(conv_w)[0]
    vec[13] = A(conv_b)[0]
    vec[14] = A(b_rg)[0]
    vec[15] = A(b_ig)[0]
    vec[16] = A(lru_lambda)[0]
    vec[17, :384] = A(q_norm)[0]
    vec[18, :256] = A(kv_norm)[0]
    shared = {
        "pool_lat": pool_lat, "pool_kr": pool_kr, "vec": vec,
        "w_gate": A(w_ffn_gate)[0], "w_up": A(w_ffn_up)[0], "w_down": A(w_ffn_down)[0], "w_in": A(w_in)[0],
        "w_uq": A(w_uq)[0], "w_uk": A(w_uk)[0].reshape(256, 1024), "w_uv": A(w_uv)[0].reshape(256, 1024),
        "w_o_mla": A(w_o_mla)[0], "w_rg": A(w_rg)[0], "w_ig": A(w_ig)[0], "w_o_rnn": A(w_o_rnn)[0], "w_out": A(w_out)[0],
        "w_mem_q": A(w_mem_q)[0], "w_mem_k": A(w_mem_k)[0], "w_mem_v": A(w_mem_v)[0], "w_mem_o": A(w_mem_o)[0],
    }
    past = page_table.shape[1] * 128
    cs_first = _rope_tables(np.arange(0, SEQ_HALF))
    cs_second = _rope_tables(np.arange(SEQ_HALF, 2 * SEQ_HALF))
    cs_smp = _rope_tables(np.tile(past + np.arange(8), NSMP))
    kmask = np.concatenate([np.ones(SEQ_HALF, f32), np.zeros(SEQ_HALF, f32)])[None, :]
    in_maps = []
    for core in range(NCORES):
        s, half = core // 2, core % 2
        bs = slice(core * NSMP, (core + 1) * NSMP)
        m = dict(shared)
        m["x_oth"] = x_prompt[s, 0:SEQ_HALF]
        m["x_own"] = x_prompt[s, half * SEQ_HALF:(half + 1) * SEQ_HALF]
        m["x_smp"] = x_sample[bs].reshape(128, D)
        m["mem_p"] = mem_prompt[s]
        m["ptab"] = page_table[bs].reshape(1, NSMP * 64)
        m["conv_s"] = A(state_rnn_conv)[0, bs].reshape(NSMP * 3, D)
        m["h_s"] = A(state_rnn_h)[0, bs]
        m["mem_k_s"] = A(cache_mem_k)[0, bs].reshape(NSMP, 256, D)
        m["mem_v_s"] = A(cache_mem_v)[0, bs].reshape(NSMP, 256, D)
        m["cs_oth"] = cs_first
        m["cs_own"] = cs_second if half else cs_first
        m["cs_smp"] = cs_smp
        m["kmask"] = kmask
        m["pmask"] = np.full((1, 8 * TT), 0.0 if half else NEG, f32)
        m["cmask"] = np.full((128, 1), 1.0 if half else 0.0, f32)
        in_maps.append(m)
    key = n_pool
    if key not in _CACHE:
        _CACHE[key] = build(n_pool)
    nc = _CACHE[key]
    res = run_bass_kernel_spmd(nc, in_maps, core_ids=list(range(NCORES))).results
    B, S = x_prompt.shape[0], x_prompt.shape[1]
    y_p = np.zeros((B, S, D), f32)
    lat_p = np.zeros((1, B, S, 256), f32)
    kr_p = np.zeros((1, B, S, 64), f32)
    conv_p = np.zeros((1, B, 3, D), f32)
    h_p = np.zeros((1, B, D), f32)
    mk_p = np.zeros((1, B, 256, 4, 256), f32)
    mv_p = np.zeros((1, B, 256, 4, 256), f32)
    y_s = np.zeros((128, 8, D), f32)
    lat_s = np.zeros((1, 128, 8, 256), f32)
    kr_s = np.zeros((1, 128, 8, 64), f32)
    conv_sn = np.zeros((1, 128, 3, D), f32)
    h_sn = np.zeros((1, 128, D), f32)
    for core in range(NCORES):
        r = res[core]
        s, half = core // 2, core % 2
        sl = slice(half * SEQ_HALF, (half + 1) * SEQ_HALF)
        bs = slice(core * NSMP, (core + 1) * NSMP)
        y_p[s, sl] = r["y_own"]
        lat_p[0, s, sl] = r["lat_own"]
        kr_p[0, s, sl] = r["kr_own"]
        if half == 1:
            conv_p[0, s] = r["conv_p"]
            h_p[0, s] = r["h_p"].reshape(D)
        else:
            mk_p[0, s] = r["mk_p"].reshape(256, 4, 256)
            mv_p[0, s] = r["mv_p"].reshape(256, 4, 256)
        y_s[bs] = r["y_smp"].reshape(NSMP, 8, D)
        lat_s[0, bs] = r["lat_smp"].reshape(NSMP, 8, 256)
        kr_s[0, bs] = r["kr_smp"].reshape(NSMP, 8, 64)
        conv_sn[0, bs] = r["conv_smp"].reshape(NSMP, 3, D)
        h_sn[0, bs] = r["h_smp"]
    return (y_p, y_s, lat_p, kr_p, lat_s, kr_s, conv_p, conv_sn, h_p, h_sn, mk_p, mv_p)
```

```python
import numpy as np
from contextlib import ExitStack
import concourse.bass as bass
import concourse.mybir as mybir
from concourse.bass_utils import run_bass_kernel_spmd

F32 = mybir.dt.float32
BF16 = mybir.dt.bfloat16
I32 = mybir.dt.int32
AF = mybir.ActivationFunctionType
ALU = mybir.AluOpType

NCORES = 8
D = 1024
DC = 8
FF = 2816
FC = 22
TT = 256
NBLK = TT // 128
SEQ_HALF = 2048
NT = SEQ_HALF // TT
NSMP = 16
EPS = 1e-6
SCALE = 192.0 ** -0.5
NEG = -30000.0


class Tok:
    __slots__ = ("sem", "val", "eng")

    def __init__(self, sem, val, eng):
        self.sem = sem
        self.val = val
        self.eng = eng


class Buf:
    def __init__(self, ctx, t, name, space):
        self.ctx = ctx
        self.t = t
        self.name = name
        self.space = space
        self.last_w = None
        self.readers = {}
        self.dsem = None
        self.dcount = 0
        self.core = self

    def __getitem__(self, idx):
        return self.t[idx]

    def get_dsem(self):
        if self.dsem is None:
            self.dsem = self.ctx.new_sem("d_" + self.name)
        return self.dsem


class EngState:
    def __init__(self, name, obj, sem):
        self.name = name
        self.obj = obj
        self.sem = sem
        self.count = 0
        self.known = {}
        self.pending = None
        self.n_wait = 0
        self.n_inst = 0


class Ctx:
    def __init__(self, nc, es):
        self.nc = nc
        self.es = es
        self.nsem = 0
        self.engs = {}
        for name, obj in (("pe", nc.tensor), ("act", nc.scalar), ("dve", nc.vector),
                          ("pool", nc.gpsimd), ("sp", nc.sync)):
            self.engs[name] = EngState(name, obj, self.new_sem("e_" + name))
        self.out_toks = []
        self.all_bufs = []

    def view(self, ap, name, core=None):
        b = Buf(self, ap, name, "sb")
        if core is not None:
            b.core = core
        else:
            self.all_bufs.append(b)
        return b

    def new_sem(self, name):
        self.nsem += 1
        return self.es.enter_context(self.nc.semaphore(name))

    def sb(self, name, shape, dtype):
        t = self.es.enter_context(self.nc.sbuf_tensor(name, list(shape), dtype))
        b = Buf(self, t, name, "sb")
        self.all_bufs.append(b)
        return b

    def ps(self, name, shape, dtype=F32):
        t = self.es.enter_context(self.nc.psum_tensor(name, list(shape), dtype))
        return Buf(self, t, name, "ps")

    def _need(self, E, reads, writes, strict=False):
        reads = [b.core for b in reads]
        writes = [b.core for b in writes]
        toks = []
        for b in reads:
            if b.last_w is not None:
                toks.append(b.last_w)
        for b in writes:
            if b.last_w is not None and (strict or b.last_w.eng != E.name):
                toks.append(b.last_w)
            for tk in b.readers.values():
                if strict or tk.eng != E.name:
                    toks.append(tk)
        best = {}
        for tk in toks:
            if tk.eng == "pe" and E.name == "pe":
                continue
            assert tk.val is not None, "dependency on un-signalled PE op"
            k = id(tk.sem)
            if E.known.get(k, 0) >= tk.val:
                continue
            if k not in best or best[k].val < tk.val:
                best[k] = tk
        for k, tk in best.items():
            E.obj.wait_ge(tk.sem, tk.val)
            E.known[k] = tk.val
            E.n_wait += 1

    def _record(self, tok, reads, writes):
        reads = [b.core for b in reads]
        writes = [b.core for b in writes]
        k = id(tok.sem)
        for b in reads:
            b.readers[k] = tok
        for b in writes:
            b.last_w = tok
            b.readers = {}

    def op(self, eng, fn, reads=(), writes=(), inc=True):
        E = self.engs[eng]
        self._need(E, reads, writes)
        inst = fn(E.obj)
        E.n_inst += 1
        if eng == "pe":
            if E.pending is None:
                E.pending = Tok(E.sem, None, "pe")
            tok = E.pending
            if inc:
                E.count += 1
                inst.then_inc(E.sem, 1)
                tok.val = E.count
                E.pending = None
        else:
            E.count += 1
            inst.then_inc(E.sem, 1)
            tok = Tok(E.sem, E.count, eng)
        self._record(tok, reads, writes)
        return inst

    def dma(self, q, parts, reads=(), writes=(), sembuf=None, is_output=False, indirect=False):
        E = self.engs[q]
        self._need(E, reads, writes, strict=True)
        sb = sembuf
        if sb is None:
            cands = [b for b in list(writes) + list(reads) if b.space != "dram"]
            sb = cands[0]
        sb = sb.core
        sem = sb.get_dsem()
        for p in parts:
            if len(p) == 3:
                inst = E.obj.indirect_dma_start(out=p[0], out_offset=None, in_=p[1],
                                                in_offset=bass.IndirectOffsetOnAxis(ap=p[2], axis=0))
            else:
                inst = E.obj.dma_start(out=p[0], in_=p[1])
            sb.dcount += 16
            inst.then_inc(sem, 16)
            E.n_inst += 1
        tok = Tok(sem, sb.dcount, "dma")
        self._record(tok, reads, writes)
        if is_output:
            self.out_toks.append(tok)
        return tok

    def barrier(self):
        dsems = [(b.dsem, b.dcount) for b in self.all_bufs if b.dsem is not None and b.dcount > 0]
        for name, E in self.engs.items():
            for oname, X in self.engs.items():
                if oname != name and X.count > 0 and E.known.get(id(X.sem), 0) < X.count:
                    E.obj.wait_ge(X.sem, X.count)
                    E.known[id(X.sem)] = X.count
            for sem, cnt in dsems:
                if E.known.get(id(sem), 0) < cnt:
                    E.obj.wait_ge(sem, cnt)
                    E.known[id(sem)] = cnt

    def finish(self, eng="sp"):
        E = self.engs[eng]
        best = {}
        for tk in self.out_toks:
            k = id(tk.sem)
            if k not in best or best[k].val < tk.val:
                best[k] = tk
        for k, tk in best.items():
            E.obj.wait_ge(tk.sem, tk.val)
        for name, X in self.engs.items():
            if X.count > 0 and name != eng:
                E.obj.wait_ge(X.sem, X.count)


def build(n_pool):
    nc = bass.Bass("TRN2", target_bir_lowering=False)

    def din(name, shape, dt=F32):
        return nc.dram_tensor(name, list(shape), dt, kind="ExternalInput").ap()

    def dout(name, shape, dt=F32):
        return nc.dram_tensor(name, list(shape), dt, kind="ExternalOutput").ap()

    x_oth = din("x_oth", [SEQ_HALF, D])
    x_own = din("x_own", [SEQ_HALF, D])
    x_smp = din("x_smp", [128, D])
    mem_p = din("mem_p", [256, D])
    pool_lat = din("pool_lat", [n_pool * 128, 256])
    pool_kr = din("pool_kr", [n_pool * 128, 64])
    ptab = din("ptab", [1, NSMP * 64], I32)
    conv_s = din("conv_s", [NSMP * 3, D])
    h_s = din("h_s", [NSMP, D])
    mem_k_s = din("mem_k_s", [NSMP, 256, D])
    mem_v_s = din("mem_v_s", [NSMP, 256, D])
    cs_oth = din("cs_oth", [2, 64, SEQ_HALF])
    cs_own = din("cs_own", [2, 64, SEQ_HALF])
    cs_smp = din("cs_smp", [2, 64, 128])
    kmask = din("kmask", [1, 2 * SEQ_HALF])
    pmask = din("pmask", [1, 8 * TT])
    cmask_d = din("cmask", [128, 1])
    vec_d = din("vec", [19, D])
    w_gate = din("w_gate", [2, D, FF])
    w_up = din("w_up", [2, D, FF])
    w_down = din("w_down", [2, FF, D])
    w_in = din("w_in", [D, 4800])
    w_uq = din("w_uq", [384, 1536])
    w_uk = din("w_uk", [256, 1024])
    w_uv = din("w_uv", [256, 1024])
    w_o_mla = din("w_o_mla", [D, D])
    w_rg = din("w_rg", [8, 128, 128])
    w_ig = din("w_ig", [8, 128, 128])
    w_o_rnn = din("w_o_rnn", [D, D])
    w_out = din("w_out", [D, D])
    w_mem_q = din("w_mem_q", [D, D])
    w_mem_k = din("w_mem_k", [D, D])
    w_mem_v = din("w_mem_v", [D, D])
    w_mem_o = din("w_mem_o", [D, D])

    y_own = dout("y_own", [SEQ_HALF, D])
    y_smp = dout("y_smp", [128, D])
    lat_own = dout("lat_own", [SEQ_HALF, 256])
    kr_own = dout("kr_own", [SEQ_HALF, 64])
    lat_smp = dout("lat_smp", [128, 256])
    kr_smp = dout("kr_smp", [128, 64])
    conv_p = dout("conv_p", [3, D])
    conv_smp = dout("conv_smp", [NSMP * 3, D])
    h_p = dout("h_p", [8, 128])
    h_smp = dout("h_smp", [NSMP, D])
    mk_p = dout("mk_p", [256, D])
    mv_p = dout("mv_p", [256, D])

    es = ExitStack()
    with es:
        c = Ctx(nc, es)
        W = TT + 8

        PS = [c.ps(f"ps{i}", [128, 512], F32) for i in range(8)]
        st = {"ps": 0, "w": 0, "tf": 0, "tb": 0, "pt": 0}

        pinned = set()

        def ps_next(pin=False):
            while True:
                b = PS[st["ps"] % 8]
                st["ps"] += 1
                if id(b) not in pinned:
                    break
            if pin:
                pinned.add(id(b))
            return b

        def unpin(*bs):
            for b in bs:
                pinned.discard(id(b))

        NW = 4
        WS = [c.sb(f"wslot{i}", [128, 4096], BF16) for i in range(NW)]
        WORK_WORDS = 22672
        WORK = es.enter_context(nc.sbuf_tensor("WORK", [128, WORK_WORDS], F32))
        wk = {"o": 0}

        def wcarve(name, shape, dtype, parts=128):
            n = int(np.prod(shape[1:]))
            words = n if dtype == F32 else (n + 1) // 2
            ap = WORK[:, wk["o"]:wk["o"] + words]
            wk["o"] += words
            assert wk["o"] <= WORK_WORDS, (name, wk["o"])
            if dtype != F32:
                ap = ap.bitcast(dtype)[:, 0:n]
            if len(shape) == 3:
                ap = ap.rearrange("p (a b) -> p a b", a=shape[1])
            elif len(shape) == 4:
                ap = ap.rearrange("p (a b c) -> p a b c", a=shape[1], b=shape[2])
            if parts != 128:
                ap = ap[0:parts]
            return c.view(ap, name)
        TF = [wcarve(f"tf{i}", [128, W], F32) for i in range(8)]
        TB = [wcarve(f"tb{i}", [128, W], BF16) for i in range(3)]
        PTs = [c.sb(f"pt{i}", [128, 512], BF16) for i in range(3)]
        RS = [wcarve(f"rs{i}", [128, W], F32) for i in range(3)]

        def tmpf():
            b = TF[st["tf"] % len(TF)]
            st["tf"] += 1
            return b

        def tmpb():
            b = TB[st["tb"] % len(TB)]
            st["tb"] += 1
            return b

        def pt_next():
            b = PTs[st["pt"] % len(PTs)]
            st["pt"] += 1
            return b

        ident_f = c.sb("ident_f", [128, 128], F32)
        ident_b = c.sb("ident_b", [128, 128], BF16)
        ones_b = c.sb("ones_b", [128, 128], BF16)
        maskb = c.sb("maskb", [128, 4, 128], BF16)
        masks = c.sb("masks", [128, NSMP, 8, 8], BF16)
        VT = c.sb("VT", [128, DC, 19], F32)
        DER = c.sb("DER", [128, DC, 4], F32)
        HB = c.sb("HB", [128, DC, 2], F32)
        CARRY = c.sb("CARRY", [128, DC], F32)
        HALO = c.sb("HALO", [128, DC, 3], F32)
        CMASK = c.sb("CMASK", [128, 1], F32)
        ARENA = es.enter_context(nc.sbuf_tensor("ARENA", [128, 20480], BF16))
        ar = {"o": 0}

        def carve(name, shape):
            n = int(np.prod(shape[1:]))
            ap = ARENA[:, ar["o"]:ar["o"] + n]
            ar["o"] += n
            assert ar["o"] <= 20480, (name, ar["o"])
            if len(shape) == 3:
                ap = ap.rearrange("p (a b) -> p a b", a=shape[1])
            elif len(shape) == 4:
                ap = ap.rearrange("p (a b c) -> p a b c", a=shape[1], b=shape[2])
            return c.view(ap, name)
        KT = [carve(f"KT{i}", [128, 3, TT]) for i in range(2 * NT)]
        VV = [carve(f"VV{i}", [128, NBLK, 256]) for i in range(2 * NT)]
        ar["o"] = 0
        PGK = [carve(f"PGK{i}", [128, 8, 256]) for i in range(3)]
        PGR = [carve(f"PGR{i}", [128, 8, 64]) for i in range(3)]
        KTP = [carve(f"KTP{i}", [128, 3, 8, 128]) for i in range(2)]
        MKS = [carve(f"MKS{i}", [128, 2, D]) for i in range(1)] * 2
        MVS = [carve(f"MVS{i}", [128, 2, D]) for i in range(1)] * 2
        MKST = [carve(f"MKST{i}", [128, DC, 256]) for i in range(1)] * 2
        KTS = c.sb("KTS", [128, 3, 128], BF16)
        VVS = c.sb("VVS", [128, 1, 256], BF16)
        MKT = c.sb("MKT", [128, DC, 256], BF16)
        MV = c.sb("MV", [128, 2, D], BF16)
        WUKT = c.sb("WUKT", [128, 8, 256], BF16)
        WUV = c.sb("WUV", [128, 2, 1024], BF16)
        WRG = c.sb("WRG", [128, 8, 128], BF16)
        WIG = c.sb("WIG", [128, 8, 128], BF16)
        XT = wcarve("XT", [128, DC, TT], F32)
        YB = wcarve("YB", [128, DC * (TT + 1)], F32)
        RXB = wcarve("RXB", [128, DC * (TT + 3)], F32)
        OLAT = wcarve("OLAT", [128, 2, 8, TT], BF16)
        H = wcarve("H", [128, DC, TT], BF16)
        off_sq = wk["o"]
        SQ = wcarve("SQ", [128, DC, TT], BF16)
        MIX = wcarve("MIX", [128, DC, TT], BF16)
        AT = wcarve("AT", [128, 24 * TT], BF16)
        BIGT = AT.t
        AT.t = BIGT[:, 0:FC * TT].rearrange("p (f n) -> p f n", f=FC)
        QL = c.view(BIGT[:, 0:16 * TT].rearrange("p (j h n) -> p j h n", j=2, h=8), "QL", core=AT)
        QN = c.view(BIGT[:, 16 * TT:24 * TT].rearrange("p (h n) -> p h n", h=8), "QN", core=AT)
        QRT = c.sb("QRT", [128, 8, TT], BF16)
        QRS = QRT
        YR = wcarve("YR", [128, DC, TT], BF16)
        MIXR = wcarve("MIXR", [128, DC, TT], BF16)
        off_rend = wk["o"]
        RNW = DC * (TT + 1)
        assert off_sq + 3 * RNW <= off_rend, (off_sq, off_rend)
        RA = c.view(WORK[:, off_sq:off_sq + RNW], "RA")
        RM = c.view(WORK[:, off_sq + RNW:off_sq + 2 * RNW], "RM")
        RW = c.view(WORK[:, off_sq + 2 * RNW:off_sq + 3 * RNW], "RW")
        SCR = wcarve("SCR", [128, 4, TT], F32)
        CQ = wcarve("CQ", [128, 3, TT], F32)
        CQN = wcarve("CQN", [128, 3, TT], BF16)
        KRT = wcarve("KRT", [128, TT], F32, parts=64)
        CS = wcarve("CS", [128, 2, TT], F32, parts=64)
        main_words = wk["o"]
        wk["o"] = 0
        TF_ = 512
        XT5 = wcarve("XT5", [128, DC, TF_], F32)
        YB5 = wcarve("YB5", [128, DC * TF_], F32)
        H5 = wcarve("H5", [128, DC, TF_], BF16)
        SQ5 = wcarve("SQ5", [128, DC, TF_], BF16)
        AT5 = wcarve("AT5", [128, FC, TF_], BF16)
        TF5 = [wcarve(f"tf5_{i}", [128, TF_ + 8], F32) for i in range(4)]
        RS5 = [wcarve(f"rs5_{i}", [128, TF_ + 8], F32) for i in range(3)]
        print("WORK words: main", main_words, "ffn", wk["o"])
        XT5b = c.view(ARENA[:, 0:8192].bitcast(F32).rearrange("p (c n) -> p c n", c=DC), "XT5b")
        H5b = c.view(ARENA[:, 8192:12288].rearrange("p (c n) -> p c n", c=DC), "H5b")
        STG5 = c.view(ARENA[:, 12288:20480].bitcast(F32), "STG5")
        OSTG = [c.sb(f"OSTG{i}", [128, D], F32) for i in range(2)]
        RD = c.sb("RD", [128, 8], F32)
        ONF = c.sb("ONF", [128, 256], F32)
        IDXG = c.sb("IDXG", [128, NSMP * 8], I32)
        IOTA = c.sb("IOTA", [128, 1], F32)
        STG = OSTG[1]
        VEC = STG
        H0S = c.sb("H0S", [128, DC, NSMP], F32)
        QM = QN
        OM = MIX

        def ACT(out, in_, func, reads, writes, **kw):
            return c.op("act", lambda e: e.activation(out, in_, func, **kw), reads=reads, writes=writes)

        def DVE(fn, reads, writes):
            return c.op("dve", fn, reads=reads, writes=writes)

        def mm(out_ap, pairs, reads, writes, start=True, stop=True):
            n = len(pairs)
            for i, (l, r) in enumerate(pairs):
                c.op("pe", lambda e, l=l, r=r, i=i: e.matmul(out_ap, l, r, start=(start and i == 0),
                                                              stop=(stop and i == n - 1)),
                     reads=reads, writes=writes, inc=(i == n - 1))

        def transpose(out_ap, in_ap, ident_ap, reads, writes):
            c.op("pe", lambda e: e.transpose(out_ap, in_ap, ident_ap), reads=reads, writes=writes, inc=True)

        class WRef:
            def __init__(self, ap, buf):
                self.ap = ap
                self.buf = buf
                self.colbufs = None

            def __getitem__(self, idx):
                buf = self.buf
                if self.colbufs and isinstance(idx, tuple) and len(idx) == 2 and isinstance(idx[1], slice):
                    c0 = idx[1].start or 0
                    for lo, hi, b in self.colbufs:
                        if lo <= c0 < hi:
                            buf = b
                return WRef(self.ap[idx], buf)

            def rearrange(self, pat, **kw):
                return WRef(self.ap.rearrange(pat, **kw), self.buf)

        def prep(name, src_ap, shape, split):
            t = nc.dram_tensor("wb_" + name, list(shape), BF16, kind="Internal")
            b = Buf(c, t, "wb_" + name, "dscr")
            c.all_bufs.append(b)
            dst = t.ap()
            lead = len(shape) - 2
            srcs = [src_ap[i] for i in range(shape[0])] if lead else [src_ap]
            dsts = [dst[i] for i in range(shape[0])] if lead else [dst]
            parts = []
            for sa, da in zip(srcs, dsts):
                if split > 1:
                    sa = sa.rearrange("k (a b) -> (k a) b", a=split)
                    da = da.rearrange("k (a b) -> (k a) b", a=split)
                parts.append((da, sa))
            c.dma("pool", parts, writes=[b], sembuf=b)
            return WRef(dst, b)

        def wload(parts):
            slot = WS[st["w"] % NW]
            st["w"] += 1
            c.dma("sp", [(fn(slot), src.ap) for fn, src in parts], reads=[src.buf for fn, src in parts], writes=[slot],
                  sembuf=slot)
            return slot

        def wview(slot, k, n):
            return slot[:, 0:k * n].rearrange("p (k n) -> p k n", k=k)

        def wsrc(w_ap, r0, r1, c0, c1):
            return w_ap[r0:r1, c0:c1].rearrange("(k p) n -> p k n", p=128)

        w_gate_f, w_up_f, w_down_f = w_gate, w_up, w_down
        def mk(name, rows, cols):
            t = nc.dram_tensor("wb_" + name, [rows, cols], BF16, kind="Internal")
            r = WRef(t.ap(), None)
            r.colbufs = []
            r.t = t
            return r

        def cast_cols(r, src2d, lo, hi):
            b = Buf(c, r.t, f"wbc_{id(r) % 100000}_{lo}", "dscr")
            c.all_bufs.append(b)
            c.dma("pool", [(r.ap[:, lo:hi], src2d[:, lo:hi])], writes=[b], sembuf=b)
            r.colbufs.append((lo, hi, b))
            r.buf = b
        g0 = mk("gate0", D, FF)
        u0 = mk("up0", D, FF)
        cast_cols(g0, w_gate_f[0], 0, 1280)
        cast_cols(u0, w_up_f[0], 0, 1280)
        cast_cols(g0, w_gate_f[0], 1280, FF)
        cast_cols(u0, w_up_f[0], 1280, FF)
        d0 = prep("down0", w_down_f[0], [FF, D], 1)
        w_in = prep("in", w_in, [D, 4800], 3)
        w_uq = prep("uq", w_uq, [384, 1536], 1)
        w_o_rnn = prep("o_rnn", w_o_rnn, [D, D], 1)
        w_o_mla = prep("o_mla", w_o_mla, [D, D], 1)
        w_out = prep("out", w_out, [D, D], 1)
        w_mem_q = prep("mem_q", w_mem_q, [D, D], 1)
        w_mem_k = prep("mem_k", w_mem_k, [D, D], 1)
        w_mem_v = prep("mem_v", w_mem_v, [D, D], 1)
        w_mem_o = prep("mem_o", w_mem_o, [D, D], 1)
        w_gate = [g0, prep("gate1", w_gate_f[1], [D, FF], 2)]
        w_up = [u0, prep("up1", w_up_f[1], [D, FF], 2)]
        w_down = [d0, prep("down1", w_down_f[1], [FF, D], 1)]

        c.op("pool", lambda e: e.memset(ident_f[:], 0.0), writes=[ident_f])
        c.op("pool", lambda e: e.affine_select(ident_f[:], ident_f[:], [[-1, 128]], ALU.not_equal, 1.0,
                                               base=0, channel_multiplier=1), reads=[ident_f], writes=[ident_f])
        DVE(lambda e: e.tensor_copy(ident_b[:], ident_f[:]), [ident_f], [ident_b])
        c.op("pool", lambda e: e.memset(ones_b[:], 1.0), writes=[ones_b])
        c.op("pool", lambda e: e.memset(maskb[:], 0.0), writes=[maskb])
        c.op("pool", lambda e: e.affine_select(maskb[:], maskb[:], [[0, 4], [1, 128]], ALU.is_ge, NEG,
                                               base=0, channel_multiplier=-1), reads=[maskb], writes=[maskb])
        c.op("pool", lambda e: e.memset(masks[:], 0.0), writes=[masks])
        c.op("pool", lambda e: e.affine_select(masks[:], masks[:], [[8, NSMP], [0, 8], [1, 8]], ALU.is_ge, NEG,
                                               base=0, channel_multiplier=-1), reads=[masks], writes=[masks])
        c.op("pool", lambda e: e.affine_select(masks[:], masks[:], [[-8, NSMP], [0, 8], [0, 8]], ALU.is_ge, NEG,
                                               base=0, channel_multiplier=1), reads=[masks], writes=[masks])
        c.op("pool", lambda e: e.memset(HALO[:], 0.0), writes=[HALO])
        c.op("pool", lambda e: e.memset(CARRY[:], 0.0), writes=[CARRY])
        c.dma("pool", [(CMASK[:], cmask_d)], writes=[CMASK])
        c.dma("pool", [(VEC[0:19, :], vec_d)], writes=[VEC])
        pv = ps_next()
        for ch in range(DC):
            transpose(pv[:, ch * 19:(ch + 1) * 19], VEC[0:19, ch * 128:(ch + 1) * 128], ident_f[0:19, 0:19],
                      [VEC, ident_f], [pv])
        DVE(lambda e: e.tensor_copy(VT[:], pv[:, 0:DC * 19].rearrange("p (c r) -> p c r", c=DC)), [pv], [VT])
        def vrow(r, ch):
            return VT[:, ch, r:r + 1]
        t0 = tmpf()
        ACT(t0[:, 0:DC], VT[:, :, 16], AF.Exp, [VT], [t0], scale=-1.0)
        ACT(t0[:, 0:DC], t0[:, 0:DC], AF.Ln, [t0], [t0], bias=1.0)
        DVE(lambda e: e.tensor_scalar(DER[:, :, 0], t0[:, 0:DC], -8.0, None, op0=ALU.mult), [t0], [DER])
        DVE(lambda e: e.tensor_scalar(DER[:, :, 1], VT[:, :, 1], 0.5, None, op0=ALU.mult), [VT], [DER])
        DVE(lambda e: e.tensor_scalar(DER[:, :, 2], VT[:, :, 7], 0.5, None, op0=ALU.mult), [VT], [DER])
        DVE(lambda e: e.tensor_scalar(DER[:, :, 3], t0[:, 0:DC], -4.0, None, op0=ALU.mult), [t0], [DER])
        DVE(lambda e: e.tensor_scalar(HB[:, :, 0], VT[:, :, 14], 0.5, None, op0=ALU.mult), [VT], [HB])
        DVE(lambda e: e.tensor_scalar(HB[:, :, 1], VT[:, :, 15], 0.5, None, op0=ALU.mult), [VT], [HB])

        c.dma("pool", [(WUV[:], w_uv.rearrange("(j p) n -> p j n", p=128))], writes=[WUV])
        c.dma("pool", [(WRG[:], w_rg.rearrange("g i j -> i g j"))], writes=[WRG])
        c.dma("pool", [(WIG[:], w_ig.rearrange("g i j -> i g j"))], writes=[WIG])
        c.dma("pool", [(YB[:, 0:2048].rearrange("p (j n) -> p j n", j=2), w_uk.rearrange("(j p) n -> p j n", p=128))],
              writes=[YB])
        for hh in range(8):
            pw = ps_next()
            for j in range(2):
                transpose(pw[:, j * 128:(j + 1) * 128], YB[:, j * 1024 + hh * 128: j * 1024 + (hh + 1) * 128],
                          ident_f[:], [YB, ident_f], [pw])
            DVE(lambda e, pw=pw, hh=hh: e.tensor_copy(WUKT[:, hh, :], pw[:, 0:256]), [pw], [WUKT])
        c.op("pool", lambda e: e.memset(QRT[64:128, :, :], 0.0), writes=[QRT])
        c.dma("pool", [(QRT[64:65, :, :].rearrange("p h n -> p (h n)"), pmask)], writes=[QRT])

        def rms_stats(xap, nch, N, Dn, xbuf):
            for ch in range(nch):
                ACT(SQ[:, ch, 0:N], xap(ch), AF.Square, [xbuf], [SQ])
            ps = ps_next()
            mm(ps[:, 0:N], [(ones_b[:], SQ[:, ch, 0:N]) for ch in range(nch)], [ones_b, SQ], [ps])
            t = RS[2]
            ACT(t[:, 0:N], ps[:, 0:N], AF.Ln, [ps], [t], bias=EPS, scale=1.0 / Dn)
            r = RS[st.setdefault("rs", 0) % 2]
            st["rs"] += 1
            ACT(r[:, 0:N], t[:, 0:N], AF.Exp, [t], [r], scale=-0.5)
            return r

        def rmsnorm_to(xap, xbuf, nch, N, Dn, gain, outap, outbuf):
            r = rms_stats(xap, nch, N, Dn, xbuf)
            for ch in range(nch):
                DVE(lambda e, ch=ch: e.scalar_tensor_tensor(outap(ch), xap(ch), gain(ch), r[:, 0:N],
                                                            op0=ALU.mult, op1=ALU.mult), [xbuf, r, VT, DER], [outbuf])

        def Yv(N):
            return YB[:, 0:DC * N].rearrange("p (c n) -> p c n", c=DC)

        def residual_add(N, gain, xt=None):
            XT_ = XT if xt is None else xt
            Y = Yv(N)
            r = rms_stats(lambda ch: Y[:, ch, :], DC, N, D, YB)
            for ch in range(DC):
                t = tmpf()
                DVE(lambda e, ch=ch, t=t: e.scalar_tensor_tensor(t[:, 0:N], Y[:, ch, :], gain(ch), r[:, 0:N],
                                                                  op0=ALU.mult, op1=ALU.mult), [YB, r, VT, DER], [t])
                DVE(lambda e, ch=ch, t=t: e.tensor_tensor(XT_[:, ch, 0:N], XT_[:, ch, 0:N], t[:, 0:N], op=ALU.add),
                    [XT_, t], [XT_])

        def proj(w_ap, col0, ncols, N, consumer, src=None, kch=DC, chunk=128):
            src = H if src is None else src
            done = 0
            while done < ncols:
                gw = min(512, ncols - done)
                slot = wload([(lambda s, gw=gw: wview(s, kch, gw), wsrc(w_ap, 0, kch * 128, col0 + done, col0 + done + gw))])
                wv = wview(slot, kch, gw)
                for cc in range(0, gw, chunk):
                    m = min(chunk, gw - cc)
                    ps = ps_next()
                    mm(ps[0:m, 0:N], [(wv[:, k, cc:cc + m], src[:, k, 0:N]) for k in range(kch)], [slot, src], [ps])
                    consumer((done + cc) // chunk, ps, m)
                done += gw

        def ffn(l, N, pre_row, post_gain):
            rmsnorm_to(lambda ch: XT[:, ch, 0:N], XT, DC, N, D, lambda ch: vrow(pre_row, ch),
                       lambda ch: H[:, ch, 0:N], H)
            for g in range(11):
                sl = wload([(lambda s_: wview(s_, DC, 512)[:, :, 0:256], wsrc(w_gate[l], 0, D, g * 256, (g + 1) * 256)),
                            (lambda s_: wview(s_, DC, 512)[:, :, 256:512], wsrc(w_up[l], 0, D, g * 256, (g + 1) * 256))])
                wv = wview(sl, DC, 512)
                for f in range(2):
                    pg = ps_next()
                    mm(pg[:, 0:N], [(wv[:, k, f * 128:(f + 1) * 128], H[:, k, 0:N]) for k in range(DC)], [sl, H], [pg])
                    pu = ps_next()
                    mm(pu[:, 0:N], [(wv[:, k, 256 + f * 128:256 + (f + 1) * 128], H[:, k, 0:N]) for k in range(DC)], [sl, H], [pu])
                    t = tmpf()
                    ACT(t[:, 0:N], pg[:, 0:N], AF.Silu, [pg], [t])
                    DVE(lambda e, t=t, pu=pu, g=g, f=f: e.tensor_tensor(AT[:, g * 2 + f, 0:N], t[:, 0:N], pu[:, 0:N],
                                                                        op=ALU.mult), [t, pu], [AT])
            Y = Yv(N)
            for dh in range(2):
                banks = [ps_next(pin=True) for _ in range(4)]
                for rg in range(3):
                    nfr = 8 if rg < 2 else 6
                    wd = wload([(lambda s, nfr=nfr: wview(s, nfr, 512),
                                 w_down[l][rg * 1024: rg * 1024 + nfr * 128, dh * 512:(dh + 1) * 512].rearrange(
                                     "(f p) n -> p f n", p=128))])
                    wdv = wview(wd, nfr, 512)
                    for dd in range(4):
                        mm(banks[dd][:, 0:N], [(wdv[:, f, dd * 128:(dd + 1) * 128], AT[:, rg * 8 + f, 0:N])
                                               for f in range(nfr)], [wd, AT], [banks[dd]], start=(rg == 0), stop=(rg == 2))
                for dd in range(4):
                    ACT(Y[:, dh * 4 + dd, :], banks[dd][:, 0:N], AF.Copy, [banks[dd]], [YB])
                unpin(*banks)
            residual_add(N, post_gain)

        def ffn_pipeline(l, pre_row, post_gain, tiles, loader, finisher, defer=True):
            XTs = [XT5, XT5b]
            Hs = [H5, H5b]
            nt = len(tiles)

            def L2(t):
                n = tiles[t]
                xt = XTs[t % 2]
                h = Hs[t % 2]
                rmsnorm_to(lambda ch: xt[:, ch, 0:n], xt, DC, n, D, lambda ch: vrow(pre_row, ch),
                           lambda ch: h[:, ch, 0:n], h)
            loader(0, XTs[0])
            L2(0)
            for t in range(nt):
                N = tiles[t]
                xt = XTs[t % 2]
                h = Hs[t % 2]
                for g in range(11):
                    sl = wload([(lambda s_: wview(s_, DC, 512)[:, :, 0:256], wsrc(w_gate[l], 0, D, g * 256, (g + 1) * 256)),
                                (lambda s_: wview(s_, DC, 512)[:, :, 256:512], wsrc(w_up[l], 0, D, g * 256, (g + 1) * 256))])
                    wv = wview(sl, DC, 512)
                    for f in range(2):
                        pg = ps_next()
                        mm(pg[:, 0:N], [(wv[:, k, f * 128:(f + 1) * 128], h[:, k, 0:N]) for k in range(DC)], [sl, h], [pg])
                        pu = ps_next()
                        mm(pu[:, 0:N], [(wv[:, k, 256 + f * 128:256 + (f + 1) * 128], h[:, k, 0:N]) for k in range(DC)], [sl, h], [pu])
                        tt_ = tmpf()
                        ACT(tt_[:, 0:N], pg[:, 0:N], AF.Silu, [pg], [tt_])
                        DVE(lambda e, tt_=tt_, pu=pu, g=g, f=f: e.tensor_tensor(AT[:, g * 2 + f, 0:N], tt_[:, 0:N], pu[:, 0:N],
                                                                                op=ALU.mult), [tt_, pu], [AT])
                    if defer and g == 2 and t > 0:
                        finisher(t - 1, XTs[(t - 1) % 2])
                    if g == 5 and t + 1 < nt:
                        loader(t + 1, XTs[(t + 1) % 2])
                if t + 1 < nt:
                    L2(t + 1)
                Y = Yv(N)
                for dh in range(2):
                    banks = [ps_next(pin=True) for _ in range(4)]
                    for rg in range(3):
                        nfr = 8 if rg < 2 else 6
                        wd = wload([(lambda s_, nfr=nfr: wview(s_, nfr, 512),
                                     w_down[l][rg * 1024: rg * 1024 + nfr * 128, dh * 512:(dh + 1) * 512].rearrange(
                                         "(f p) n -> p f n", p=128))])
                        wdv = wview(wd, nfr, 512)
                        for dd in range(4):
                            mm(banks[dd][:, 0:N], [(wdv[:, f, dd * 128:(dd + 1) * 128], AT[:, rg * 8 + f, 0:N])
                                                   for f in range(nfr)], [wd, AT], [banks[dd]], start=(rg == 0), stop=(rg == 2))
                    for dd in range(4):
                        ACT(Y[:, dh * 4 + dd, :], banks[dd][:, 0:N], AF.Copy, [banks[dd]], [YB])
                    unpin(*banks)
                residual_add(N, post_gain, xt=xt)
                if not defer:
                    finisher(t, xt)
            if defer:
                finisher(nt - 1, XTs[(nt - 1) % 2])

        def load_xT(x_ap, row0, N):
            nb = N // 128
            xin = YB[:, 0:nb * D].rearrange("p (b f) -> p b f", b=nb)
            c.dma("pool", [(xin, x_ap[row0:row0 + N, :].rearrange("(b p) f -> p b f", p=128))], writes=[YB])
            for ch in range(DC):
                ps = ps_next()
                for b in range(nb):
                    transpose(ps[:, b * 128:(b + 1) * 128], xin[:, b, ch * 128:(ch + 1) * 128], ident_f[:],
                              [YB, ident_f], [ps])
                ACT(XT[:, ch, 0:N], ps[:, 0:N], AF.Copy, [ps], [XT])

        def store_rows(out_ap, row0, N, src_ap_fn, srcbuf, nfeat):
            nch = (nfeat + 127) // 128
            for b in range(N // 128):
                og = OSTG[st.setdefault("os", 0) % 2]
                st["os"] += 1
                for c0 in range(0, nch, 4):
                    ps = ps_next()
                    ncc = min(4, nch - c0)
                    for ch in range(c0, c0 + ncc):
                        m = min(128, nfeat - ch * 128)
                        transpose(ps[:, (ch - c0) * 128:(ch - c0) * 128 + m], src_ap_fn(ch)[0:m, b * 128:(b + 1) * 128],
                                  ident_f[0:m, 0:m], [srcbuf, ident_f], [ps])
                    wid = min(512, nfeat - c0 * 128)
                    ACT(og[:, c0 * 128:c0 * 128 + wid], ps[:, 0:wid], AF.Copy, [ps], [og])
                c.dma("act", [(out_ap[row0 + b * 128: row0 + (b + 1) * 128, :], og[:, 0:nfeat])], reads=[og], is_output=True)
                yield b, og

        def rope(psA, psB, N, out_ap, outbuf, m=64):
            t1 = tmpf()
            t2 = tmpf()
            DVE(lambda e: e.tensor_tensor(t1[0:m, 0:N], psA[0:m, 0:N], CS[:, 0, 0:N], op=ALU.mult), [psA, CS], [t1])
            DVE(lambda e: e.tensor_tensor(t2[0:m, 0:N], psB[0:m, 0:N], CS[:, 1, 0:N], op=ALU.mult), [psB, CS], [t2])
            DVE(lambda e: e.tensor_tensor(out_ap, t1[0:m, 0:N], t2[0:m, 0:N], op=ALU.add), [t1, t2], [outbuf])

        def rnn_tile(N, B, L, h0_fn, smp):
            RX = RXB[:, 0:DC * B * (L + 3)].rearrange("p (c b l) -> p c b l", c=DC, b=B)
            HS = YB[:, 0:DC * B * (L + 1)].rearrange("p (c b l) -> p c b l", c=DC, b=B)
            W1 = L + 1 if smp else L
            off = 1 if smp else 0
            n1 = B * W1
            A4 = RA[:, 0:DC * n1].rearrange("p (c b l) -> p c b l", c=DC, b=B)
            M4 = RM[:, 0:DC * n1].rearrange("p (c b l) -> p c b l", c=DC, b=B)
            W4 = RW[:, 0:DC * n1].rearrange("p (c b l) -> p c b l", c=DC, b=B)
            alias = [SQ, MIX, AT, YR, MIXR]
            DVE(lambda e: e.memset(RA[:, 0:1], 0.0), [], [RA, RM, RW] + alias)
            for ch in range(DC):
                xc = tmpf()
                xcv = xc[:, 0:N].rearrange("p (b l) -> p b l", b=B)
                DVE(lambda e: e.tensor_scalar(xcv, RX[:, ch, :, 0:L], vrow(9, ch), vrow(13, ch), op0=ALU.mult, op1=ALU.add),
                    [RXB, VT], [xc])
                for k in range(1, 4):
                    DVE(lambda e, k=k: e.scalar_tensor_tensor(xcv, RX[:, ch, :, k:k + L], vrow(9 + k, ch), xcv,
                                                              op0=ALU.mult, op1=ALU.add), [RXB, VT, xc], [xc])
                xb = tmpb()
                DVE(lambda e: e.tensor_copy(xb[:, 0:N], xc[:, 0:N]), [xc], [xb])
                pr = ps_next()
                mm(pr[:, 0:N], [(WRG[:, ch, :], xb[:, 0:N])], [WRG, xb], [pr])
                pi = ps_next()
                mm(pi[:, 0:N], [(WIG[:, ch, :], xb[:, 0:N])], [WIG, xb], [pi])
                rt = tmpf()
                it = tmpf()
                ACT(rt[:, 0:N], pr[:, 0:N], AF.Tanh, [pr, HB], [rt], bias=HB[:, ch, 0:1], scale=0.5)
                ACT(it[:, 0:N], pi[:, 0:N], AF.Tanh, [pi, HB], [it], bias=HB[:, ch, 1:2], scale=0.5)
                rtv = rt[:, 0:N].rearrange("p (b l) -> p b l", b=B)
                itv = it[:, 0:N].rearrange("p (b l) -> p b l", b=B)
                ACT(A4[:, ch, :, off:off + L], rtv, AF.Exp, [rt, DER], [RA], scale=DER[:, ch, 3:4], bias=DER[:, ch, 3:4])
                ACT(M4[:, ch, :, off:off + L], A4[:, ch, :, off:off + L], AF.Square, [RA], [RM])
                DVE(lambda e: e.scalar_tensor_tensor(W4[:, ch, :, off:off + L], itv, 1.0, xcv, op0=ALU.add, op1=ALU.mult),
                    [it, xc], [RW])
                if smp:
                    DVE(lambda e: e.memset(A4[:, ch, :, 0:1], 0.0), [], [RA])
                    DVE(lambda e: e.memset(M4[:, ch, :, 0:1], -3.0), [], [RM])
                    DVE(lambda e: e.tensor_copy(W4[:, ch, :, 0], h0_fn(ch)), [H0S], [RW])
            ACT(RM[:, 0:DC * n1], RM[:, 0:DC * n1], AF.Sqrt, [RM], [RM], bias=0.25, scale=-0.25)
            DVE(lambda e: e.tensor_tensor(RW[:, 0:DC * n1], RW[:, 0:DC * n1], RM[:, 0:DC * n1], op=ALU.mult), [RW, RM], [RW])
            for ch in range(DC):
                if smp:
                    DVE(lambda e: e.tensor_tensor_scan(HS[:, ch].rearrange("p b l -> p (b l)"), A4[:, ch].rearrange("p b l -> p (b l)"),
                                                       W4[:, ch].rearrange("p b l -> p (b l)"), 0.0, op0=ALU.mult, op1=ALU.add),
                        [RA, RW], [YB])
                else:
                    DVE(lambda e: e.tensor_tensor_scan(HS[:, ch, 0, 1:L + 1], A4[:, ch, 0, :], W4[:, ch, 0, :], CARRY[:, ch:ch + 1],
                                                       op0=ALU.mult, op1=ALU.add), [RA, RW, CARRY], [YB])
            DVE(lambda e: e.memset(RA[:, 0:1], 0.0), [], [RA, RM, RW] + alias)
            return RX, HS

        def kv_path(N, kt_buf, vv_buf, lat_out, kr_out, row0):
            slot = wload([(lambda s: wview(s, DC, 384)[:, :, 0:320], wsrc(w_in, 0, D, 384, 704)),
                          (lambda s: wview(s, DC, 384)[:, :, 320:352], wsrc(w_in, 0, D, 672, 704)),
                          (lambda s: wview(s, DC, 384)[:, :, 352:384], wsrc(w_in, 0, D, 640, 672))])
            wv = wview(slot, DC, 384)
            for j in range(2):
                ps = ps_next()
                mm(ps[:, 0:N], [(wv[:, k, j * 128:(j + 1) * 128], H[:, k, 0:N]) for k in range(DC)], [slot, H], [ps])
                ACT(SCR[:, j, 0:N], ps[:, 0:N], AF.Copy, [ps], [SCR])
            pA = ps_next()
            mm(pA[0:64, 0:N], [(wv[:, k, 256:320], H[:, k, 0:N]) for k in range(DC)], [slot, H], [pA])
            pB = ps_next()
            mm(pB[0:64, 0:N], [(wv[:, k, 320:384], H[:, k, 0:N]) for k in range(DC)], [slot, H], [pB])
            rope(pA, pB, N, KRT[:, 0:N], KRT)
            DVE(lambda e: e.tensor_copy(kt_buf[0:64, 2, 0:N], KRT[:, 0:N]), [KRT], [kt_buf])
            rmsnorm_to(lambda ch: SCR[:, ch, 0:N], SCR, 2, N, 256.0, lambda ch: vrow(18, ch),
                       lambda ch: SCR[:, 2 + ch, 0:N], SCR)
            for j in range(2):
                DVE(lambda e, j=j: e.tensor_copy(kt_buf[:, j, 0:N], SCR[:, 2 + j, 0:N]), [SCR], [kt_buf])
            for b, og in store_rows(lat_out, row0, N, lambda ch: SCR[:, 2 + ch, 0:N], SCR, 256) if lat_out is not None else \
                    transposed_blocks(N, lambda ch: SCR[:, 2 + ch, 0:N], SCR, 256):
                DVE(lambda e, b=b, og=og: e.tensor_copy(vv_buf[:, b, :], og[:, 0:256]), [og], [vv_buf])
            if kr_out is not None:
                for _ in store_rows(kr_out, row0, N, lambda ch: KRT[:, 0:N], KRT, 64):
                    pass

        def transposed_blocks(N, src_ap_fn, srcbuf, nfeat):
            nch = (nfeat + 127) // 128
            for b in range(N // 128):
                og = OSTG[st.setdefault("os", 0) % 2]
                st["os"] += 1
                ps = ps_next()
                for ch in range(nch):
                    transpose(ps[:, ch * 128:(ch + 1) * 128], src_ap_fn(ch)[:, b * 128:(b + 1) * 128], ident_f[:],
                              [srcbuf, ident_f], [ps])
                ACT(og[:, 0:nfeat], ps[:, 0:nfeat], AF.Copy, [ps], [og])
                yield b, og

        def q_path(N, qr_buf):
            def cons(ci, ps, m):
                ACT(CQ[:, ci, 0:N], ps[:, 0:N], AF.Copy, [ps], [CQ])
            proj(w_in, 0, 384, N, cons)
            rmsnorm_to(lambda ch: CQ[:, ch, 0:N], CQ, 3, N, 384.0, lambda ch: vrow(17, ch),
                       lambda ch: CQN[:, ch, 0:N], CQN)
            uq4 = w_uq.rearrange("(k p) (h f) -> p k h f", p=128, f=192)
            s2parts = []
            for k in range(3):
                s2parts.append((lambda s, k=k: wview(s, 3, 512).rearrange("p k (h f) -> p k h f", f=64)[:, k, :, 0:32], uq4[:, k, :, 160:192]))
                s2parts.append((lambda s, k=k: wview(s, 3, 512).rearrange("p k (h f) -> p k h f", f=64)[:, k, :, 32:64], uq4[:, k, :, 128:160]))
            s2 = wload(s2parts)
            uqs = wview(s2, 3, 512)
            s1h = [wload([(lambda s: wview(s, 3, 768), w_uq[:, hf * 768:(hf + 1) * 768].rearrange("(k p) n -> p k n", p=128))])
                   for hf in range(2)]
            for hh in range(8):
                s1 = s1h[hh // 4]
                uq = wview(s1, 3, 768)
                hl = hh % 4
                ps = ps_next()
                mm(ps[:, 0:N], [(uq[:, k, hl * 192: hl * 192 + 128], CQN[:, k, 0:N]) for k in range(3)], [s1, CQN], [ps])
                ACT(QN[:, hh, 0:N], ps[:, 0:N], AF.Copy, [ps], [QN])
                pA = ps_next()
                mm(pA[0:64, 0:N], [(uq[:, k, hl * 192 + 128: hl * 192 + 192], CQN[:, k, 0:N]) for k in range(3)], [s1, CQN], [pA])
                pB = ps_next()
                mm(pB[0:64, 0:N], [(uqs[:, k, hh * 64:(hh + 1) * 64], CQN[:, k, 0:N]) for k in range(3)], [s2, CQN], [pB])
                rope(pA, pB, N, qr_buf[0:64, hh, 0:N], qr_buf)
                for j in range(2):
                    ps2 = ps_next()
                    mm(ps2[:, 0:N], [(WUKT[:, hh, j * 128:(j + 1) * 128], QN[:, hh, 0:N])], [WUKT, QN], [ps2])
                    ACT(QL[:, j, hh, 0:N], ps2[:, 0:N], AF.Copy, [ps2], [QL])

        def finish_heads(accs, den, n_rows, dst_fn, src_view=lambda ap: ap):
            DVE(lambda e: e.reciprocal(RD[0:n_rows, 0:len(accs)], den[0:n_rows, 0:len(accs)]), [den], [RD])
            for hh, (abuf, aap) in enumerate(accs):
                ACT(ONF[0:n_rows, :], aap, AF.Copy, [abuf, RD], [ONF], scale=RD[0:n_rows, hh:hh + 1])
                for j in range(2):
                    ps = ps_next()
                    transpose(ps[:, 0:n_rows], ONF[0:n_rows, j * 128:(j + 1) * 128], ident_f[0:n_rows, 0:n_rows],
                              [ONF, ident_f], [ps])
                    ACT(dst_fn(j, hh), src_view(ps[:, 0:n_rows]), AF.Copy, [ps], [OLAT])

        def prompt_attention(ot):
            for qb in range(NBLK):
                keyblocks = [(kt, kb) for kt in range(NT) for kb in range(NBLK)]
                keyblocks += [(NT + kt, kb) for kt in range(ot) for kb in range(NBLK)]
                keyblocks += [(NT + ot, kb) for kb in range(qb + 1)]
                for g in range(2):
                    acc = [ps_next(pin=True), ps_next(pin=True)]
                    den = ps_next(pin=True)
                    nk = len(keyblocks)

                    def stage_s(ki):
                        kt, kb = keyblocks[ki]
                        diag = (kt == NT + ot and kb == qb)
                        S = ps_next()
                        pairs = [(KT[kt][:, j, kb * 128:(kb + 1) * 128], QL[:, j, 4 * g:4 * g + 4, qb * 128:(qb + 1) * 128])
                                 for j in range(2)]
                        pairs.append((KT[kt][:, 2, kb * 128:(kb + 1) * 128], QRT[:, 4 * g:4 * g + 4, qb * 128:(qb + 1) * 128]))
                        rd = [KT[kt], QL, QRT]
                        if diag:
                            pairs.append((ident_b[:], maskb[:]))
                            rd += [ident_b, maskb]
                        mm(S[:].rearrange("p (h q) -> p h q", h=4), pairs, rd, [S])
                        P = pt_next()
                        ACT(P[:], S[:], AF.Exp, [S], [P], scale=SCALE)
                        return P

                    def stage_pv(ki, P):
                        kt, kb = keyblocks[ki]
                        for hh in range(4):
                            a = acc[hh // 2]
                            mm(a[:, (hh % 2) * 256:(hh % 2 + 1) * 256], [(P[:, hh * 128:(hh + 1) * 128], VV[kt][:, kb, :])],
                               [P, VV[kt]], [a], start=(ki == 0), stop=(ki == nk - 1))
                            mm(den[:, hh:hh + 1], [(P[:, hh * 128:(hh + 1) * 128], ones_b[:, 0:1])], [P, ones_b], [den],
                               start=(ki == 0), stop=(ki == nk - 1))
                    prev = stage_s(0)
                    for ki in range(1, nk):
                        cur = stage_s(ki)
                        stage_pv(ki - 1, prev)
                        prev = cur
                    stage_pv(nk - 1, prev)
                    finish_heads([(acc[hh // 2], acc[hh // 2][:, (hh % 2) * 256:(hh % 2 + 1) * 256]) for hh in range(4)],
                                 den, 128, lambda j, hh, g=g, qb=qb: OLAT[:, j, 4 * g + hh, qb * 128:(qb + 1) * 128])
                    unpin(acc[0], acc[1], den)

        def sample_attention():
            c.barrier()
            GP = 8
            NG = 64 // GP
            lat_g = pool_lat.rearrange("(g r) f -> g (r f)", r=8)
            kr_g = pool_kr.rearrange("(g r) f -> g (r f)", r=8)
            groups = [(b, gi) for b in range(NSMP) for gi in range(NG)]

            def stage_a(n):
                b, gi = groups[n]
                pk = PGK[n % 3]
                pr = PGR[n % 3]
                ktp = KTP[n % 2]
                col = b * NG + gi
                c.dma("pool", [(pk[:].rearrange("p r f -> p (r f)"), lat_g, IDXG[:, col:col + 1])], reads=[IDXG], writes=[pk])
                c.dma("pool", [(pr[:].rearrange("p r f -> p (r f)"), kr_g, IDXG[:, col:col + 1])], reads=[IDXG], writes=[pr])
                for p in range(GP):
                    tp = ps_next()
                    tpb = tp[:].bitcast(BF16)
                    for j in range(2):
                        transpose(tpb[:, j * 128:(j + 1) * 128], pk[:, p, j * 128:(j + 1) * 128], ident_b[:],
                                  [pk, ident_b], [tp])
                    transpose(tpb[0:64, 256:384], pr[:, p, :], ident_b[:], [pr, ident_b], [tp])
                    ACT(ktp[:, 0:2, p, :], tpb[:, 0:256].rearrange("p (j k) -> p j k", j=2), AF.Copy, [tp], [ktp])
                    DVE(lambda e, tpb=tpb, p=p, ktp=ktp: e.tensor_copy(ktp[0:64, 2, p, :], tpb[0:64, 256:384]), [tp], [ktp])

            def stage_b(n):
                b, gi = groups[n]
                ktp = KTP[n % 2]
                S = ps_next()
                for p in range(GP):
                    pairs = [(ktp[:, j, p, :], QL[:, j, :, b * 8:(b + 1) * 8]) for j in range(2)]
                    pairs.append((ktp[0:64, 2, p, :], QRS[0:64, :, b * 8:(b + 1) * 8]))
                    mm(S[:, p * 64:(p + 1) * 64].rearrange("p (h t) -> p h t", h=8), pairs, [ktp, QL, QRS], [S])
                P = pt_next()
                ACT(P[:, 0:GP * 64], S[:, 0:GP * 64], AF.Exp, [S], [P], scale=SCALE)
                return P

            def pv(acc, den, P, vlist, first, last):
                for p, (vb, vap) in enumerate(vlist):
                    mm(acc[0:64, 0:256], [(P[:, p * 64:(p + 1) * 64], vap)], [P, vb], [acc], start=(first and p == 0), stop=last)
                    mm(den[0:64, 0:1], [(P[:, p * 64:(p + 1) * 64], ones_b[:, 0:1])], [P, ones_b], [den],
                       start=(first and p == 0), stop=last)

            ng = len(groups)
            stage_a(0)
            acc = den = None
            for n in range(ng):
                b, gi = groups[n]
                if gi == 0:
                    acc = ps_next(pin=True)
                    den = ps_next(pin=True)
                P = stage_b(n)
                if n + 1 < ng:
                    stage_a(n + 1)
                pk = PGK[n % 3]
                pv(acc, den, P, [(pk, pk[:, p, :]) for p in range(GP)], gi == 0, False)
                if gi == NG - 1:
                    S = ps_next()
                    pairs = [(KTS[:, j, :], QL[:, j, :, b * 8:(b + 1) * 8]) for j in range(2)]
                    pairs.append((KTS[0:64, 2, :], QRS[0:64, :, b * 8:(b + 1) * 8]))
                    pairs.append((ident_b[:], masks[:, b, :, :]))
                    mm(S[:, 0:64].rearrange("p (h t) -> p h t", h=8), pairs, [KTS, QL, QRS, ident_b, masks], [S])
                    P2 = pt_next()
                    ACT(P2[:, 0:64], S[:, 0:64], AF.Exp, [S], [P2], scale=SCALE)
                    pv(acc, den, P2, [(VVS, VVS[:, 0, :])], False, True)
                    finish_heads([(acc, acc[0:64, 0:256])], den, 64,
                                 lambda j, hh, b=b: OLAT[:, j, :, b * 8:(b + 1) * 8],
                                 src_view=lambda ap: ap.rearrange("p (h t) -> p h t", h=8))
                    unpin(acc, den)

        def mem_attention_prompt(N):
            for hh in range(4):
                Ps = []
                for mb in range(2):
                    S = ps_next()
                    mm(S[:, 0:N], [(MKT[:, 2 * hh + j, mb * 128:(mb + 1) * 128], QM[:, 2 * hh + j, 0:N]) for j in range(2)],
                       [MKT, QM], [S])
                    P = pt_next()
                    ACT(P[:, 0:N], S[:, 0:N], AF.Exp, [S], [P], scale=1.0 / 16.0)
                    Ps.append(P)
                dn = ps_next()
                mm(dn[:, 0:N], [(ones_b[:], Ps[mb][:, 0:N]) for mb in range(2)], [ones_b] + Ps, [dn])
                rdn = tmpf()
                DVE(lambda e: e.reciprocal(rdn[:, 0:N], dn[:, 0:N]), [dn], [rdn])
                for j in range(2):
                    po = ps_next()
                    mm(po[:, 0:N], [(MV[:, mb, (2 * hh + j) * 128:(2 * hh + j + 1) * 128], Ps[mb][:, 0:N]) for mb in range(2)],
                       [MV] + Ps, [po])
                    DVE(lambda e, po=po, j=j: e.tensor_tensor(OM[:, 2 * hh + j, 0:N], po[:, 0:N], rdn[:, 0:N], op=ALU.mult),
                        [po, rdn], [OM])

        def mem_attention_sample():
            for b in range(NSMP):
                mk = MKS[b % 2]
                mv = MVS[b % 2]
                mkt = MKST[b % 2]
                c.dma("pool", [(mk[:], mem_k_s[b].rearrange("(m p) f -> p m f", p=128))], writes=[mk])
                c.dma("pool", [(mv[:], mem_v_s[b].rearrange("(m p) f -> p m f", p=128))], writes=[mv])
                for ch in range(DC):
                    tp = ps_next()
                    tpb = tp[:].bitcast(BF16)
                    for mb in range(2):
                        transpose(tpb[:, mb * 128:(mb + 1) * 128], mk[:, mb, ch * 128:(ch + 1) * 128], ident_b[:],
                                  [mk, ident_b], [tp])
                    ACT(mkt[:, ch, :], tpb[:, 0:256], AF.Copy, [tp], [mkt])
                S = ps_next()
                for hh in range(4):
                    for mb in range(2):
                        mm(S[:, mb * 32 + hh * 8: mb * 32 + (hh + 1) * 8],
                           [(mkt[:, 2 * hh + j, mb * 128:(mb + 1) * 128], QM[:, 2 * hh + j, b * 8:(b + 1) * 8]) for j in range(2)],
                           [mkt, QM], [S])
                Pb = pt_next()
                ACT(Pb[:, 0:64], S[:, 0:64], AF.Exp, [S], [Pb], scale=1.0 / 16.0)
                dn = ps_next()
                mm(dn[:, 0:32], [(ones_b[:], Pb[:, mb * 32:(mb + 1) * 32]) for mb in range(2)], [ones_b, Pb], [dn])
                rdn = tmpf()
                DVE(lambda e, dn=dn, rdn=rdn: e.reciprocal(rdn[:, 0:32], dn[:, 0:32]), [dn], [rdn])
                po = ps_next()
                for ch in range(DC):
                    hh = ch // 2
                    mm(po[:, ch * 8:(ch + 1) * 8],
                       [(mv[:, mb, ch * 128:(ch + 1) * 128], Pb[:, mb * 32 + hh * 8: mb * 32 + (hh + 1) * 8]) for mb in range(2)],
                       [mv, Pb], [po])
                for ch in range(DC):
                    hh = ch // 2
                    DVE(lambda e, po=po, rdn=rdn, hh=hh, ch=ch, b=b: e.tensor_tensor(
                        OM[:, ch, b * 8:(b + 1) * 8], po[:, ch * 8:(ch + 1) * 8], rdn[:, hh * 8:(hh + 1) * 8], op=ALU.mult),
                        [po, rdn], [OM])

        def process_tile(kind, ti):
            own = kind == "own"
            smp = kind == "smp"
            N = 128 if smp else TT
            B, L = (NSMP, 8) if smp else (1, TT)
            xsrc = {"oth": x_oth, "own": x_own, "smp": x_smp}[kind]
            cssrc = {"oth": cs_oth, "own": cs_own, "smp": cs_smp}[kind]
            row0 = 0 if smp else ti * TT
            tok0 = {"oth": 0, "own": SEQ_HALF, "smp": 2 * SEQ_HALF}[kind] + row0
            c.dma("pool", [(XT[:, :, 0:N], X1.ap[:, :, tok0:tok0 + N].rearrange("c p n -> p c n"))], reads=[X1.buf], writes=[XT])
            c.dma("pool", [(CS[:, :, 0:N], cssrc[:, :, row0:row0 + N].rearrange("a r n -> r a n"))], writes=[CS])
            rmsnorm_to(lambda ch: XT[:, ch, 0:N], XT, DC, N, D, lambda ch: vrow(2, ch), lambda ch: H[:, ch, 0:N], H)
            if smp:
                kv_path(N, KTS, VVS, lat_smp, kr_smp, 0)
            elif own:
                kv_path(N, KT[NT + ti], VV[NT + ti], lat_own, kr_own, row0)
            else:
                kv_path(N, KT[ti], VV[ti], None, None, 0)
            RX = RXB[:, 0:DC * B * (L + 3)].rearrange("p (c b l) -> p c b l", c=DC, b=B)
            if smp:
                c.dma("pool", [(STG[0:48, :], conv_s)], writes=[STG])
                for ch in range(DC):
                    ps = ps_next()
                    transpose(ps[:, 0:48], STG[0:48, ch * 128:(ch + 1) * 128], ident_f[0:48, 0:48], [STG, ident_f], [ps])
                    ACT(RX[:, ch, :, 0:3], ps[:, 0:48].rearrange("p (b k) -> p b k", b=NSMP), AF.Copy, [ps], [RXB])
                c.dma("pool", [(STG[0:16, :], h_s)], writes=[STG])
                for ch in range(DC):
                    ps = ps_next()
                    transpose(ps[:, 0:16], STG[0:16, ch * 128:(ch + 1) * 128], ident_f[0:16, 0:16], [STG, ident_f], [ps])
                    ACT(H0S[:, ch, :], ps[:, 0:16], AF.Copy, [ps], [H0S])
            else:
                if own and ti == 0:
                    DVE(lambda e: e.tensor_scalar(HALO[:], HALO[:], CMASK[:, 0:1], None, op0=ALU.mult), [HALO, CMASK], [HALO])
                    DVE(lambda e: e.tensor_scalar(CARRY[:], CARRY[:], CMASK[:, 0:1], None, op0=ALU.mult), [CARRY, CMASK], [CARRY])
                DVE(lambda e: e.tensor_copy(RX[:, :, 0, 0:3], HALO[:]), [HALO], [RXB])

            def cons_rx(ci, ps, m):
                ACT(RX[:, ci, :, 3:3 + L], ps[:, 0:N].rearrange("p (b l) -> p b l", b=B), AF.Copy, [ps], [RXB])
            proj(w_in, 704, 1024, N, cons_rx)
            if not smp:
                DVE(lambda e: e.tensor_copy(HALO[:], RX[:, :, 0, L:L + 3]), [RXB], [HALO])
            if smp:
                for ch in range(DC):
                    ps = ps_next()
                    t = tmpf()
                    DVE(lambda e, t=t, ch=ch: e.tensor_copy(t[:, 0:48].rearrange("p (b k) -> p b k", b=NSMP), RX[:, ch, :, 8:11]),
                        [RXB], [t])
                    transpose(ps[0:48, 0:128], t[:, 0:48], ident_f[:], [t, ident_f], [ps])
                    ACT(STG[0:48, ch * 128:(ch + 1) * 128], ps[0:48, 0:128], AF.Copy, [ps], [STG])
                c.dma("act", [(conv_smp, STG[0:48, :])], reads=[STG], is_output=True)
            elif own and ti == NT - 1:
                for ch in range(DC):
                    ps = ps_next()
                    t = tmpf()
                    DVE(lambda e, t=t, ch=ch: e.tensor_copy(t[:, 0:3], RX[:, ch, 0, L:L + 3]), [RXB], [t])
                    transpose(ps[0:3, 0:128], t[:, 0:3], ident_f[:], [t, ident_f], [ps])
                    ACT(STG[0:3, ch * 128:(ch + 1) * 128], ps[0:3, 0:128], AF.Copy, [ps], [STG])
                c.dma("act", [(conv_p, STG[0:3, :])], reads=[STG], is_output=True)
            h0_fn = (lambda ch: H0S[:, ch, :]) if smp else (lambda ch: CARRY[:, ch:ch + 1])
            RX, HS = rnn_tile(N, B, L, h0_fn, smp)
            if smp:
                for ch in range(DC):
                    ps = ps_next()
                    t = tmpf()
                    DVE(lambda e, t=t, ch=ch: e.tensor_copy(t[:, 0:NSMP], HS[:, ch, :, L]), [YB], [t])
                    transpose(ps[0:16, 0:128], t[:, 0:16], ident_f[:], [t, ident_f], [ps])
                    ACT(STG[0:16, ch * 128:(ch + 1) * 128], ps[0:16, 0:128], AF.Copy, [ps], [STG])
                c.dma("act", [(h_smp, STG[0:16, :])], reads=[STG], is_output=True)
            else:
                DVE(lambda e: e.tensor_copy(CARRY[:], HS[:, :, 0, L]), [YB], [CARRY])
                if own and ti == NT - 1:
                    ps = ps_next()
                    transpose(ps[0:8, 0:128], CARRY[:], ident_f[:], [CARRY, ident_f], [ps])
                    ACT(STG[0:8, 0:128], ps[0:8, 0:128], AF.Copy, [ps], [STG])
                    c.dma("act", [(h_p, STG[0:8, 0:128])], reads=[STG], is_output=True)
            if not (own or smp):
                return

            def cons_rg(ci, ps, m):
                t = tmpf()
                ACT(t[:, 0:N], ps[:, 0:N], AF.Gelu, [ps], [t])
                DVE(lambda e, t=t, ci=ci: e.tensor_tensor(YR[:, ci, 0:N].rearrange("p (b l) -> p b l", b=B), HS[:, ci, :, 1:L + 1],
                                                          t[:, 0:N].rearrange("p (b l) -> p b l", b=B), op=ALU.mult), [YB, t], [YR])
            proj(w_in, 1728, 1024, N, cons_rg)
            sg = {}

            def cons_sg(ci, ps, m):
                t = tmpf()
                ACT(t[:, 0:N], ps[:, 0:N], AF.Sigmoid, [ps], [t])
                sg[ci] = t
            for half in range(2):
                sg.clear()
                proj(w_in, 3776 + half * 512, 512, N, cons_sg)

                def cons_orn(ci, ps, m, half=half):
                    DVE(lambda e, ps=ps, ci=ci: e.tensor_tensor(MIXR[:, half * 4 + ci, 0:N], sg[ci][:, 0:N], ps[:, 0:N], op=ALU.mult),
                        [sg[ci], ps], [MIXR])
                proj(w_o_rnn, half * 512, 512, N, cons_orn, src=YR)
            q_path(N, QRS if smp else QRT)
            if smp:
                sample_attention()
            else:
                prompt_attention(ti)
            for hh in range(8):
                ps = ps_next()
                mm(ps[:, 0:N], [(WUV[:, j, hh * 128:(hh + 1) * 128], OLAT[:, j, hh, 0:N]) for j in range(2)], [WUV, OLAT], [ps])
                ACT(YR[:, hh, 0:N], ps[:, 0:N], AF.Copy, [ps], [YR])
            for half in range(2):
                sg.clear()
                proj(w_in, 2752 + half * 512, 512, N, cons_sg)

                def cons_om(ci, ps, m, half=half):
                    t = tmpf()
                    DVE(lambda e, ps=ps, ci=ci, t=t: e.tensor_tensor(t[:, 0:N], sg[ci][:, 0:N], ps[:, 0:N], op=ALU.mult), [sg[ci], ps], [t])
                    DVE(lambda e, ci=ci, t=t: e.tensor_tensor(MIX[:, half * 4 + ci, 0:N], t[:, 0:N], MIXR[:, half * 4 + ci, 0:N], op=ALU.add),
                        [t, MIXR], [MIX])
                proj(w_o_mla, half * 512, 512, N, cons_om, src=YR)
            Y = Yv(N)

            def cons_y(ci, ps, m):
                ACT(Y[:, ci, :], ps[:, 0:N], AF.Copy, [ps], [YB])
            proj(w_out, 0, 1024, N, cons_y, src=MIX)
            residual_add(N, lambda ch: vrow(3, ch))
            rmsnorm_to(lambda ch: XT[:, ch, 0:N], XT, DC, N, D, lambda ch: vrow(4, ch), lambda ch: H[:, ch, 0:N], H)

            def cons_qm(ci, ps, m):
                ACT(QM[:, ci, 0:N], ps[:, 0:N], AF.Copy, [ps], [QM])
            proj(w_mem_q, 0, 1024, N, cons_qm)
            if smp:
                mem_attention_sample()
            else:
                mem_attention_prompt(N)
            proj(w_mem_o, 0, 1024, N, cons_y, src=OM)
            residual_add(N, lambda ch: vrow(5, ch))
            tok3 = (SEQ_HALF if smp else 0) + row0
            c.dma("pool", [(X3.ap[:, :, tok3:tok3 + N].rearrange("c p n -> p c n"), XT[:, :, 0:N])], reads=[XT], writes=[X3.buf],
                  sembuf=X3.buf)

        def prompt_mem_kv():
            N = 256
            load_xT(mem_p, 0, N)
            rmsnorm_to(lambda ch: XT[:, ch, 0:N], XT, DC, N, D, lambda ch: vrow(8, ch), lambda ch: H[:, ch, 0:N], H)

            def cons_k(ci, ps, m):
                ACT(MKT[:, ci, :], ps[:, 0:N], AF.Copy, [ps], [MKT])
            proj(w_mem_k, 0, 1024, N, cons_k)
            for wi, (w_ap, o_ap) in enumerate(((w_mem_k, mk_p), (w_mem_v, mv_p))):
                for half in range(2):
                    slot = wload([(lambda s: wview(s, DC, 512), wsrc(w_ap, 0, D, half * 512, (half + 1) * 512))])
                    wv = wview(slot, DC, 512)
                    for mb in range(2):
                        ps = ps_next()
                        mm(ps[:, 0:512], [(H[:, k, mb * 128:(mb + 1) * 128], wv[:, k, :]) for k in range(DC)], [slot, H], [ps])
                        og = OSTG[st.setdefault("os", 0) % 2]
                        st["os"] += 1
                        ACT(og[:, 0:512], ps[:, 0:512], AF.Copy, [ps], [og])
                        c.dma("act", [(o_ap[mb * 128:(mb + 1) * 128, half * 512:(half + 1) * 512], og[:, 0:512])], reads=[og], is_output=True)
                        if wi == 1:
                            DVE(lambda e, og=og, mb=mb, half=half: e.tensor_copy(MV[:, mb, half * 512:(half + 1) * 512], og[:, 0:512]),
                                [og], [MV])

        PI = OSTG[1]
        PF = OSTG[0]
        c.dma("pool", [(PI[:].bitcast(I32), ptab.to_broadcast([128, NSMP * 64]))], writes=[PI])
        c.op("pool", lambda e: e.iota(IOTA[:], [[0, 1]], base=0, channel_multiplier=1, allow_small_or_imprecise_dtypes=True),
             writes=[IOTA])
        DVE(lambda e: e.tensor_copy(PF[:], PI[:].bitcast(I32)), [PI], [PF])
        OH = tmpf()
        c.op("pool", lambda e: e.memset(OH[:, 0:8], 1.0), writes=[OH])
        c.op("pool", lambda e: e.affine_select(OH[:, 0:8], OH[:, 0:8], [[-16, 8]], ALU.is_ge, 0.0, base=0, channel_multiplier=1),
             reads=[OH], writes=[OH])
        c.op("pool", lambda e: e.affine_select(OH[:, 0:8], OH[:, 0:8], [[16, 8]], ALU.is_ge, 0.0, base=15, channel_multiplier=-1),
             reads=[OH], writes=[OH])
        ACCI = tmpf()
        PF3 = PF[:].rearrange("p (c k) -> p c k", k=8)
        DVE(lambda e: e.tensor_scalar(ACCI[:, 0:128], PF3[:, :, 0], OH[:, 0:1], None, op0=ALU.mult), [PF, OH], [ACCI])
        for k in range(1, 8):
            DVE(lambda e, k=k: e.scalar_tensor_tensor(ACCI[:, 0:128], PF3[:, :, k], OH[:, k:k + 1], ACCI[:, 0:128],
                                                      op0=ALU.mult, op1=ALU.add), [PF, OH, ACCI], [ACCI])
        PM = tmpf()
        DVE(lambda e: e.tensor_scalar(PM[:, 1:2], OH[:, 1:2], 1.0, None, op0=ALU.mult), [OH], [PM])
        for k in range(2, 8):
            DVE(lambda e, k=k: e.scalar_tensor_tensor(PM[:, 1:2], OH[:, k:k + 1], float(k), PM[:, 1:2], op0=ALU.mult, op1=ALU.add),
                [OH, PM], [PM])
        DVE(lambda e: e.scalar_tensor_tensor(PM[:, 0:1], PM[:, 1:2], -16.0, IOTA[:, 0:1], op0=ALU.mult, op1=ALU.add), [PM, IOTA], [PM])
        DVE(lambda e: e.tensor_scalar(ACCI[:, 0:128], ACCI[:, 0:128], 16.0, PM[:, 0:1], op0=ALU.mult, op1=ALU.add), [ACCI, PM], [ACCI])
        DVE(lambda e: e.tensor_copy(IDXG[:], ACCI[:, 0:128]), [ACCI], [IDXG])

        def scratch(name, shape):
            t = nc.dram_tensor(name, list(shape), F32, kind="Internal")
            b = Buf(c, t, name, "dscr")
            c.all_bufs.append(b)
            return WRef(t.ap(), b)
        X1 = scratch("X1", [DC, 128, 2 * SEQ_HALF + 128])
        X3 = scratch("X3", [DC, 128, SEQ_HALF + 128])
        main_set = (XT, YB, H, SQ, AT, TF, RS)
        ffn_set = (XT5, YB5, H5, SQ5, AT5, TF5, RS5)

        c.barrier()
        XT, YB, H, SQ, AT, TF, RS = ffn_set
        jobs = [(x_oth, t * TF_, TF_, t * TF_) for t in range(SEQ_HALF // TF_)]
        jobs += [(x_own, t * TF_, TF_, SEQ_HALF + t * TF_) for t in range(SEQ_HALF // TF_)]
        jobs += [(x_smp, 0, 128, 2 * SEQ_HALF)]

        def pre_loader(t, xt):
            xs, r0, n, tk = jobs[t]
            nb = n // 128
            xin = STG5[:, 0:nb * D].rearrange("p (b f) -> p b f", b=nb)
            c.dma("pool", [(xin, xs[r0:r0 + n, :].rearrange("(b p) f -> p b f", p=128))], writes=[STG5])
            for ch in range(DC):
                ps = ps_next()
                for b in range(nb):
                    transpose(ps[:, b * 128:(b + 1) * 128], xin[:, b, ch * 128:(ch + 1) * 128], ident_f[:], [STG5, ident_f], [ps])
                ACT(xt[:, ch, 0:n], ps[:, 0:n], AF.Copy, [ps], [xt])

        def pre_finisher(t, xt):
            xs, r0, n, tk = jobs[t]
            c.dma("pool", [(X1.ap[:, :, tk:tk + n].rearrange("c p n -> p c n"), xt[:, :, 0:n])], reads=[xt], writes=[X1.buf],
                  sembuf=X1.buf)
        ffn_pipeline(0, 0, lambda ch: DER[:, ch, 1:2], [j[2] for j in jobs], pre_loader, pre_finisher)
        c.barrier()
        XT, YB, H, SQ, AT, TF, RS = main_set
        for i in range(2 * NT):
            c.op("pool", lambda e, i=i: e.memset(KT[i][64:128, 2, :], 0.0), writes=[KT[i]])
            c.dma("pool", [(KT[i][64:65, 2, :], kmask[:, i * TT:(i + 1) * TT])], writes=[KT[i]])

        prompt_mem_kv()
        for ti in range(NT):
            process_tile("oth", ti)
        for ti in range(NT):
            process_tile("own", ti)
        process_tile("smp", 0)

        c.barrier()
        XT, YB, H, SQ, AT, TF, RS = ffn_set
        jobs2 = [(y_own, t * TF_, TF_, t * TF_) for t in range(SEQ_HALF // TF_)] + [(y_smp, 0, 128, SEQ_HALF)]

        def post_loader(t, xt):
            yo, r0, n, tk = jobs2[t]
            c.dma("pool", [(xt[:, :, 0:n], X3.ap[:, :, tk:tk + n].rearrange("c p n -> p c n"))], reads=[X3.buf], writes=[xt])

        def post_finisher(t, xt):
            yo, r0, n, tk = jobs2[t]
            for _ in store_rows(yo, r0, n, lambda ch: xt[:, ch, 0:n], xt, D):
                pass
        ffn_pipeline(1, 6, lambda ch: DER[:, ch, 2:3], [j[2] for j in jobs2], post_loader, post_finisher)
        c.finish("sp")
        print("build: sems", c.nsem, {k: (v.n_inst, v.n_wait) for k, v in c.engs.items()})
    return nc


_CACHE = {}


def _rope_tables(pos):
    inv = (np.float32(10000.0) ** (-np.arange(0, 64, 2, dtype=np.float32) / np.float32(64))).astype(np.float32)
    ang = pos.astype(np.float32)[:, None] * inv[None, :]
    cos = np.cos(ang).astype(np.float32).T
    sin = np.sin(ang).astype(np.float32).T
    return np.ascontiguousarray(np.stack([np.concatenate([cos, cos], 0), np.concatenate([-sin, sin], 0)], 0))


def kernel(x_prompt, x_sample, mem_prompt, cache_mla_latent, cache_mla_krope, page_table,
           state_rnn_conv, state_rnn_h, cache_mem_k, cache_mem_v,
           norms, w_ffn_gate, w_ffn_up, w_ffn_down, w_in, q_norm, kv_norm, w_uq, w_uk, w_uv,
           w_o_mla, conv_w, conv_b, w_rg, b_rg, w_ig, b_ig, lru_lambda, w_o_rnn, w_out,
           mem_norm, w_mem_q, w_mem_k, w_mem_v, w_mem_o):
    f32 = np.float32
    A = lambda a: np.ascontiguousarray(np.asarray(a))
    x_prompt, x_sample, mem_prompt = A(x_prompt), A(x_sample), A(mem_prompt)
    n_pool = cache_mla_latent.shape[1]
    pool_lat = A(cache_mla_latent).reshape(n_pool * 128, 256)
    pool_kr = A(cache_mla_krope).reshape(n_pool * 128, 64)
    page_table = A(page_table).astype(np.int32)
    vec = np.zeros((19, D), f32)
    vec[0:8] = A(norms)[0]
    vec[8] = A(mem_norm)[0]
    vec[9:13] = A(conv_w)[0]
    vec[13] = A(conv_b)[0]
    vec[14] = A(b_rg)[0]
    vec[15] = A(b_ig)[0]
    vec[16] = A(lru_lambda)[0]
    vec[17, :384] = A(q_norm)[0]
    vec[18, :256] = A(kv_norm)[0]
    shared = {
        "pool_lat": pool_lat, "pool_kr": pool_kr, "vec": vec,
        "w_gate": A(w_ffn_gate)[0], "w_up": A(w_ffn_up)[0], "w_down": A(w_ffn_down)[0], "w_in": A(w_in)[0],
        "w_uq": A(w_uq)[0], "w_uk": A(w_uk)[0].reshape(256, 1024), "w_uv": A(w_uv)[0].reshape(256, 1024),
        "w_o_mla": A(w_o_mla)[0], "w_rg": A(w_rg)[0], "w_ig": A(w_ig)[0], "w_o_rnn": A(w_o_rnn)[0], "w_out": A(w_out)[0],
        "w_mem_q": A(w_mem_q)[0], "w_mem_k": A(w_mem_k)[0], "w_mem_v": A(w_mem_v)[0], "w_mem_o": A(w_mem_o)[0],
    }
    past = page_table.shape[1] * 128
    cs_first = _rope_tables(np.arange(0, SEQ_HALF))
    cs_second = _rope_tables(np.arange(SEQ_HALF, 2 * SEQ_HALF))
    cs_smp = _rope_tables(np.tile(past + np.arange(8), NSMP))
    kmask = np.concatenate([np.ones(SEQ_HALF, f32), np.zeros(SEQ_HALF, f32)])[None, :]
    in_maps = []
    for core in range(NCORES):
        s, half = core // 2, core % 2
        bs = slice(core * NSMP, (core + 1) * NSMP)
        m = dict(shared)
        m["x_oth"] = x_prompt[s, 0:SEQ_HALF]
        m["x_own"] = x_prompt[s, half * SEQ_HALF:(half + 1) * SEQ_HALF]
        m["x_smp"] = x_sample[bs].reshape(128, D)
        m["mem_p"] = mem_prompt[s]
        m["ptab"] = page_table[bs].reshape(1, NSMP * 64)
        m["conv_s"] = A(state_rnn_conv)[0, bs].reshape(NSMP * 3, D)
        m["h_s"] = A(state_rnn_h)[0, bs]
        m["mem_k_s"] = A(cache_mem_k)[0, bs].reshape(NSMP, 256, D)
        m["mem_v_s"] = A(cache_mem_v)[0, bs].reshape(NSMP, 256, D)
        m["cs_oth"] = cs_first
        m["cs_own"] = cs_second if half else cs_first
        m["cs_smp"] = cs_smp
        m["kmask"] = kmask
        m["pmask"] = np.full((1, 8 * TT), 0.0 if half else NEG, f32)
        m["cmask"] = np.full((128, 1), 1.0 if half else 0.0, f32)
        in_maps.append(m)
    key = n_pool
    if key not in _CACHE:
        _CACHE[key] = build(n_pool)
    nc = _CACHE[key]
    res = run_bass_kernel_spmd(nc, in_maps, core_ids=list(range(NCORES))).results
    B, S = x_prompt.shape[0], x_prompt.shape[1]
    y_p = np.zeros((B, S, D), f32)
    lat_p = np.zeros((1, B, S, 256), f32)
    kr_p = np.zeros((1, B, S, 64), f32)
    conv_p = np.zeros((1, B, 3, D), f32)
    h_p = np.zeros((1, B, D), f32)
    mk_p = np.zeros((1, B, 256, 4, 256), f32)
    mv_p = np.zeros((1, B, 256, 4, 256), f32)
    y_s = np.zeros((128, 8, D), f32)
    lat_s = np.zeros((1, 128, 8, 256), f32)
    kr_s = np.zeros((1, 128, 8, 64), f32)
    conv_sn = np.zeros((1, 128, 3, D), f32)
    h_sn = np.zeros((1, 128, D), f32)
    for core in range(NCORES):
        r = res[core]
        s, half = core // 2, core % 2
        sl = slice(half * SEQ_HALF, (half + 1) * SEQ_HALF)
        bs = slice(core * NSMP, (core + 1) * NSMP)
        y_p[s, sl] = r["y_own"]
        lat_p[0, s, sl] = r["lat_own"]
        kr_p[0, s, sl] = r["kr_own"]
        if half == 1:
            conv_p[0, s] = r["conv_p"]
            h_p[0, s] = r["h_p"].reshape(D)
        else:
            mk_p[0, s] = r["mk_p"].reshape(256, 4, 256)
            mv_p[0, s] = r["mv_p"].reshape(256, 4, 256)
        y_s[bs] = r["y_smp"].reshape(NSMP, 8, D)
        lat_s[0, bs] = r["lat_smp"].reshape(NSMP, 8, 256)
        kr_s[0, bs] = r["kr_smp"].reshape(NSMP, 8, 64)
        conv_sn[0, bs] = r["conv_smp"].reshape(NSMP, 3, D)
        h_sn[0, bs] = r["h_smp"]
    return (y_p, y_s, lat_p, kr_p, lat_s, kr_s, conv_p, conv_sn, h_p, h_sn, mk_p, mv_p)
```
